# Optimizing a Trainium2 kernel written in Bass

```python
import jax, jax.numpy as jnp
from jax import lax
import numpy as np

D_MODEL = 2048
BATCH = 2
SEQ = 4096
DEPTH = 4
DEC_BATCH = 8
DEC_SEQ = 32
PAST_LEN = 1024

CHUNK = 64
N_EVEN = (DEPTH + 1) // 2
N_ODD = DEPTH // 2
D_FF = 5632
EPS = 1e-6
D_A = D_MODEL // 2
CONV_A = 3
D_B = D_MODEL // 2
CONV_B = 31
EVEN_SPLITS = (D_A, D_A, D_A, D_B, D_B)
EVEN_IN = sum(EVEN_SPLITS)
DK_C = 128
DV_C = 128
H_C = D_MODEL // 256
D_C = H_C * DV_C
DK_D = 128
DV_D = 128
H_D = D_MODEL // 256
D_DK = H_D * DK_D
D_DV = H_D * DV_D
ODD_SPLITS = (H_C * DK_C, H_C * DK_C, D_C, D_C, D_DK, D_DK, D_DV, D_DV, H_C, H_C)
ODD_IN = sum(ODD_SPLITS)

kernel_name = "hybrid_streaming_conv_mlstm_hgrn2_step"

F32 = jnp.float32


def _split(z, sizes):
    idx = [int(v) for v in np.cumsum(sizes)[:-1]]
    return jnp.split(z, idx, axis=-1)


def rmsnorm(x, g):
    xf = x.astype(F32)
    y = xf * lax.rsqrt(jnp.mean(xf * xf, axis=-1, keepdims=True) + EPS)
    return (y * g.astype(F32)).astype(x.dtype)


def layernorm(x, g, b):
    xf = x.astype(F32)
    mu = jnp.mean(xf, axis=-1, keepdims=True)
    var = jnp.mean(jnp.square(xf - mu), axis=-1, keepdims=True)
    y = (xf - mu) * lax.rsqrt(var + EPS)
    return (y * g.astype(F32) + b.astype(F32)).astype(x.dtype)


def head_rmsnorm(h, g):
    y = h * lax.rsqrt(jnp.mean(h * h, axis=-1, keepdims=True) + EPS)
    return y.reshape(h.shape[0], h.shape[1], -1) * g.astype(F32)


def swiglu(h, w_gate, w_up, w_down):
    return (jax.nn.silu(h @ w_gate) * (h @ w_up)) @ w_down


def causal_dwconv(u, buf, w):
    full = jnp.concatenate([buf.astype(u.dtype), u], axis=1)
    out = lax.conv_general_dilated(full, w[:, None, :].astype(u.dtype), window_strides=(1,), padding='VALID',
                                   dimension_numbers=('NWC', 'WIO', 'NWC'), feature_group_count=u.shape[-1])
    return out, full[:, full.shape[1] - (w.shape[0] - 1):]


def _block_len(t):
    return CHUNK if t % CHUNK == 0 else t


def _to_blocks(a, nc, L):
    return a.reshape(a.shape[0], nc, L, *a.shape[2:]).swapaxes(0, 1)


def mlstm_chunkwise(q, k, v, log_i, log_f, c0, n0, m0):
    B, T, H, _ = q.shape
    L = _block_len(T)
    nc = T // L
    xs = tuple(_to_blocks(a, nc, L) for a in (q, k, v, log_i, log_f))
    causal = jnp.tril(jnp.ones((L, L), bool))

    def step(carry, xc):
        c, n, m = carry
        qc, kc, vc, ic, fc = xc
        bt = jnp.cumsum(fc, axis=1).transpose(0, 2, 1)
        it = ic.transpose(0, 2, 1)
        dmat = jnp.where(causal, bt[..., :, None] - bt[..., None, :] + it[..., None, :], -jnp.inf)
        inter = bt + m[..., None]
        m_row = jnp.maximum(inter, dmat.max(-1))
        s = jnp.einsum('blhd,bshd->bhls', qc, kc) * jnp.exp(dmat - m_row[..., None])
        w_inter = jnp.exp(inter - m_row).transpose(0, 2, 1)
        num = jnp.einsum('bhls,bshe->blhe', s, vc) + w_inter[..., None] * jnp.einsum('blhd,bhde->blhe', qc, c)
        den = s.sum(-1).transpose(0, 2, 1) + w_inter * jnp.einsum('blhd,bhd->blh', qc, n)
        h = num / jnp.maximum(jnp.abs(den), jnp.exp(-m_row).transpose(0, 2, 1))[..., None]
        b_last = bt[..., -1]
        g = b_last[..., None] - bt + it
        m_new = jnp.maximum(b_last + m, g.max(-1))
        w_s = jnp.exp(g - m_new[..., None])
        decay = jnp.exp(b_last + m - m_new)
        c_new = decay[..., None, None] * c + jnp.einsum('bhs,bshd,bshe->bhde', w_s, kc, vc)
        n_new = decay[..., None] * n + jnp.einsum('bhs,bshd->bhd', w_s, kc)
        return (c_new, n_new, m_new), h

    (c1, n1, m1), hs = lax.scan(step, (c0.astype(F32), n0.astype(F32), m0.astype(F32)), xs)
    return hs.swapaxes(0, 1).reshape(B, T, H, -1), c1, n1, m1


def hgrn2_chunkwise(q, k, v, log_f, s0):
    B, T, H, _ = q.shape
    L = _block_len(T)
    nc = T // L
    xs = tuple(_to_blocks(a, nc, L) for a in (q, k, v, log_f))
    causal = jnp.tril(jnp.ones((L, L), bool))[None, :, :, None, None]

    def step(s, xc):
        qc, kc, vc, gc = xc
        b = jnp.cumsum(gc, axis=1)
        diff = jnp.where(causal, b[:, :, None] - b[:, None, :], -jnp.inf)
        a = jnp.einsum('bthd,bshd,btshd->bhts', qc, kc, jnp.exp(diff))
        o = jnp.einsum('bhts,bshe->bthe', a, vc) + jnp.einsum('bthd,bhde->bthe', qc * jnp.exp(b), s)
        b_last = b[:, -1]
        s_new = jnp.exp(b_last)[..., None] * s + jnp.einsum('bshd,bshe->bhde', kc * jnp.exp(b_last[:, None] - b), vc)
        return s_new, o

    s1, os_ = lax.scan(step, s0.astype(F32), xs)
    return os_.swapaxes(0, 1).reshape(B, T, H, -1), s1


def even_mixer(h, buf_a, buf_b, w_in, conv_a, conv_b, conv_b_bias, ln_g, ln_b, w_out):
    xa, gate_b, gate_c, glu_v, glu_g = _split(h @ w_in, EVEN_SPLITS)
    ca, new_a = causal_dwconv(gate_c * xa, buf_a, conv_a)
    y_a = gate_b * ca
    cb, new_b = causal_dwconv(glu_v * jax.nn.sigmoid(glu_g), buf_b, conv_b)
    y_b = jax.nn.silu(layernorm(cb + conv_b_bias.astype(cb.dtype), ln_g, ln_b))
    return jnp.concatenate([y_a, y_b], axis=-1) @ w_out, new_a, new_b


def odd_mixer(h, c0, n0, m0, s0, w_in, bias_i, bias_f, norm_c, lb, norm_d, w_out):
    B, T, _ = h.shape
    q_c, k_c, v_c, o_c, q_d, f_d, i_d, g_d, i_c, f_c = _split(h @ w_in, ODD_SPLITS)
    heads = lambda a, n: a.astype(F32).reshape(B, T, n, -1)
    log_i = i_c.astype(F32) + bias_i.astype(F32)
    log_f = jax.nn.log_sigmoid(f_c.astype(F32) + bias_f.astype(F32))
    hc, c1, n1, m1 = mlstm_chunkwise(heads(q_c, H_C), heads(k_c, H_C) * (DK_C ** -0.5), heads(v_c, H_C),
                                     log_i, log_f, c0, n0, m0)
    y_c = jax.nn.sigmoid(o_c) * head_rmsnorm(hc, norm_c).astype(h.dtype)
    lbh = lb.astype(F32).reshape(H_D, DK_D)
    forget = lbh + (1.0 - lbh) * jax.nn.sigmoid(heads(f_d, H_D))
    hd, s1 = hgrn2_chunkwise(jax.nn.silu(heads(q_d, H_D)), 1.0 - forget, heads(i_d, H_D), jnp.log(forget), s0)
    y_d = jax.nn.sigmoid(g_d) * head_rmsnorm(hd, norm_d).astype(h.dtype)
    return jnp.concatenate([y_c, y_d], axis=-1) @ w_out, c1, n1, m1, s1


def _trunk(x, st_a, st_b, st_c, st_n, st_m, st_s, norm_g, norm_f, ffn_w_gate, ffn_w_up, ffn_w_down,
           even_w_in, even_conv_a, even_conv_b, even_conv_b_bias, even_ln_g, even_ln_b, even_w_out,
           odd_w_in, odd_bias_i, odd_bias_f, odd_norm_c, lb_all, odd_norm_d, odd_w_out):
    na, nb, ncs, nns, nms, nss = [], [], [], [], [], []
    for l in range(DEPTH):
        j = l // 2
        x = x + 0.5 * swiglu(rmsnorm(x, norm_g[l, 0]), ffn_w_gate[l, 0], ffn_w_up[l, 0], ffn_w_down[l, 0])
        h = rmsnorm(x, norm_g[l, 1])
        if l % 2 == 0:
            y, a1, b1 = even_mixer(h, st_a[j], st_b[j], even_w_in[j], even_conv_a[j], even_conv_b[j],
                                   even_conv_b_bias[j], even_ln_g[j], even_ln_b[j], even_w_out[j])
            na.append(a1)
            nb.append(b1)
        else:
            y, c1, n1, m1, s1 = odd_mixer(h, st_c[j], st_n[j], st_m[j], st_s[j], odd_w_in[j], odd_bias_i[j],
                                          odd_bias_f[j], odd_norm_c[j], lb_all[j], odd_norm_d[j], odd_w_out[j])
            ncs.append(c1)
            nns.append(n1)
            nms.append(m1)
            nss.append(s1)
        x = x + y.astype(x.dtype)
        x = x + 0.5 * swiglu(rmsnorm(x, norm_g[l, 2]), ffn_w_gate[l, 1], ffn_w_up[l, 1], ffn_w_down[l, 1])
    return (rmsnorm(x, norm_f), jnp.stack(na), jnp.stack(nb), jnp.stack(ncs), jnp.stack(nns),
            jnp.stack(nms), jnp.stack(nss))


def setup_inputs(seed: int = 0) -> dict:
    key = jax.random.key(seed)
    ks = jax.random.split(key, 32)
    nrm = lambda k, shape, s: jax.random.normal(k, shape, F32) * s
    f_bias = jnp.linspace(3.0, 6.0, H_C, dtype=F32)[None, :] + nrm(ks[20], (N_ODD, H_C), 0.1)
    return {
        "x_prompt": nrm(ks[0], (BATCH, SEQ, D_MODEL), 1.0),
        "x_sample": nrm(ks[1], (DEC_BATCH, DEC_SEQ, D_MODEL), 1.0),
        "state_conv_a": nrm(ks[2], (N_EVEN, DEC_BATCH, CONV_A - 1, D_A), 1.0),
        "state_conv_b": nrm(ks[3], (N_EVEN, DEC_BATCH, CONV_B - 1, D_B), 1.0),
        "state_mlstm_c": nrm(ks[4], (N_ODD, DEC_BATCH, H_C, DK_C, DV_C), 0.1),
        "state_mlstm_n": nrm(ks[5], (N_ODD, DEC_BATCH, H_C, DK_C), 0.1),
        "state_mlstm_m": nrm(ks[6], (N_ODD, DEC_BATCH, H_C), 0.5),
        "state_hgrn_s": nrm(ks[7], (N_ODD, DEC_BATCH, H_D, DK_D, DV_D), 0.5),
        "norm_g": 1.0 + nrm(ks[8], (DEPTH, 3, D_MODEL), 0.02),
        "norm_f": 1.0 + nrm(ks[9], (D_MODEL,), 0.02),
        "ffn_w_gate": nrm(ks[10], (DEPTH, 2, D_MODEL, D_FF), D_MODEL ** -0.5),
        "ffn_w_up": nrm(ks[11], (DEPTH, 2, D_MODEL, D_FF), D_MODEL ** -0.5),
        "ffn_w_down": nrm(ks[12], (DEPTH, 2, D_FF, D_MODEL), D_FF ** -0.5),
        "even_w_in": nrm(ks[13], (N_EVEN, D_MODEL, EVEN_IN), D_MODEL ** -0.5),
        "even_conv_a": nrm(ks[14], (N_EVEN, CONV_A, D_A), CONV_A ** -0.5),
        "even_conv_b": nrm(ks[15], (N_EVEN, CONV_B, D_B), CONV_B ** -0.5),
        "even_conv_b_bias": nrm(ks[16], (N_EVEN, D_B), 0.02),
        "even_ln_g": 1.0 + nrm(ks[17], (N_EVEN, D_B), 0.02),
        "even_ln_b": nrm(ks[18], (N_EVEN, D_B), 0.02),
        "even_w_out": nrm(ks[19], (N_EVEN, D_A + D_B, D_MODEL), (D_A + D_B) ** -0.5),
        "odd_w_in": nrm(ks[21], (N_ODD, D_MODEL, ODD_IN), D_MODEL ** -0.5),
        "odd_bias_i": nrm(ks[22], (N_ODD, H_C), 0.1),
        "odd_bias_f": f_bias,
        "odd_norm_c": 1.0 + nrm(ks[23], (N_ODD, D_C), 0.02),
        "odd_lb_logits": nrm(ks[24], (N_ODD, D_DK), 0.1),
        "odd_norm_d": 1.0 + nrm(ks[25], (N_ODD, D_DV), 0.02),
        "odd_w_out": nrm(ks[26], (N_ODD, D_C + D_DV, D_MODEL), (D_C + D_DV) ** -0.5),
    }


def reference(x_prompt, x_sample, state_conv_a, state_conv_b, state_mlstm_c, state_mlstm_n, state_mlstm_m,
              state_hgrn_s, norm_g, norm_f, ffn_w_gate, ffn_w_up, ffn_w_down, even_w_in, even_conv_a,
              even_conv_b, even_conv_b_bias, even_ln_g, even_ln_b, even_w_out, odd_w_in, odd_bias_i,
              odd_bias_f, odd_norm_c, odd_lb_logits, odd_norm_d, odd_w_out):
    lb_sm = jax.nn.softmax(odd_lb_logits.astype(F32), axis=0)
    lb_all = jnp.cumsum(lb_sm, axis=0) - lb_sm[0]
    weights = (norm_g, norm_f, ffn_w_gate, ffn_w_up, ffn_w_down, even_w_in, even_conv_a, even_conv_b,
               even_conv_b_bias, even_ln_g, even_ln_b, even_w_out, odd_w_in, odd_bias_i, odd_bias_f,
               odd_norm_c, lb_all, odd_norm_d, odd_w_out)
    z_a = jnp.zeros((N_EVEN, BATCH, CONV_A - 1, D_A), x_prompt.dtype)
    z_b = jnp.zeros((N_EVEN, BATCH, CONV_B - 1, D_B), x_prompt.dtype)
    z_c = jnp.zeros((N_ODD, BATCH, H_C, DK_C, DV_C), F32)
    z_n = jnp.zeros((N_ODD, BATCH, H_C, DK_C), F32)
    z_m = jnp.zeros((N_ODD, BATCH, H_C), F32)
    z_s = jnp.zeros((N_ODD, BATCH, H_D, DK_D, DV_D), F32)
    y_prompt, p_conv_a, p_conv_b, p_mlstm_c, p_mlstm_n, p_mlstm_m, p_hgrn_s = _trunk(
        x_prompt, z_a, z_b, z_c, z_n, z_m, z_s, *weights)
    y_sample, s_conv_a, s_conv_b, s_mlstm_c, s_mlstm_n, s_mlstm_m, s_hgrn_s = _trunk(
        x_sample, state_conv_a, state_conv_b, state_mlstm_c, state_mlstm_n, state_mlstm_m, state_hgrn_s,
        *weights)
    return (y_prompt, y_sample, p_conv_a, p_conv_b, p_mlstm_c, p_mlstm_n, p_mlstm_m, p_hgrn_s,
            s_conv_a, s_conv_b, s_mlstm_c, s_mlstm_n, s_mlstm_m, s_hgrn_s)
```

```python
import numpy as np
import concourse.bass as bass
import concourse.mybir as mybir
from concourse.bass_utils import run_bass_kernel_spmd

F32 = mybir.dt.float32
BF16 = mybir.dt.bfloat16
AF = mybir.ActivationFunctionType
ALU = mybir.AluOpType
AX = mybir.AxisListType

D = 2048
NCH = 16
DFF = 5632
NT = 1056
NPR = 1024
NSM = 32
TILES = [(0, 512), (512, 512), (1024, 32)]
DEPTH = 4
EPS = 1e-6
EVEN_IN = 5120
ODD_IN = 8208
NEG = -1.0e30
RING_UNITS = 5
ARENA_F32 = 13100


def xt(c):
    return [("x", c, ti) for ti in range(3)]


class Sched:
    ENG = ["pe", "act", "dve", "pool", "sp"]

    def __init__(self, nc, sems):
        self.nc = nc
        self.sems = sems
        self.free = list(range(len(sems)))
        self.e = {}
        for n in self.ENG:
            self.e[n] = dict(sem=self.free.pop(0), cnt=0, waited={}, ops=[])
        self.lastw = {}
        self.readers = {}
        self.dq = {}
        for q in ("sp", "pool", "act"):
            self.dq[q] = dict(pool=[self.free.pop(0) for _ in range(8)], pos=0)
        self.dcnt = {}
        self.cc_sems = []

    def _deps(self, eng, reads, writes):
        hard, soft = [], []
        for t in reads:
            w = self.lastw.get(t)
            if w is not None:
                hard.extend(w)
        for t in writes:
            w = self.lastw.get(t)
            if w is not None:
                hard.extend(w)
            r = self.readers.get(t)
            if r:
                soft.extend(r.items())
        return hard, soft

    def _commit(self, tok, reads, writes):
        toks = tok if isinstance(tok, list) else [tok]
        for t in reads:
            r = self.readers.setdefault(t, {})
            for tk in toks:
                if r.get(tk[0], 0) < tk[1]:
                    r[tk[0]] = tk[1]
        for t in writes:
            self.lastw[t] = list(toks)
            self.readers[t] = {}

    def _waits(self, eng, hard, soft):
        E = self.e[eng]
        out = []
        for kind, lst in (("h", hard), ("s", soft)):
            for (si, val) in lst:
                if si == E["sem"]:
                    if eng == "pe" or kind == "s":
                        continue
                if E["waited"].get(si, 0) >= val:
                    continue
                E["waited"][si] = val
                out.append((si, val))
        return out

    def op(self, eng, fn, reads=(), writes=()):
        E = self.e[eng]
        hard, soft = self._deps(eng, reads, writes)
        waits = self._waits(eng, hard, soft)
        E["cnt"] += 1
        tok = (E["sem"], E["cnt"])
        E["ops"].append((waits, fn, ("inc", E["sem"], 1)))
        self._commit(tok, reads, writes)
        return tok

    def dma(self, q, out, in_, reads=(), writes=(), **kw):
        E = self.e[q]
        Q = self.dq[q]
        si = Q["pool"][Q["pos"] % len(Q["pool"])]
        Q["pos"] += 1
        prev = self.dcnt.get(si, 0)
        hard, soft = self._deps(q, reads, writes)
        if prev:
            hard.append((si, prev))
        waits = self._waits(q, hard, soft)
        self.dcnt[si] = prev + 16
        tok = (si, prev + 16)
        E["ops"].append((waits, lambda h: h.dma_start(out=out, in_=in_, **kw), ("inc", si, 16)))
        self._commit(tok, reads, writes)
        return tok

    def dma_multi(self, q, pieces, reads=(), writes=(), **kw):
        E = self.e[q]
        Q = self.dq[q]
        hard, soft = self._deps(q, reads, writes)
        toks = []
        for (out, in_) in pieces:
            si = Q["pool"][Q["pos"] % len(Q["pool"])]
            Q["pos"] += 1
            prev = self.dcnt.get(si, 0)
            h2 = list(hard)
            if prev:
                h2.append((si, prev))
            waits = self._waits(q, h2, soft)
            self.dcnt[si] = prev + 16
            toks.append((si, prev + 16))
            E["ops"].append((waits, lambda h, out=out, in_=in_: h.dma_start(out=out, in_=in_, **kw), ("inc", si, 16)))
        self._commit(toks, reads, writes)
        return toks

    def collective(self, fn, reads=(), writes=()):
        E = self.e["pool"]
        si = self.free.pop(0)
        hard, soft = self._deps("pool", reads, writes)
        waits = self._waits("pool", hard, soft)
        tok = (si, 1)
        E["ops"].append((waits, fn, ("inc", si, None)))
        self._commit(tok, reads, writes)
        return tok

    def barrier(self, engines=("pe", "act", "dve", "sp")):
        toks = []
        for n in engines:
            X = self.e[n]
            if X["cnt"]:
                toks.append((X["sem"], X["cnt"]))
        for q in ("sp", "act"):
            for si in self.dq[q]["pool"]:
                v = self.dcnt.get(si, 0)
                if v:
                    toks.append((si, v))
        for n in engines:
            E = self.e[n]
            waits = []
            for si, v in toks:
                if si == E["sem"]:
                    continue
                if E["waited"].get(si, 0) >= v:
                    continue
                E["waited"][si] = v
                waits.append((si, v))
            if waits:
                E["ops"].append((waits, None, None))

    def finish(self):
        E = self.e["sp"]
        waits = []
        for si, v in self.dcnt.items():
            if E["waited"].get(si, 0) < v:
                waits.append((si, v))
        for n in self.ENG:
            if n == "sp":
                continue
            X = self.e[n]
            if X["cnt"]:
                waits.append((X["sem"], X["cnt"]))
        E["ops"].append((waits, None, None))

    def emit(self, block):
        names = dict(pe="tensor", act="scalar", dve="vector", pool="gpsimd", sp="sync")
        sems = self.sems

        def mk(eng):
            ops = self.e[eng]["ops"]

            def body(h):
                for waits, fn, inc in ops:
                    for si, v in waits:
                        h.wait_ge(sems[si], v)
                    if fn is None:
                        continue
                    ins = fn(h)
                    if inc[2] is None:
                        ins.then_inc(sems[inc[1]])
                    else:
                        ins.then_inc(sems[inc[1]], inc[2])
            return body

        for eng in self.ENG:
            if self.e[eng]["ops"]:
                getattr(block, names[eng])(mk(eng))


class Builder:
    def __init__(self, nc, S, T, cfg):
        self.nc, self.S, self.T, self.cfg = nc, S, T, cfg
        self.ring_pos = 0
        self.RU = T["ring"].shape[1]
        self.ps_rr = 0

    def wload(self, src_ap, nunits, shape):
        if self.ring_pos + nunits > self.RU:
            self.ring_pos = 0
        u0 = self.ring_pos
        self.ring_pos += nunits
        tags = [("ring", u) for u in range(u0, u0 + nunits)]
        flat = self.T["ring"][:, u0:u0 + nunits, :]
        n = 1
        for s in shape:
            n *= s
        assert n <= nunits * 4096, (shape, nunits)
        view = self.T["ring"][:, u0:u0 + nunits, :].rearrange("p u f -> p (u f)")[:, 0:n]
        if len(shape) == 2:
            view = view.rearrange("p (a b) -> p a b", b=shape[1])
        elif len(shape) == 3:
            view = view.rearrange("p (a b c) -> p a b c", b=shape[1], c=shape[2])
        if isinstance(src_ap, list):
            self.S.dma_multi("pool", [(view[:, i], a) for i, a in enumerate(src_ap)], reads=(), writes=tags)
        else:
            self.S.dma("pool", view, src_ap, reads=(), writes=tags)
        return view, tags

    def psum(self):
        i = self.ps_rr % 8
        self.ps_rr += 1
        return self.T["ps"][i], ("ps", i)

    def mm(self, out, pairs, reads, writes):
        def fn(h, out=out, pairs=pairs):
            n = len(pairs)
            ins = None
            for i, (l, r) in enumerate(pairs):
                ins = h.matmul(out, lhsT=l, rhs=r, start=(i == 0), stop=(i == n - 1))
            return ins
        return self.S.op("pe", fn, reads=reads, writes=writes)

    def tr(self, out, in_, ident, reads, writes):
        return self.S.op("pe", lambda h: h.transpose(out, in_, ident), reads=reads, writes=writes)

    def phase(self, specs):
        self.S.barrier()
        off = 0
        out = {}
        offs = {}
        for sp in specs:
            name, shape, dt = sp[0], list(sp[1]), sp[2]
            n = 1
            for v in shape:
                n *= v
            words = n if dt == F32 else (n + 1) // 2
            if len(sp) > 3:
                o = offs[sp[3]]
            else:
                o = off
                off += words
            offs[name] = o
            v = self.T["arena"][:, o:o + words]
            if dt == BF16:
                v = v.bitcast(BF16)[:, 0:n]
            if len(shape) == 2:
                v = v.rearrange("p (a b) -> p a b", b=shape[1])
            elif len(shape) == 3:
                v = v.rearrange("p (a b c) -> p a b c", b=shape[1], c=shape[2])
            out[name] = v
        assert off <= ARENA_F32, (off, ARENA_F32)
        self.T.update(out)
        return out

    def load_consts(self, A):
        S, T = self.S, self.T
        S.dma("sp", T["cst"][:], A["cst"], writes=[("cst",)])
        S.dma("sp", T["ng"][:], A["norm_g"], writes=[("ng",)])
        S.dma("sp", T["nf"][:], A["norm_f"], writes=[("nf",)])
        S.dma("sp", T["pe_"][:], A["pe"], writes=[("pe_",)])
        S.dma("sp", T["coef"][:], A["coef"], writes=[("coef",)])
        if "po" in A:
            S.dma("sp", T["po"][:], A["po"], writes=[("po",)])
            S.dma("sp", T["pg"][:], A["pg"], writes=[("pg",)])
        S.op("dve", lambda h: h.memset(T["ones_bf"][:], 1.0), writes=[("ones_bf",)])

    def load_x(self, A):
        S, T = self.S, self.T
        P = self.phase([("stage0", [D], F32), ("stage1", [D], F32)])
        T["stage"] = [P["stage0"], P["stage1"]]
        ident = T["cst"][:, 0:128]
        blocks = [(b * 128, 128) for b in range(8)] + [(1024, 32)]
        for bi, (t0, n) in enumerate(blocks):
            st = T["stage"][bi % 2]
            stag = ("stage", bi % 2)
            S.dma("sp", st[0:n, :], A["xin"][t0:t0 + n, :], writes=[stag])
            for q in range(4):
                ps, ptag = self.psum()
                for cc in range(4):
                    c = q * 4 + cc
                    self.tr(ps[:, cc * 128:cc * 128 + n], st[0:n, c * 128:(c + 1) * 128], ident[0:n, 0:n],
                            reads=[stag, ("cst",)], writes=[ptag])
                src = ps[:, :].rearrange("p (c t) -> p c t", t=128)[:, :, 0:n]
                dst = T["x"][:, q * 4:q * 4 + 4, t0:t0 + n]
                eng = "act" if (q % 2) else "dve"
                if eng == "act":
                    S.op("act", lambda h, d=dst, s=src: h.copy(out=d, in_=s), reads=[ptag],
                         writes=[("x", c, min(bi // 4, 2)) for c in range(q * 4, q * 4 + 4)])
                else:
                    S.op("dve", lambda h, d=dst, s=src: h.tensor_copy(out=d, in_=s), reads=[ptag],
                         writes=[("x", c, min(bi // 4, 2)) for c in range(q * 4, q * 4 + 4)])

    def rstd_from_x(self):
        S, T = self.S, self.T
        pss = [self.psum() for _ in TILES]
        for c in range(NCH):
            sq = T["sq"][c % 2]
            sqt = ("sq", c % 2)
            S.op("act", lambda h, o=sq, i=T["x"][:, c, :]: h.activation(out=o[:], in_=i, func=AF.Square),
                 reads=xt(c), writes=[sqt])
            for (t0, n), (ps, ptag) in zip(TILES, pss):
                def fn(h, ps=ps, sq=sq, t0=t0, n=n, c=c):
                    return h.matmul(ps[:, 0:n], lhsT=T["ones_bf"][:], rhs=sq[:, t0:t0 + n], start=(c == 0),
                                    stop=(c == NCH - 1))
                S.op("pe", fn, reads=[sqt, ("ones_bf",)], writes=[ptag])
        for (t0, n), (ps, ptag) in zip(TILES, pss):
            S.op("act", lambda h, ps=ps, t0=t0, n=n: h.activation(out=T["rstd"][:, t0:t0 + n], in_=ps[:, 0:n],
                                                                  func=AF.Sqrt, scale=1.0 / D,
                                                                  bias=T["cst"][:, 256:257]),
                 reads=[ptag, ("cst",)], writes=[("rstd", t0)])
            S.op("dve", lambda h, t0=t0, n=n: h.reciprocal(out=T["rstd"][:, t0:t0 + n], in_=T["rstd"][:, t0:t0 + n]),
                 reads=[("rstd", t0)], writes=[("rstd", t0)])

    def norm_to_xn(self, gidx):
        S, T = self.S, self.T
        self.rstd_from_x()
        rs = [("rstd", t0) for t0, _ in TILES]
        for c in range(NCH):
            S.op("dve", lambda h, c=c: h.scalar_tensor_tensor(out=T["xn"][:, c, :], in0=T["x"][:, c, :],
                                                               scalar=T["ng"][:, gidx, c:c + 1], in1=T["rstd"][:],
                                                               op0=ALU.mult, op1=ALU.mult),
                 reads=xt(c) + [("ng",)] + rs, writes=[("xn", c)])

    def ffn(self, A, l, i):
        S, T = self.S, self.T
        G = 2
        P = self.phase([("hg0", [G, NT], BF16), ("hg1", [G, NT], BF16), ("silu0", [512], F32), ("silu1", [512], F32)])
        T["hg"] = [P["hg0"], P["hg1"]]
        T["silu"] = [P["silu0"], P["silu1"]]
        xn_tags = [("xn", c) for c in range(NCH)]
        ngroups = self.cfg.get("ffn_groups", DFF // (128 * G))
        for g in range(ngroups):
            c0 = g * 128 * G
            wg, wg_t = self.wload(A["ffn_w_gate"][l, i, :, c0:c0 + 128 * G].rearrange("(k p) c -> p k c", p=128), 1,
                                  [NCH, 128 * G])
            wu, wu_t = self.wload(A["ffn_w_up"][l, i, :, c0:c0 + 128 * G].rearrange("(k p) c -> p k c", p=128), 1,
                                  [NCH, 128 * G])
            wd, wd_t = self.wload(A["ffn_w_down"][l, i, c0:c0 + 128 * G, :].rearrange("(k p) c -> p k c", p=128), 1,
                                  [G, D])
            hg = T["hg"][g % 2]
            for hc in range(G):
                for (t0, n) in TILES:
                    psg, tg = self.psum()
                    psu, tu = self.psum()
                    self.mm(psg[:, 0:n], [(wg[:, k, hc * 128:(hc + 1) * 128], T["xn"][:, k, t0:t0 + n]) for k in range(NCH)],
                            reads=wg_t + xn_tags, writes=[tg])
                    self.mm(psu[:, 0:n], [(wu[:, k, hc * 128:(hc + 1) * 128], T["xn"][:, k, t0:t0 + n]) for k in range(NCH)],
                            reads=wu_t + xn_tags, writes=[tu])
                    self.silu_rr = getattr(self, "silu_rr", 0) + 1
                    sl = T["silu"][self.silu_rr % 2]
                    slt = ("silu", self.silu_rr % 2)
                    S.op("act", lambda h, sl=sl, psg=psg, n=n: h.activation(out=sl[:, 0:n], in_=psg[:, 0:n], func=AF.Silu),
                         reads=[tg], writes=[slt])
                    S.op("dve", lambda h, sl=sl, psu=psu, n=n, t0=t0, hc=hc, hg=hg: h.tensor_tensor(
                        out=hg[:, hc, t0:t0 + n], in0=sl[:, 0:n], in1=psu[:, 0:n], op=ALU.mult),
                         reads=[slt, tu], writes=[("hg", g % 2, hc, t0)])
            for m in range(NCH):
                for ti, (t0, n) in enumerate(TILES):
                    psd, td = self.psum()
                    self.mm(psd[:, 0:n], [(wd[:, hc, m * 128:(m + 1) * 128], hg[:, hc, t0:t0 + n]) for hc in range(G)],
                            reads=wd_t + [("hg", g % 2, hc, t0) for hc in range(G)], writes=[td])
                    S.op("dve", lambda h, psd=psd, m=m, t0=t0, n=n: h.scalar_tensor_tensor(
                        out=T["x"][:, m, t0:t0 + n], in0=psd[:, 0:n], scalar=0.5, in1=T["x"][:, m, t0:t0 + n],
                        op0=ALU.mult, op1=ALU.add), reads=[td, ("x", m, ti)], writes=[("x", m, ti)])


    def even_mixer(self, A, j, l):
        S, T = self.S, self.T
        self.norm_to_xn(l * 3 + 1)
        P = self.phase([("cbf", [8, NT], BF16), ("ub", [1116], F32), ("acc", [1088], F32), ("sg0", [512], F32),
                        ("sg1", [512], F32), ("ua", [1060], F32), ("ca", [1060], F32), ("gbs", [NT], F32),
                        ("ya0", [NT], BF16), ("ya1", [NT], BF16), ("hin", [8, 32], F32), ("stin", [8, 32], F32),
                        ("stc", [2, 8, 32], F32), ("hall", [4, 256], F32, "acc"), ("hbuf", [8, 32], F32, "ca")])
        xn_tags = [("xn", c) for c in range(NCH)]
        sg = [P["sg0"], P["sg1"]]
        pe_ = T["pe_"]
        win = A["even_w_in"][j].rearrange("(k p) (s c q) -> p s c k q", p=128, s=5, c=8, q=128)
        if not self.cfg.get("skip_stin"):
            S.dma("sp", P["stin"], A["st_conv"][j], writes=[("stin",)])
        rr = [0]

        def sgbuf():
            rr[0] += 1
            return sg[rr[0] % 2], ("sg", rr[0] % 2)

        def proj(w, si, t0, n, wt):
            ps, pt = self.psum()
            self.mm(ps[:, 0:n], [(w[:, si, k, :], T["xn"][:, k, t0:t0 + n]) for k in range(NCH)],
                    reads=wt + xn_tags, writes=[pt])
            return ps, pt

        es_ = self.cfg.get("even_stop")
        if es_ == 11:
            return
        for c in range(8):
            wB, wBt = self.wload([win[:, 3, c], win[:, 4, c]], 1, [2, NCH, 128])
            wA, wAt = self.wload([win[:, 0, c], win[:, 2, c]], 1, [2, NCH, 128])
            t0, n = 992, 32
            psx, tx = proj(wA, 0, t0, n, wAt)
            psc, tc = proj(wA, 1, t0, n, wAt)
            psv, tv = proj(wB, 0, t0, n, wBt)
            psg, tg = proj(wB, 1, t0, n, wBt)
            if es_ == 12:
                continue
            b0, bt0 = sgbuf()
            S.op("act", lambda h, b0=b0, psg=psg: h.activation(out=b0[:, 0:32], in_=psg[:, 0:32], func=AF.Sigmoid),
                 reads=[tg], writes=[bt0])
            S.op("dve", lambda h, b0=b0, psv=psv, c=c: h.tensor_tensor(out=P["hbuf"][:, c, 0:30], in0=psv[:, 2:32],
                                                                     in1=b0[:, 2:32], op=ALU.mult),
                 reads=[tv, bt0], writes=[("hbuf",)])
            b1, bt1 = sgbuf()
            S.op("act", lambda h, b1=b1, psx=psx: h.copy(out=b1[:, 0:32], in_=psx[:, 0:32]), reads=[tx], writes=[bt1])
            S.op("dve", lambda h, b1=b1, psc=psc, c=c: h.tensor_tensor(out=P["hbuf"][:, c, 30:32], in0=psc[:, 30:32],
                                                                     in1=b1[:, 30:32], op=ALU.mult),
                 reads=[tc, bt1], writes=[("hbuf",)])
        if es_ in (12, 13):
            return
        S.dma("sp", A["hg_in"], P["hbuf"].rearrange("p c t -> p (c t)"), reads=[("hbuf",)], writes=[("hg_in",)])
        if es_ == 14:
            return
        if self.cfg.get("no_cc"):
            S.dma_multi("sp", [(A["hg_out"][r * 128:(r + 1) * 128, :], A["hg_in"]) for r in range(4)],
                        reads=[("hg_in",)], writes=[("hg_out",)])
        else:
            S.collective(lambda h: h.collective_compute("AllGather", ALU.bypass, replica_groups=[[0, 1, 2, 3], [4, 5, 6, 7]],
                                                        ins=[A["hg_in"]], outs=[A["hg_out"]]),
                         reads=[("hg_in",)], writes=[("hg_out",)])
        S.dma("sp", P["hall"], A["hg_out"].rearrange("(r p) f -> p r f", p=128), reads=[("hg_out",)],
              writes=[("hall",)])
        hin2 = P["hin"].rearrange("p c t -> p (c t)")
        S.op("dve", lambda h: h.tensor_scalar(out=hin2, in0=P["hall"][:, 0, :], scalar1=T["coef"][:, 0:1], scalar2=None,
                                              op0=ALU.mult), reads=[("hall",), ("coef",)], writes=[("hin",)])
        for q in range(1, 4):
            S.op("dve", lambda h, q=q: h.scalar_tensor_tensor(out=hin2, in0=P["hall"][:, q, :],
                                                               scalar=T["coef"][:, q:q + 1], in1=hin2,
                                                               op0=ALU.mult, op1=ALU.add),
                 reads=[("hall",), ("coef",), ("hin",)], writes=[("hin",)])

        if self.cfg.get("even_stop") == 1:
            return
        UBO = [30, 542, 1084]
        UAO = [2, 514, 1028]
        for c in range(8):
            wB, wBt = self.wload([win[:, 3, c], win[:, 4, c]], 1, [2, NCH, 128])
            wA, wAt = self.wload([win[:, 0, c], win[:, 1, c], win[:, 2, c]], 2, [3, NCH, 128])
            S.op("act", lambda h, c=c: h.copy(out=P["ub"][:, 0:30], in_=P["hin"][:, c, 0:30]),
                 reads=[("hin",)], writes=[("ub", 0)])
            S.op("act", lambda h, c=c: h.copy(out=P["ub"][:, 1054:1084], in_=P["stin"][:, c, 0:30]),
                 reads=[("stin",)], writes=[("ub", 0)])
            for ti, (t0, n) in enumerate(TILES):
                psv, tv = proj(wB, 0, t0, n, wBt)
                psg, tg = proj(wB, 1, t0, n, wBt)
                b0, bt0 = sgbuf()
                S.op("act", lambda h, b0=b0, psg=psg, n=n: h.activation(out=b0[:, 0:n], in_=psg[:, 0:n], func=AF.Sigmoid),
                     reads=[tg], writes=[bt0])
                S.op("dve", lambda h, b0=b0, psv=psv, n=n, o=UBO[ti]: h.tensor_tensor(
                    out=P["ub"][:, o:o + n], in0=psv[:, 0:n], in1=b0[:, 0:n], op=ALU.mult),
                     reads=[tv, bt0], writes=[("ub", 0)])
            S.op("dve", lambda h, c=c: h.tensor_scalar(out=P["acc"][:, 0:1086], in0=P["ub"][:, 0:1086],
                                                       scalar1=pe_[:, j, c, 0:1], scalar2=pe_[:, j, c, 34:35],
                                                       op0=ALU.mult, op1=ALU.add),
                 reads=[("ub", 0), ("pe_",)], writes=[("acc",)])
            for k in range(1, 31):
                S.op("dve", lambda h, c=c, k=k: h.scalar_tensor_tensor(
                    out=P["acc"][:, 0:1086], in0=P["ub"][:, k:k + 1086], scalar=pe_[:, j, c, k:k + 1],
                    in1=P["acc"][:, 0:1086], op0=ALU.mult, op1=ALU.add),
                     reads=[("ub", 0), ("acc",)], writes=[("acc",)])
            S.op("act", lambda h, c=c: h.copy(out=P["cbf"][:, c, 0:1024], in_=P["acc"][:, 0:1024]),
                 reads=[("acc",), ("ub", 0)], writes=[("cbf", c)])
            S.op("act", lambda h, c=c: h.copy(out=P["cbf"][:, c, 1024:1056], in_=P["acc"][:, 1054:1086]),
                 reads=[("acc",)], writes=[("cbf", c)])
            S.op("act", lambda h, c=c: h.copy(out=P["stc"][:, 0, c, 0:30], in_=P["ub"][:, 1024:1054]),
                 reads=[("ub", 0)], writes=[("stc",)])
            S.op("act", lambda h, c=c: h.copy(out=P["stc"][:, 1, c, 0:30], in_=P["ub"][:, 1086:1116]),
                 reads=[("ub", 0)], writes=[("stc",)])
            S.op("act", lambda h, c=c: h.copy(out=P["ua"][:, 0:2], in_=P["hin"][:, c, 30:32]),
                 reads=[("hin",)], writes=[("ua",)])
            S.op("act", lambda h, c=c: h.copy(out=P["ua"][:, 1026:1028], in_=P["stin"][:, c, 30:32]),
                 reads=[("stin",)], writes=[("ua",)])
            for ti, (t0, n) in enumerate(TILES):
                psx, tx = proj(wA, 0, t0, n, wAt)
                psb, tb = proj(wA, 1, t0, n, wAt)
                psc, tc = proj(wA, 2, t0, n, wAt)
                b1, bt1 = sgbuf()
                S.op("act", lambda h, b1=b1, psx=psx, n=n: h.copy(out=b1[:, 0:n], in_=psx[:, 0:n]), reads=[tx], writes=[bt1])
                S.op("dve", lambda h, b1=b1, psc=psc, n=n, o=UAO[ti]: h.tensor_tensor(
                    out=P["ua"][:, o:o + n], in0=psc[:, 0:n], in1=b1[:, 0:n], op=ALU.mult),
                     reads=[tc, bt1], writes=[("ua",)])
                S.op("act", lambda h, psb=psb, t0=t0, n=n: h.copy(out=P["gbs"][:, t0:t0 + n], in_=psb[:, 0:n]),
                     reads=[tb], writes=[("gbs",)])
            S.op("dve", lambda h, c=c: h.tensor_scalar(out=P["ca"][:, 0:1058], in0=P["ua"][:, 0:1058],
                                                       scalar1=pe_[:, j, c, 31:32], scalar2=None, op0=ALU.mult),
                 reads=[("ua",), ("pe_",)], writes=[("ca",)])
            for k in (1, 2):
                S.op("dve", lambda h, c=c, k=k: h.scalar_tensor_tensor(
                    out=P["ca"][:, 0:1058], in0=P["ua"][:, k:k + 1058], scalar=pe_[:, j, c, 31 + k:32 + k],
                    in1=P["ca"][:, 0:1058], op0=ALU.mult, op1=ALU.add), reads=[("ua",), ("ca",)], writes=[("ca",)])
            ya = P["ya%d" % (c % 2)]
            yat = ("ya", c % 2)
            S.op("dve", lambda h, ya=ya: h.tensor_tensor(out=ya[:, 0:1024], in0=P["gbs"][:, 0:1024], in1=P["ca"][:, 0:1024],
                                                         op=ALU.mult), reads=[("gbs",), ("ca",)], writes=[yat])
            S.op("dve", lambda h, ya=ya: h.tensor_tensor(out=ya[:, 1024:1056], in0=P["gbs"][:, 1024:1056],
                                                         in1=P["ca"][:, 1026:1058], op=ALU.mult),
                 reads=[("gbs",), ("ca",)], writes=[yat])
            S.dma("sp", A["ya_scr"][:, c, :], ya.bitcast(F32), reads=[yat], writes=[("ya_scr",)])
            S.op("act", lambda h, c=c: h.copy(out=P["stc"][:, 0, c, 30:32], in_=P["ua"][:, 1024:1026]),
                 reads=[("ua",)], writes=[("stc",)])
            S.op("act", lambda h, c=c: h.copy(out=P["stc"][:, 1, c, 30:32], in_=P["ua"][:, 1058:1060]),
                 reads=[("ua",)], writes=[("stc",)])
        S.dma("sp", A["out_conv"][2 * j:2 * j + 2].rearrange("g p f -> p g f"),
              P["stc"].rearrange("p g c t -> p g (c t)"), reads=[("stc",)])

        if self.cfg.get("even_stop") == 2:
            return
        meanb, varb = P["acc"], P["ub"]
        for ti, (t0, n) in enumerate(TILES):
            psa, ta = self.psum()
            psq, tq = self.psum()
            for c in range(8):
                sq = T["sq"][c % 2]
                sqt = ("sq", c % 2)
                S.op("act", lambda h, sq=sq, c=c, t0=t0, n=n: h.activation(out=sq[:, 0:n], in_=P["cbf"][:, c, t0:t0 + n],
                                                                           func=AF.Square),
                     reads=[("cbf", c)], writes=[sqt])
                S.op("pe", lambda h, psa=psa, c=c, t0=t0, n=n: h.matmul(psa[:, 0:n], lhsT=T["ones_bf"][:],
                                                                        rhs=P["cbf"][:, c, t0:t0 + n], start=(c == 0),
                                                                        stop=(c == 7)),
                     reads=[("cbf", c), ("ones_bf",)], writes=[ta])
                S.op("pe", lambda h, psq=psq, sq=sq, c=c, n=n: h.matmul(psq[:, 0:n], lhsT=T["ones_bf"][:], rhs=sq[:, 0:n],
                                                                        start=(c == 0), stop=(c == 7)),
                     reads=[sqt, ("ones_bf",)], writes=[tq])
            S.op("act", lambda h, psa=psa, t0=t0, n=n: h.activation(out=meanb[:, t0:t0 + n], in_=psa[:, 0:n],
                                                                    func=AF.Copy, scale=1.0 / 1024),
                 reads=[ta], writes=[("meanb", ti)])
            S.op("dve", lambda h, t0=t0, n=n: h.tensor_tensor(out=P["gbs"][:, t0:t0 + n], in0=meanb[:, t0:t0 + n],
                                                              in1=meanb[:, t0:t0 + n], op=ALU.mult),
                 reads=[("meanb", ti)], writes=[("msq", ti)])
            S.op("dve", lambda h, psq=psq, t0=t0, n=n: h.scalar_tensor_tensor(
                out=varb[:, t0:t0 + n], in0=psq[:, 0:n], scalar=1.0 / 1024, in1=P["gbs"][:, t0:t0 + n],
                op0=ALU.mult, op1=ALU.subtract), reads=[tq, ("msq", ti)], writes=[("varb", ti)])
            S.op("act", lambda h, t0=t0, n=n: h.activation(out=varb[:, t0:t0 + n], in_=varb[:, t0:t0 + n], func=AF.Sqrt,
                                                           bias=T["cst"][:, 256:257]),
                 reads=[("varb", ti), ("cst",)], writes=[("varb", ti)])
            S.op("dve", lambda h, t0=t0, n=n: h.reciprocal(out=varb[:, t0:t0 + n], in_=varb[:, t0:t0 + n]),
                 reads=[("varb", ti)], writes=[("varb", ti)])
        st = [("meanb", ti) for ti in range(3)] + [("varb", ti) for ti in range(3)]
        tmps = [P["ua"], P["ca"]]
        for c in range(8):
            tm = tmps[c % 2]
            tmt = ("lntmp", c % 2)
            S.op("dve", lambda h, tm=tm, c=c: h.tensor_tensor(out=tm[:, 0:NT], in0=P["cbf"][:, c, :], in1=meanb[:, 0:NT],
                                                              op=ALU.subtract), reads=[("cbf", c)] + st, writes=[tmt])
            S.op("dve", lambda h, tm=tm: h.tensor_tensor(out=tm[:, 0:NT], in0=tm[:, 0:NT], in1=varb[:, 0:NT], op=ALU.mult),
                 reads=[tmt] + st, writes=[tmt])
            S.op("act", lambda h, tm=tm, c=c: h.activation(out=T["xn"][:, 8 + c, :], in_=tm[:, 0:NT], func=AF.Silu,
                                                           scale=pe_[:, j, c, 35:36], bias=pe_[:, j, c, 36:37]),
                 reads=[tmt, ("pe_",)], writes=[("xn", 8 + c)])
        if self.cfg.get("even_stop") == 3:
            return
        S.dma("sp", T["xn"][:, 0:8, :].bitcast(F32), A["ya_scr"], reads=[("ya_scr",)],
              writes=[("xn", c) for c in range(8)])
        self.out_proj(A["even_w_out"][j])


    def odd_mixer(self, A, j, l):
        S, T, cfg = self.S, self.T, self.cfg
        self.norm_to_xn(l * 3 + 1)
        CH = [(b * 128, 128) for b in range(8)] + [(1024, 32)]
        P = self.phase([("cs", [NT], F32), ("U", [NT], F32),
                        ("pT", [9, 8], F32), ("pTl", [8, 8], F32), ("flT", [9, 8], F32), ("decB", [8, 12], F32),
                        ("hs", [64], F32), ("bc", [64], F32), ("small", [128], F32),
                        ("w0", [NT], F32), ("w1", [NT], F32), ("w2", [1060], F32), ("w3", [NT], F32), ("w4", [NT], F32),
                        ("b0", [NT], BF16), ("b1", [NT], BF16), ("b2", [9, 128], BF16), ("b3", [9, 130], BF16),
                        ("yb0", [NT], BF16), ("yb1", [NT], BF16),
                        ("st", [132], F32), ("stb", [132], BF16), ("t0", [132], F32), ("t1", [132], F32),
                        ("m0", [128], BF16), ("m1", [128], BF16), ("m2", [128], BF16), ("m3", [128], BF16),
                        ("gl", [4, 80], F32), ("cl", [4, 132], F32), ("cvec", [16], F32), ("contrib", [24], F32), ("lbv", [16], F32)])
        P["hm"] = P["w4"]
        zero_b = T["cst"][:, 259:260]
        S.op("dve", lambda h: h.memset(P["b3"][:, :, :], 0.0), writes=[("b3",)])
        S.op("dve", lambda h: h.memset(P["st"][:, :], 0.0), writes=[("st",)])
        S.op("dve", lambda h: h.memset(P["stb"][:, :], 0.0), writes=[("stb",)])
        xn_tags = [("xn", c) for c in range(NCH)]
        cst, coef, po, pg = T["cst"], T["coef"], T["po"], T["pg"]
        ident = cst[:, 0:128]
        mask = cst[:, 128:256]
        one_c = cst[:, 257:258]
        eps_c = cst[:, 256:257]
        neg_c = cst[:, 258:259]
        win = A["odd_w_in"][j].rearrange("(k p) c -> p k c", p=128)
        SEG = [(0, 1024), (1024, 32)]
        uid = [0]

        PHYS = dict(ktok="b2", pv="b3", fg="w1", glog="w0", qq="w3", sgg="w4", vtok="b2", BB="w2", kgt="b3", qT="b0", kT="b1", sm="small",
                    smd="small", sm2="bc9")

        def tg(name):
            return (PHYS[name],)

        def fproj(w, wt, t0, n, m=128):
            ps, pt = self.psum()
            self.mm(ps[0:m, 0:n], [(w[:, k, :], T["xn"][:, k, t0:t0 + n]) for k in range(NCH)], reads=wt + xn_tags, writes=[pt])
            return ps, pt

        def tproj(w, wt, t0, n, col):
            ps, pt = col[0], col[1]
            self.mm(ps[0:n, col[2]:col[2] + 128], [(T["xn"][:, k, t0:t0 + n], w[:, k, :]) for k in range(NCH)],
                    reads=wt + xn_tags, writes=[pt])

        wgate, wgt = self.wload(win[:, :, 8192:8208], 1, [NCH, 16])
        S.op("act", lambda h: h.mul(out=P["cvec"][0:8, 0:1], in_=pg[0:8, j, 1:2], mul=-1.0), reads=[("pg",)], writes=[("cvec",)])
        gI, cs, U, hm = P["w0"], P["cs"], P["U"], P["hm"]
        for (t0, n) in TILES:
            ps, pt = self.psum()
            self.mm(ps[0:8, 0:n], [(wgate[:, k, 0:8], T["xn"][:, k, t0:t0 + n]) for k in range(NCH)], reads=wgt + xn_tags, writes=[pt])
            S.op("act", lambda h, ps=ps, t0=t0, n=n: h.activation(out=gI[0:8, t0:t0 + n], in_=ps[0:8, 0:n], func=AF.Identity,
                                                                  bias=pg[0:8, j, 0:1]), reads=[pt, ("pg",)], writes=[("w0",)])
            ps2, pt2 = self.psum()
            self.mm(ps2[0:8, 0:n], [(wgate[:, k, 8:16], T["xn"][:, k, t0:t0 + n]) for k in range(NCH)], reads=wgt + xn_tags, writes=[pt2])
            S.op("act", lambda h, ps2=ps2, t0=t0, n=n: h.activation(out=hm[0:8, t0:t0 + n], in_=ps2[0:8, 0:n], func=AF.Exp, scale=-1.0,
                                                                    bias=P["cvec"][0:8, 0:1]), reads=[pt2, ("cvec",)], writes=[("w4",)])
            S.op("act", lambda h, t0=t0, n=n: h.activation(out=hm[0:8, t0:t0 + n], in_=hm[0:8, t0:t0 + n], func=AF.Ln,
                                                           bias=one_c[0:8, :]), reads=[("w4",), ("cst",)], writes=[("w4",)])
        hm_t = [("w4",)]
        gi_t = [("w0",)]
        for (s0, sn) in SEG:
            S.op("dve", lambda h, s0=s0, sn=sn: h.tensor_tensor_scan(out=cs[0:8, s0:s0 + sn], data0=hm[0:8, s0:s0 + sn],
                                                                     data1=zero_b[0:8, :].broadcast_to([8, sn]), initial=0.0, op0=ALU.add, op1=ALU.add),
                 reads=hm_t + [("cst",)], writes=[("cs", s0)])
        cs_t = [("cs", s0) for s0, _ in SEG]
        S.op("dve", lambda h: h.tensor_tensor(out=U[0:8, :], in0=gI[0:8, :], in1=cs[0:8, :], op=ALU.add), reads=gi_t + cs_t, writes=[("U",)])
        for (s0, sn) in SEG:
            S.op("dve", lambda h, s0=s0, sn=sn: h.tensor_tensor_scan(out=hm[0:8, s0:s0 + sn], data0=U[0:8, s0:s0 + sn],
                                                                     data1=U[0:8, s0:s0 + sn], initial=NEG, op0=ALU.max, op1=ALU.max),
                 reads=[("U",)] + hm_t, writes=[("w4",)])
        cm_t = [("w4",)]
        hs = P["hs"]
        S.op("dve", lambda h: h.tensor_copy(out=hs[0:8, 0:8], in_=hm[0:8, 127:1024:128]), reads=cm_t, writes=[("hs", 0)])
        S.op("dve", lambda h: h.tensor_copy(out=hs[0:8, 8:9], in_=hm[0:8, 1055:1056]), reads=cm_t, writes=[("hs", 0)])
        S.op("dve", lambda h: h.tensor_scalar(out=hs[0:8, 16:17], in0=cs[0:8, 1023:1024], scalar1=-1.0, scalar2=None, op0=ALU.mult),
             reads=cs_t, writes=[("hs", 1)])
        S.op("dve", lambda h: h.tensor_tensor(out=hs[0:8, 17:18], in0=hs[0:8, 16:17], in1=hs[0:8, 7:8], op=ALU.add),
             reads=[("hs", 0), ("hs", 1)], writes=[("hs", 1)])
        S.op("dve", lambda h: h.tensor_scalar(out=hs[0:8, 18:19], in0=hs[0:8, 7:8], scalar1=-1.0, scalar2=None, op0=ALU.mult),
             reads=[("hs", 0)], writes=[("hs", 2)])
        S.op("act", lambda h: h.activation(out=P["w2"][0:8, 0:1024], in_=U[0:8, 0:1024], func=AF.Exp, bias=hs[0:8, 18:19]),
             reads=[("U",), ("hs", 2)], writes=[("w2",)])
        for b in range(8):
            ps, pt = self.psum()
            self.tr(ps[0:128, 0:8], P["w2"][0:8, b * 128:(b + 1) * 128], ident[0:8, 0:8], reads=[("w2",), ("cst",)], writes=[pt])
            S.op("act", lambda h, ps=ps, b=b: h.copy(out=P["pTl"][:, b, :], in_=ps[:, 0:8]), reads=[pt], writes=[("pTl",)])
        S.dma_multi("sp", [(A["hm_scr"][a:a + 1, :].rearrange("a h -> h a"), hs[0:8, 16 + a:17 + a]) for a in range(2)],
                    reads=[("hs", 1)], writes=[("hm_scr",)])
        S.dma("sp", P["contrib"][:, 8:24], A["hm_scr"].rearrange("a h -> (a h)").partition_broadcast(128),
              reads=[("hm_scr",)], writes=[("contrib", 1)])

        if cfg.get("odd_stop") == 1:
            return
        if j == 0:
            S.op("dve", lambda h: h.memset(P["lbv"][:, 0:8], 0.0), writes=[("lb",)])
        else:
            S.op("dve", lambda h: h.tensor_tensor(out=P["lbv"][:, 0:8], in0=po[:, j, 24:32], in1=po[:, j, 16:24], op=ALU.subtract),
                 reads=[("po",)], writes=[("lb",)])
            S.op("act", lambda h: h.activation(out=P["lbv"][:, 0:8], in_=P["lbv"][:, 0:8], func=AF.Sigmoid), reads=[("lb",)], writes=[("lb",)])
        S.op("dve", lambda h: h.tensor_scalar(out=P["lbv"][:, 8:16], in0=P["lbv"][:, 0:8], scalar1=-1.0, scalar2=1.0, op0=ALU.mult, op1=ALU.add),
             reads=[("lb",)], writes=[("oml",)])
        lbt, omlt = P["lbv"][:, 0:8], P["lbv"][:, 8:16]

        def prep_c_kv(hd, ptile):
            wk, wkt = self.wload([win[:, :, 1024 + hd * 128:1024 + (hd + 1) * 128], win[:, :, 2048 + hd * 128:2048 + (hd + 1) * 128]], 1, [2, NCH, 128])
            ktag, vtag = tg("ktok"), tg("pv")
            for ci, (t0, n) in enumerate(CH):
                if ptile is P["pTl"] and ci == 8:
                    continue
                ps, pt = self.psum()
                psv_, ptv_ = self.psum()
                tproj(wk[:, 0], wkt, t0, n, (ps, pt, 0))
                tproj(wk[:, 1], wkt, t0, n, (psv_, ptv_, 0))
                if cfg.get("odd_stop") == 15:
                    continue
                S.op("act", lambda h, ps=ps, ci=ci, n=n: h.activation(out=P["b2"][0:n, ci, :], in_=ps[0:n, 0:128], func=AF.Copy, scale=128 ** -0.5),
                     reads=[pt], writes=[ktag])
                S.op("dve", lambda h, ps=psv_, ci=ci, n=n: h.tensor_scalar(out=P["b3"][0:n, ci, 0:128], in0=ps[0:n, 0:128], scalar1=ptile[0:n, ci, hd:hd + 1],
                                                                           scalar2=None, op0=ALU.mult), reads=[ptv_, ("pT",), ("pTl",)], writes=[vtag])
                if cfg.get("odd_stop") != 16:
                    S.op("act", lambda h, ci=ci, n=n: h.copy(out=P["b3"][0:n, ci, 128:129], in_=ptile[0:n, ci, hd:hd + 1]), reads=[("pT",), ("pTl",)], writes=[vtag])
            return ktag, vtag

        def prep_d(hd, need_q):
            pieces = [win[:, :, 5120 + hd * 128:5120 + (hd + 1) * 128], win[:, :, 6144 + hd * 128:6144 + (hd + 1) * 128]]
            if need_q:
                pieces += [win[:, :, 4096 + hd * 128:4096 + (hd + 1) * 128], win[:, :, 7168 + hd * 128:7168 + (hd + 1) * 128]]
            wd_, wdt = self.wload(pieces, 1 if not need_q else 2, [len(pieces), NCH, 128])
            fgt, glt, qt, sgt, vt = tg("fg"), tg("glog"), tg("qq"), tg("sgg"), tg("vtok")
            for (t0, n) in TILES:
                ps, pt = fproj(wd_[:, 0], wdt, t0, n)
                S.op("act", lambda h, ps=ps, t0=t0, n=n: h.activation(out=P["w1"][:, t0:t0 + n], in_=ps[:, 0:n], func=AF.Sigmoid), reads=[pt], writes=[fgt])
                if need_q:
                    ps, pt = fproj(wd_[:, 2], wdt, t0, n)
                    S.op("act", lambda h, ps=ps, t0=t0, n=n: h.activation(out=P["w3"][:, t0:t0 + n], in_=ps[:, 0:n], func=AF.Silu), reads=[pt], writes=[qt])
                    ps, pt = fproj(wd_[:, 3], wdt, t0, n)
                    S.op("act", lambda h, ps=ps, t0=t0, n=n: h.activation(out=P["w4"][:, t0:t0 + n], in_=ps[:, 0:n], func=AF.Sigmoid), reads=[pt], writes=[sgt])
            S.op("dve", lambda h: h.tensor_scalar(out=P["w1"][:, :], in0=P["w1"][:, :], scalar1=omlt[:, hd:hd + 1], scalar2=lbt[:, hd:hd + 1],
                                                  op0=ALU.mult, op1=ALU.add), reads=[fgt, ("lb",), ("oml",)], writes=[fgt])
            S.op("act", lambda h: h.activation(out=P["w0"][:, :], in_=P["w1"][:, :], func=AF.Ln), reads=[fgt], writes=[glt])
            S.op("dve", lambda h: h.tensor_scalar(out=P["w1"][:, :], in0=P["w1"][:, :], scalar1=-1.0, scalar2=1.0, op0=ALU.mult, op1=ALU.add),
                 reads=[fgt, glt], writes=[fgt])
            bbt = tg("BB")
            S.op("dve", lambda h: h.memset(P["w2"][:, 0:1], 0.0), writes=[bbt])
            S.op("dve", lambda h: h.memset(P["w2"][:, 1025:1026], 0.0), reads=[bbt], writes=[bbt])
            S.op("dve", lambda h: h.tensor_tensor_scan(out=P["w2"][:, 1:1025], data0=P["w0"][:, 0:1024], data1=zero_b.broadcast_to([128, 1024]), initial=0.0,
                                                       op0=ALU.add, op1=ALU.add), reads=[glt, bbt, ("cst",)], writes=[bbt])
            S.op("dve", lambda h: h.tensor_tensor_scan(out=P["w2"][:, 1026:1058], data0=P["w0"][:, 1024:1056], data1=zero_b.broadcast_to([128, 32]), initial=0.0,
                                                       op0=ALU.add, op1=ALU.add), reads=[glt, bbt, ("cst",)], writes=[bbt])
            for ci, (t0, n) in enumerate(CH):
                ps, pt = self.psum()
                tproj(wd_[:, 1], wdt, t0, n, (ps, pt, 0))
                S.op("act", lambda h, ps=ps, ci=ci, n=n: h.copy(out=P["b2"][0:n, ci, :], in_=ps[0:n, 0:128]), reads=[pt], writes=[vt])
            return fgt, bbt, qt, sgt, vt

        def bbi(t):
            return (t + 1) if t < 1024 else (t + 2)

        F_C, F_S, F_B = 0, 0, 1024

        if cfg.get("odd_stop") == 13:
            return
        for hd in range(8):
            ktag, vtag = prep_c_kv(hd, P["pTl"])
            if cfg.get("odd_stop") in (14, 15, 16):
                continue
            ps, pt = self.psum()
            self.mm(ps[:, 0:130], [(P["b2"][:, b, :], P["b3"][:, b, 0:130]) for b in range(8)], reads=[ktag, vtag], writes=[pt])
            S.op("act", lambda h, ps=ps: h.copy(out=P["st"][:, 0:129], in_=ps[:, 0:129]), reads=[pt], writes=[("st",)])
            S.dma("sp", A["oga_in"][:, F_C + hd * 129:F_C + (hd + 1) * 129], P["st"][:, 0:129], reads=[("st",)], writes=[("oga_in",)])
        if cfg.get("odd_stop") == 12:
            return
        for hd in range(8):
            fgt, bbt, qt, sgt, vt = prep_d(hd, False)
            S.op("act", lambda h: h.activation(out=P["w3"][:, 0:1024], in_=P["w2"][:, 1:1025], func=AF.Exp, scale=-1.0, bias=P["w2"][:, 1024:1025]),
                 reads=[bbt], writes=[("w3",)])
            S.op("dve", lambda h: h.tensor_tensor(out=P["w3"][:, 0:1024], in0=P["w3"][:, 0:1024], in1=P["w1"][:, 0:1024], op=ALU.mult),
                 reads=[("w3",), fgt], writes=[("w3",)])
            S.op("act", lambda h, hd=hd: h.copy(out=P["contrib"][:, hd:hd + 1], in_=P["w2"][:, 1024:1025]), reads=[bbt], writes=[("contrib", 0)])
            kts = []
            for b in range(8):
                ps, pt = self.psum()
                self.tr(ps[:, 0:128], P["w3"][:, b * 128:(b + 1) * 128], ident, reads=[("w3",), ("cst",)], writes=[pt])
                kt = tg("kgt")
                S.op("act", lambda h, ps=ps, b=b: h.copy(out=P["b3"][:, b, 0:128], in_=ps[:, 0:128]), reads=[pt], writes=[kt])
                kts.append(kt)
            ps, pt = self.psum()
            self.mm(ps[:, 0:128], [(P["b3"][:, b, 0:128], P["b2"][:, b, :]) for b in range(8)], reads=kts + [vt], writes=[pt])
            S.op("act", lambda h, ps=ps: h.copy(out=P["st"][:, 0:128], in_=ps[:, 0:128]), reads=[pt], writes=[("st",)])
            S.dma("sp", A["ogb_in"][:, F_S + hd * 128:F_S + (hd + 1) * 128], P["st"][:, 0:128], reads=[("st",)], writes=[("ogb_in",)])
        S.dma("sp", A["ogb_in"][:, F_B:F_B + 24], P["contrib"][:, 0:24], reads=[("contrib", 0), ("contrib", 1)], writes=[("ogb_in",)])
        for nm in ("oga", "ogb"):
            if cfg.get("no_cc"):
                S.dma_multi("sp", [(A[nm + "_out"][r * 128:(r + 1) * 128, :], A[nm + "_in"]) for r in range(4)], reads=[(nm + "_in",)], writes=[(nm + "_out",)])
            else:
                S.collective(lambda h, nm=nm: h.collective_compute("AllGather", ALU.bypass, replica_groups=[[0, 1, 2, 3], [4, 5, 6, 7]],
                                                                   ins=[A[nm + "_in"]], outs=[A[nm + "_out"]]), reads=[(nm + "_in",)], writes=[(nm + "_out",)])
        ogva = A["oga_out"].rearrange("(r p) f -> p r f", p=128)
        ogvb = A["ogb_out"].rearrange("(r p) f -> p r f", p=128)

        if cfg.get("odd_stop") == 2:
            return
        gl = P["gl"]
        S.dma("sp", gl[:, :, 0:24], ogvb[:, :, F_B:F_B + 24], reads=[("ogb_out",)], writes=[("gl",)])
        for q in range(4):
            S.op("dve", lambda h, q=q: h.tensor_scalar(out=gl[:, q, 32:40], in0=gl[:, q, 16:24], scalar1=coef[:, 4 + q:5 + q], scalar2=None, op0=ALU.add),
                 reads=[("gl",), ("coef",)], writes=[("gE", q)])
            S.op("dve", lambda h, q=q: h.tensor_scalar(out=gl[:, q, 64:72], in0=gl[:, q, 0:8], scalar1=0.0, scalar2=coef[:, 4 + q:5 + q], op0=ALU.mult, op1=ALU.add),
                 reads=[("gl",), ("coef",)], writes=[("gG", q)])
            for q2 in range(4):
                S.op("dve", lambda h, q=q, q2=q2: h.scalar_tensor_tensor(out=gl[:, q, 32:40], in0=gl[:, q2, 8:16], scalar=coef[:, 8 + 4 * q + q2:9 + 4 * q + q2],
                                                                         in1=gl[:, q, 32:40], op0=ALU.mult, op1=ALU.add),
                     reads=[("gl",), ("coef",), ("gE", q)], writes=[("gE", q)])
                S.op("dve", lambda h, q=q, q2=q2: h.scalar_tensor_tensor(out=gl[:, q, 64:72], in0=gl[:, q2, 0:8], scalar=coef[:, 8 + 4 * q + q2:9 + 4 * q + q2],
                                                                         in1=gl[:, q, 64:72], op0=ALU.mult, op1=ALU.add),
                     reads=[("gl",), ("coef",), ("gG", q)], writes=[("gG", q)])
            S.op("act", lambda h, q=q: h.activation(out=gl[:, q, 64:72], in_=gl[:, q, 64:72], func=AF.Exp), reads=[("gG", q)], writes=[("gG", q)])
        bc = P["bc"]
        S.op("dve", lambda h: h.tensor_scalar(out=bc[:, 8:16], in0=gl[:, 0, 8:16], scalar1=coef[:, 24:25], scalar2=None, op0=ALU.mult),
             reads=[("gl",), ("coef",)], writes=[("bc", 1)])
        for q2 in range(1, 4):
            S.op("dve", lambda h, q2=q2: h.scalar_tensor_tensor(out=bc[:, 8:16], in0=gl[:, q2, 8:16], scalar=coef[:, 24 + q2:25 + q2], in1=bc[:, 8:16],
                                                                op0=ALU.mult, op1=ALU.add), reads=[("gl",), ("coef",), ("bc", 1)], writes=[("bc", 1)])
        S.op("dve", lambda h: h.tensor_tensor(out=bc[:, 0:8], in0=bc[:, 8:16], in1=gl[:, 0, 32:40], op=ALU.max), reads=[("bc", 1), ("gE", 0)], writes=[("bc", 0)])
        for q in range(1, 4):
            S.op("dve", lambda h, q=q: h.tensor_tensor(out=bc[:, 0:8], in0=bc[:, 0:8], in1=gl[:, q, 32:40], op=ALU.max), reads=[("bc", 0), ("gE", q)], writes=[("bc", 0)])
        for q in range(4):
            S.op("dve", lambda h, q=q: h.tensor_tensor(out=gl[:, q, 32:40], in0=gl[:, q, 32:40], in1=bc[:, 0:8], op=ALU.subtract), reads=[("bc", 0), ("gE", q)], writes=[("gE", q)])
            S.op("act", lambda h, q=q: h.activation(out=gl[:, q, 32:40], in_=gl[:, q, 32:40], func=AF.Exp), reads=[("gE", q)], writes=[("gE", q)])
        S.dma("sp", bc[:, 16:24], A["st_mb"][:, j, :], writes=[("bc", 2)])
        S.op("dve", lambda h: h.tensor_tensor(out=P["small"][0:8, 0:8], in0=bc[0:8, 0:8], in1=ident[0:8, 0:8], op=ALU.mult), reads=[("bc", 0), ("cst",)], writes=[("small",)])
        S.op("dve", lambda h: h.tensor_reduce(out=hs[0:8, 40:41], in_=P["small"][0:8, 0:8], axis=AX.X, op=ALU.add), reads=[("small",)], writes=[("hs", 3)])
        S.dma("sp", hs[0:8, 41:42], A["st_m"][:, j:j + 1], writes=[("hs", 4)], allow_slow_non_contiguous=True)
        S.op("dve", lambda h: h.tensor_scalar(out=hs[0:8, 50:58], in0=hs[0:8, 0:8], scalar1=hs[0:8, 40:41], scalar2=None, op0=ALU.max), reads=[("hs", 0), ("hs", 3)], writes=[("hs", 5)])
        S.op("dve", lambda h: h.tensor_scalar(out=hs[0:8, 58:59], in0=hs[0:8, 8:9], scalar1=hs[0:8, 41:42], scalar2=None, op0=ALU.max), reads=[("hs", 0), ("hs", 4), ("hs", 5)], writes=[("hs", 5)])
        S.op("dve", lambda h: h.tensor_scalar(out=hs[0:8, 20:29], in0=hs[0:8, 50:59], scalar1=-1.0, scalar2=None, op0=ALU.mult), reads=[("hs", 5)], writes=[("hs", 6)])
        S.op("dve", lambda h: h.tensor_copy(out=hs[0:8, 31:38], in_=hs[0:8, 50:57]), reads=[("hs", 5)], writes=[("hs", 7)])
        S.op("dve", lambda h: h.tensor_copy(out=hs[0:8, 30:31], in_=hs[0:8, 40:41]), reads=[("hs", 3), ("hs", 7)], writes=[("hs", 7)])
        S.op("dve", lambda h: h.tensor_copy(out=hs[0:8, 38:39], in_=hs[0:8, 41:42]), reads=[("hs", 4), ("hs", 7)], writes=[("hs", 7)])
        S.op("dve", lambda h: h.tensor_tensor(out=hs[0:8, 30:39], in0=hs[0:8, 30:39], in1=hs[0:8, 50:59], op=ALU.subtract), reads=[("hs", 7), ("hs", 5)], writes=[("hs", 7)])
        S.op("act", lambda h: h.activation(out=hs[0:8, 30:39], in_=hs[0:8, 30:39], func=AF.Exp), reads=[("hs", 7)], writes=[("hs", 7)])
        S.dma("sp", A["dec_scr"], hs[0:8, 30:39], reads=[("hs", 7)], writes=[("dec_scr",)])
        for hh in range(8):
            S.dma("sp", P["decB"][:, hh, 0:9], A["dec_scr"][hh, :].partition_broadcast(128), reads=[("dec_scr",)], writes=[("decB",)])
        S.op("dve", lambda h: h.tensor_tensor(out=hs[0:8, 44:45], in0=hs[0:8, 57:58], in1=cs[0:8, 1023:1024], op=ALU.subtract), reads=[("hs", 5)] + cs_t, writes=[("hs", 8)])
        S.op("dve", lambda h: h.tensor_tensor(out=hs[0:8, 45:46], in0=hs[0:8, 58:59], in1=cs[0:8, 1055:1056], op=ALU.subtract), reads=[("hs", 5), ("hs", 8)] + cs_t, writes=[("hs", 8)])
        S.dma("sp", A["out_m"][:, 2 * j:2 * j + 2], hs[0:8, 44:46], reads=[("hs", 8)], allow_slow_non_contiguous=True)
        for ci, (t0, n) in enumerate(CH):
            S.op("act", lambda h, ci=ci, t0=t0, n=n: h.activation(out=P["w2"][0:8, t0:t0 + n], in_=U[0:8, t0:t0 + n], func=AF.Exp, bias=hs[0:8, 20 + ci:21 + ci]),
                 reads=[("U",), ("hs", 6), ("w3",)], writes=[("w2",)])
            S.op("act", lambda h, ci=ci, t0=t0, n=n: h.activation(out=P["w3"][0:8, t0:t0 + n], in_=cs[0:8, t0:t0 + n], func=AF.Exp, bias=hs[0:8, 20 + ci:21 + ci]),
                 reads=cs_t + [("hs", 6), ("w3",)], writes=[("w3",)])
            ps, pt = self.psum()
            self.tr(ps[0:n, 0:8], P["w2"][0:8, t0:t0 + n], ident[0:8, 0:8], reads=[("w2",), ("cst",)], writes=[pt])
            S.op("act", lambda h, ps=ps, ci=ci, n=n: h.copy(out=P["pT"][0:n, ci, :], in_=ps[0:n, 0:8]), reads=[pt], writes=[("pT",)])
            ps, pt = self.psum()
            self.tr(ps[0:n, 0:8], P["w3"][0:8, t0:t0 + n], ident[0:8, 0:8], reads=[("w3",), ("cst",)], writes=[pt])
            S.op("act", lambda h, ps=ps, ci=ci, n=n: h.copy(out=P["flT"][0:n, ci, :], in_=ps[0:n, 0:8]), reads=[pt], writes=[("flT",)])


        if cfg.get("odd_stop") == 3:
            return
        ybs = [P["yb0"], P["yb1"]]
        mts = [P["m0"], P["m1"], P["m2"], P["m3"]]
        stt = ("st",)

        def load_state_c(hd, g):
            if g == 0:
                S.dma("sp", P["cl"][:, :, 0:129], ogva[:, :, F_C + hd * 129:F_C + (hd + 1) * 129], reads=[("oga_out",)], writes=[("cl",)])
                S.op("dve", lambda h: h.tensor_scalar(out=P["st"][:, 0:129], in0=P["cl"][:, 0, 0:129], scalar1=gl[:, 0, 32 + hd:33 + hd], scalar2=None, op0=ALU.mult),
                     reads=[("cl",), ("gE", 0)], writes=[stt])
                for q in range(1, 4):
                    S.op("dve", lambda h, q=q: h.scalar_tensor_tensor(out=P["st"][:, 0:129], in0=P["cl"][:, q, 0:129], scalar=gl[:, q, 32 + hd:33 + hd], in1=P["st"][:, 0:129],
                                                                     op0=ALU.mult, op1=ALU.add), reads=[("cl",), ("gE", q), stt], writes=[stt])
            else:
                S.dma("sp", P["st"][:, 0:129], A["st_c"][j, :, hd, :], writes=[stt])

        def load_state_s(hd, g):
            if g == 0:
                S.dma("sp", P["cl"][:, :, 0:128], ogvb[:, :, F_S + hd * 128:F_S + (hd + 1) * 128], reads=[("ogb_out",)], writes=[("cl",)])
                S.op("dve", lambda h: h.tensor_scalar(out=P["st"][:, 0:128], in0=P["cl"][:, 0, 0:128], scalar1=gl[:, 0, 64 + hd:65 + hd], scalar2=None, op0=ALU.mult),
                     reads=[("cl",), ("gG", 0)], writes=[stt])
                for q in range(1, 4):
                    S.op("dve", lambda h, q=q: h.scalar_tensor_tensor(out=P["st"][:, 0:128], in0=P["cl"][:, q, 0:128], scalar=gl[:, q, 64 + hd:65 + hd], in1=P["st"][:, 0:128],
                                                                     op0=ALU.mult, op1=ALU.add), reads=[("cl",), ("gG", q), stt], writes=[stt])
            else:
                S.dma("sp", P["st"][:, 0:128], A["st_s"][j, :, hd, :], writes=[stt])

        def emit_y(yb, ybt, trs, gate, ncol, hd, gtag=None):
            for (ps, pt, t0, n) in trs:
                S.op("dve", lambda h, ps=ps, t0=t0, n=n: h.scalar_tensor_tensor(out=yb[:, t0:t0 + n], in0=ps[:, 0:n], scalar=po[:, j, ncol + hd:ncol + hd + 1],
                                                                               in1=gate[:, t0:t0 + n], op0=ALU.mult, op1=ALU.mult),
                     reads=[pt, ("po",), gtag], writes=[ybt])

        rr = [0]

        def c_head(hd):
            ktag, vtag = prep_c_kv(hd, P["pT"])
            wq, wqt = self.wload([win[:, :, hd * 128:(hd + 1) * 128], win[:, :, 1024 + hd * 128:1024 + (hd + 1) * 128],
                                  win[:, :, 3072 + hd * 128:3072 + (hd + 1) * 128]], 2, [3, NCH, 128])
            qtg, ktg = tg("qT"), tg("kT")
            for (t0, n) in TILES:
                ps, pt = fproj(wq[:, 0], wqt, t0, n)
                S.op("act", lambda h, ps=ps, t0=t0, n=n: h.copy(out=P["b0"][:, t0:t0 + n], in_=ps[:, 0:n]), reads=[pt], writes=[qtg])
                ps, pt = fproj(wq[:, 1], wqt, t0, n)
                S.op("act", lambda h, ps=ps, t0=t0, n=n: h.activation(out=P["b1"][:, t0:t0 + n], in_=ps[:, 0:n], func=AF.Copy, scale=128 ** -0.5), reads=[pt], writes=[ktg])
                ps, pt = fproj(wq[:, 2], wqt, t0, n)
                S.op("act", lambda h, ps=ps, t0=t0, n=n: h.activation(out=P["w4"][:, t0:t0 + n], in_=ps[:, 0:n], func=AF.Sigmoid), reads=[pt], writes=[("w4",)])
            yb, ybt = ybs[hd % 2], ("yb%d" % (hd % 2),)
            for g, (s0, sn) in enumerate(SEG):
                load_state_c(hd, g)
                for ci, (t0, n) in enumerate(CH):
                    if (t0 >= 1024) != (g == 1):
                        continue
                    S.op("dve", lambda h, ci=ci: h.tensor_scalar(out=P["st"][:, 0:129], in0=P["st"][:, 0:129], scalar1=P["decB"][:, hd, ci:ci + 1], scalar2=None, op0=ALU.mult),
                         reads=[stt, ("decB",)], writes=[stt])
                    S.op("act", lambda h: h.copy(out=P["stb"][:, 0:130], in_=P["st"][:, 0:130]), reads=[stt], writes=[("stb",)])
                    ps, pt = self.psum()
                    self.mm(ps[0:n, 0:n], [(P["b1"][:, t0:t0 + n], P["b0"][:, t0:t0 + n])], reads=[qtg, ktg], writes=[pt])
                    rr[0] += 1
                    mtile, mtag = mts[rr[0] % 4], ("m%d" % (rr[0] % 4),)
                    S.op("dve", lambda h, ps=ps, n=n, mtile=mtile: h.tensor_tensor(out=mtile[0:n, 0:n], in0=ps[0:n, 0:n], in1=mask[0:n, 0:n], op=ALU.mult),
                         reads=[pt, ("cst",)], writes=[mtag])
                    po_, pot = self.psum()
                    self.mm(po_[0:n, 0:130], [(mtile[0:n, 0:n], P["b3"][0:n, ci, 0:130]), (P["b0"][:, t0:t0 + n], P["stb"][:, 0:130])],
                            reads=[mtag, vtag, qtg, ("stb",)], writes=[pot])
                    pc, pct = self.psum()
                    self.mm(pc[:, 0:130], [(P["b2"][0:n, ci, :], P["b3"][0:n, ci, 0:130])], reads=[ktag, vtag], writes=[pct])
                    S.op("dve", lambda h, pc=pc: h.tensor_tensor(out=P["st"][:, 0:129], in0=P["st"][:, 0:129], in1=pc[:, 0:129], op=ALU.add),
                         reads=[stt, pct, ("stb",)], writes=[stt])
                    slot = rr[0] % 4
                    sm, smt = P["small"][:, 16 * slot:16 * slot + 16], ("small", slot)
                    S.op("act", lambda h, po_=po_, n=n, sm=sm: h.activation(out=sm[0:n, 0:1], in_=po_[0:n, 128:129], func=AF.Abs), reads=[pot], writes=[smt])
                    S.op("dve", lambda h, n=n, ci=ci, sm=sm: h.tensor_tensor(out=sm[0:n, 0:1], in0=sm[0:n, 0:1], in1=P["flT"][0:n, ci, hd:hd + 1], op=ALU.max),
                         reads=[smt, ("flT",)], writes=[smt])
                    S.op("dve", lambda h, n=n, sm=sm: h.reciprocal(out=sm[0:n, 1:2], in_=sm[0:n, 0:1]), reads=[smt], writes=[smt])
                    tt, ttt = (P["t0"], ("t0",)) if rr[0] % 2 else (P["t1"], ("t1",))
                    S.op("act", lambda h, po_=po_, n=n, tt=tt, sm=sm: h.activation(out=tt[0:n, 0:128], in_=po_[0:n, 0:128], func=AF.Square, scale=sm[0:n, 1:2], accum_out=sm[0:n, 2:3]),
                         reads=[pot, smt], writes=[ttt, smt])
                    S.op("act", lambda h, n=n, sm=sm: h.activation(out=sm[0:n, 3:4], in_=sm[0:n, 2:3], func=AF.Sqrt, scale=1.0 / 128, bias=eps_c[0:n, :]), reads=[smt, ("cst",)], writes=[smt])
                    S.op("dve", lambda h, n=n, sm=sm: h.reciprocal(out=sm[0:n, 3:4], in_=sm[0:n, 3:4]), reads=[smt], writes=[smt])
                    S.op("dve", lambda h, n=n, sm=sm: h.tensor_tensor(out=sm[0:n, 4:5], in0=sm[0:n, 3:4], in1=sm[0:n, 1:2], op=ALU.mult), reads=[smt], writes=[smt])
                    S.op("act", lambda h, po_=po_, n=n, tt=tt, sm=sm: h.activation(out=tt[0:n, 0:128], in_=po_[0:n, 0:128], func=AF.Copy, scale=sm[0:n, 4:5]),
                         reads=[pot, smt, ttt], writes=[ttt])
                    ptr, ptt = self.psum()
                    self.tr(ptr[:, 0:n], tt[0:n, 0:128], ident[0:n, 0:n], reads=[ttt, ("cst",)], writes=[ptt])
                    emit_y(yb, ybt, [(ptr, ptt, t0, n)], P["w4"], 0, hd, ("w4",))
                S.dma("sp", A["out_c"][2 * j + g, :, hd * 129:(hd + 1) * 129], P["st"][:, 0:129], reads=[stt])
            S.dma("sp", A["y_scr"][:, hd, :], yb.bitcast(F32), reads=[ybt], writes=[("y_scr",)])

        for hd_ in range(8):
            c_head(hd_)

        if cfg.get("odd_stop") == 4:
            return
        BB = P["w2"]

        def d_head(hd):
            fgt, bbt, qt, sgt, vt = prep_d(hd, True)
            S.op("act", lambda h: h.copy(out=P["w0"][:, :], in_=P["w4"][:, :]), reads=[sgt], writes=[("w0",)])
            yb, ybt = ybs[hd % 2], ("yb%d" % (hd % 2),)
            for g, (s0, sn) in enumerate(SEG):
                load_state_s(hd, g)
                for ci, (t0, n) in enumerate(CH):
                    if (t0 >= 1024) != (g == 1):
                        continue
                    half = n // 2
                    i0, im, il = bbi(t0 - 1), bbi(t0 + half - 1), bbi(t0 + n - 1)
                    if t0 == 1024:
                        i0 = 1025
                    sm, smt = P["small"], tg("smd")
                    S.op("dve", lambda h, i0=i0, im=im: h.tensor_scalar(out=sm[:, 8:9], in0=BB[:, im:im + 1], scalar1=-1.0, scalar2=None, op0=ALU.mult), reads=[bbt], writes=[smt])
                    S.op("dve", lambda h, i0=i0: h.tensor_scalar(out=sm[:, 9:10], in0=BB[:, i0:i0 + 1], scalar1=-1.0, scalar2=None, op0=ALU.mult), reads=[bbt, smt], writes=[smt])
                    bsl = BB[:, bbi(t0):bbi(t0) + n]
                    e0, e0t = (P["t0"], ("t0",))
                    e1, e1t = (P["t1"], ("t1",))
                    S.op("act", lambda h, n=n, bsl=bsl: h.activation(out=e0[:, 0:n], in_=bsl, func=AF.Exp, bias=sm[:, 8:9]), reads=[bbt, smt], writes=[e0t])
                    S.op("dve", lambda h, n=n, t0=t0: h.tensor_tensor(out=P["m0"][:, 0:n], in0=e0[:, 0:n], in1=P["w3"][:, t0:t0 + n], op=ALU.mult), reads=[e0t, qt], writes=[("m0",)])
                    S.op("act", lambda h, n=n, bsl=bsl, im=im: h.activation(out=e1[:, 0:n], in_=bsl, func=AF.Exp, scale=-1.0, bias=BB[:, im:im + 1]), reads=[bbt], writes=[e1t])
                    S.op("dve", lambda h, n=n, t0=t0: h.tensor_tensor(out=P["m1"][:, 0:n], in0=e1[:, 0:n], in1=P["w1"][:, t0:t0 + n], op=ALU.mult), reads=[e1t, fgt], writes=[("m1",)])
                    S.op("act", lambda h, n=n, bsl=bsl: h.activation(out=e0[:, 0:n], in_=bsl, func=AF.Exp, bias=sm[:, 9:10]), reads=[bbt, smt, ("m0",)], writes=[e0t])
                    S.op("dve", lambda h, n=n, t0=t0: h.tensor_tensor(out=P["m2"][:, 0:n], in0=e0[:, 0:n], in1=P["w3"][:, t0:t0 + n], op=ALU.mult), reads=[e0t, qt], writes=[("m2",)])
                    S.op("act", lambda h, n=n, bsl=bsl, il=il: h.activation(out=e1[:, 0:n], in_=bsl, func=AF.Exp, scale=-1.0, bias=BB[:, il:il + 1]), reads=[bbt, ("m1",)], writes=[e1t])
                    S.op("dve", lambda h, n=n, t0=t0: h.tensor_tensor(out=e1[:, 0:n], in0=e1[:, 0:n], in1=P["w1"][:, t0:t0 + n], op=ALU.mult), reads=[e1t, fgt], writes=[e1t])
                    S.op("act", lambda h, il=il: h.activation(out=sm[:, 10:11], in_=BB[:, il:il + 1], func=AF.Exp, bias=sm[:, 9:10]), reads=[bbt, smt], writes=[smt])
                    pa, pat = self.psum()
                    self.mm(pa[0:n, half:n], [(P["m1"][:, 0:n], P["m0"][:, half:n])], reads=[("m0",), ("m1",)], writes=[pat])
                    pb, pbt = self.psum()
                    self.mm(pb[0:half, 0:half], [(P["m1"][:, 0:half], P["m0"][:, 0:half])], reads=[("m0",), ("m1",)], writes=[pbt])
                    S.op("dve", lambda h, n=n: h.memset(P["m3"][0:n, 0:n], 0.0), reads=[("m3",)], writes=[("m3",)])
                    S.op("dve", lambda h, pa=pa, n=n, half=half: h.tensor_tensor(out=P["m3"][0:n, half:n], in0=pa[0:n, half:n], in1=mask[0:n, half:n], op=ALU.mult),
                         reads=[pat, ("cst",), ("m3",)], writes=[("m3",)])
                    S.op("dve", lambda h, pb=pb, half=half: h.tensor_tensor(out=P["m3"][0:half, 0:half], in0=pb[0:half, 0:half], in1=mask[0:half, 0:half], op=ALU.mult),
                         reads=[pbt, ("cst",), ("m3",)], writes=[("m3",)])
                    S.op("act", lambda h: h.copy(out=P["stb"][:, 0:128], in_=P["st"][:, 0:128]), reads=[stt], writes=[("stb",)])
                    po_, pot = self.psum()
                    self.mm(po_[0:n, 0:128], [(P["m3"][0:n, 0:n], P["b2"][0:n, ci, :]), (P["m2"][:, 0:n], P["stb"][:, 0:128])],
                            reads=[("m3",), vt, ("m2",), ("stb",)], writes=[pot])
                    pk, pkt = self.psum()
                    self.tr(pk[0:n, 0:128], e1[:, 0:n], ident, reads=[e1t, ("cst",)], writes=[pkt])
                    S.op("act", lambda h, pk=pk, n=n, ci=ci: h.copy(out=P["b3"][0:n, ci, 0:128], in_=pk[0:n, 0:128]), reads=[pkt], writes=[("b3",)])
                    pn, pnt = self.psum()
                    self.mm(pn[:, 0:128], [(P["b3"][0:n, ci, 0:128], P["b2"][0:n, ci, :])], reads=[("b3",), vt], writes=[pnt])
                    S.op("dve", lambda h, pn=pn: h.scalar_tensor_tensor(out=P["st"][:, 0:128], in0=P["st"][:, 0:128], scalar=sm[:, 10:11], in1=pn[:, 0:128], op0=ALU.mult, op1=ALU.add),
                         reads=[stt, pnt, smt, ("stb",)], writes=[stt])
                    sm2, sm2t = P["bc"], tg("sm2")
                    S.op("act", lambda h, po_=po_, n=n: h.activation(out=e0[0:n, 0:128], in_=po_[0:n, 0:128], func=AF.Square, accum_out=sm2[0:n, 32:33]),
                         reads=[pot, ("m2",), e0t], writes=[e0t, sm2t])
                    S.op("act", lambda h, n=n: h.activation(out=sm2[0:n, 33:34], in_=sm2[0:n, 32:33], func=AF.Sqrt, scale=1.0 / 128, bias=eps_c[0:n, :]), reads=[sm2t, ("cst",)], writes=[sm2t])
                    S.op("dve", lambda h, n=n: h.reciprocal(out=sm2[0:n, 33:34], in_=sm2[0:n, 33:34]), reads=[sm2t], writes=[sm2t])
                    S.op("act", lambda h, po_=po_, n=n: h.activation(out=e0[0:n, 0:128], in_=po_[0:n, 0:128], func=AF.Copy, scale=sm2[0:n, 33:34]), reads=[pot, sm2t, e0t], writes=[e0t])
                    ptr, ptt = self.psum()
                    self.tr(ptr[:, 0:n], e0[0:n, 0:128], ident[0:n, 0:n], reads=[e0t, ("cst",)], writes=[ptt])
                    emit_y(yb, ybt, [(ptr, ptt, t0, n)], P["w0"], 8, hd, ("w0",))
                S.dma("sp", A["out_s"][2 * j + g, :, hd * 128:(hd + 1) * 128], P["st"][:, 0:128], reads=[stt])
            S.dma("sp", A["y_scr"][:, 8 + hd, :], yb.bitcast(F32), reads=[ybt], writes=[("y_scr",)])
        for hd_ in range(8):
            d_head(hd_)
        S.dma("sp", T["xn"][:, :, :].bitcast(F32), A["y_scr"], reads=[("y_scr",)], writes=[("xn", c) for c in range(NCH)])
        self.out_proj(A["odd_w_out"][j])

    def out_proj(self, wout):
        S, T = self.S, self.T
        xn_tags = [("xn", c) for c in range(NCH)]
        wv = wout.rearrange("(k p) c -> p k c", p=128)
        for mb in range(8):
            w, wt = self.wload(wv[:, :, mb * 256:(mb + 1) * 256], 1, [NCH, 256])
            for mm_ in range(2):
                m = mb * 2 + mm_
                for ti, (t0, n) in enumerate(TILES):
                    ps, pt = self.psum()
                    self.mm(ps[:, 0:n], [(w[:, k, mm_ * 128:(mm_ + 1) * 128], T["xn"][:, k, t0:t0 + n]) for k in range(NCH)],
                            reads=wt + xn_tags, writes=[pt])
                    S.op("dve", lambda h, ps=ps, m=m, t0=t0, n=n: h.tensor_tensor(
                        out=T["x"][:, m, t0:t0 + n], in0=ps[:, 0:n], in1=T["x"][:, m, t0:t0 + n], op=ALU.add),
                         reads=[pt, ("x", m, ti)], writes=[("x", m, ti)])

    def store_x(self, dst, final_norm):
        S, T = self.S, self.T
        P = self.phase([("stage0", [D], F32), ("stage1", [D], F32)])
        T["stage"] = [P["stage0"], P["stage1"]]
        ident = T["cst"][:, 0:128]
        src_name = "x"
        if final_norm:
            self.rstd_from_x()
            rs = [("rstd", t0) for t0, _ in TILES]
            for c in range(NCH):
                S.op("dve", lambda h, c=c: h.scalar_tensor_tensor(out=T["x"][:, c, :], in0=T["x"][:, c, :],
                                                                   scalar=T["nf"][:, c:c + 1], in1=T["rstd"][:],
                                                                   op0=ALU.mult, op1=ALU.mult),
                     reads=xt(c) + [("nf",)] + rs, writes=xt(c))
        blocks = [(b * 128, 128) for b in range(8)] + [(1024, 32)]
        for bi, (t0, n) in enumerate(blocks):
            st = T["stage"][bi % 2]
            stag = ("stage", bi % 2)
            for q in range(4):
                ps, ptag = self.psum()
                for cc in range(4):
                    c = q * 4 + cc
                    self.tr(ps[0:n, cc * 128:(cc + 1) * 128], T["x"][:, c, t0:t0 + n], ident,
                            reads=xt(c) + [("cst",)], writes=[ptag])
                if q % 2:
                    S.op("act", lambda h, st=st, ps=ps, q=q, n=n: h.copy(out=st[0:n, q * 512:(q + 1) * 512], in_=ps[0:n, :]),
                         reads=[ptag], writes=[stag])
                else:
                    S.op("dve", lambda h, st=st, ps=ps, q=q, n=n: h.tensor_copy(out=st[0:n, q * 512:(q + 1) * 512], in_=ps[0:n, :]),
                         reads=[ptag], writes=[stag])
            S.dma("sp", dst[t0:t0 + n, :], st[0:n, :], reads=[stag])


def build_nc(cfg):
    nc = bass.Bass("TRN2", target_bir_lowering=False)
    A = {}

    def din(name, shape):
        A[name] = nc.dram_tensor(name, list(shape), F32, kind="ExternalInput").ap()

    def dout(name, shape):
        A[name] = nc.dram_tensor(name, list(shape), F32, kind="ExternalOutput").ap()

    only_ = cfg.get("only")
    if only_ is not None:
        cfg = dict(cfg, n_even=1, n_odd=1)
    din("xin", (NT, D))
    din("cst", (128, 320))
    din("norm_g", (128, DEPTH * 3, NCH))
    din("norm_f", (128, NCH))
    din("pe", (128, 2, 8, 40))
    din("coef", (128, 64))
    use_even = (only_ is None and cfg.get("stages") != (0, "ffn1")) or (only_ is not None and only_[1] == "mix" and only_[0] % 2 == 0)
    if use_even:
        din("st_conv", (2, 128, 8, 32))
        din("even_w_in", (cfg.get("n_even", 2), D, EVEN_IN))
        din("even_w_out", (cfg.get("n_even", 2), D, D))
        dout("out_conv", (4, 128, 256))
    A["hg_in"] = nc.dram_tensor("hg_in", [128, 256], F32, kind="Internal").ap()
    A["hg_out"] = nc.dram_tensor("hg_out", [512, 256], F32, kind="Internal").ap()
    A["ya_scr"] = nc.dram_tensor("ya_scr", [128, 8, NT // 2], F32, kind="Internal").ap()
    use_odd = (only_ is None and cfg.get("depth", DEPTH) > 1) or (only_ is not None and only_[1] == "mix" and only_[0] % 2 == 1)
    if use_odd:
        din("po", (128, 2, 32))
        din("pg", (8, 2, 4))
        din("st_c", (2, 128, 8, 129))
        din("st_s", (2, 128, 8, 128))
        din("st_m", (8, 2))
        din("st_mb", (128, 2, 8))
        din("odd_w_in", (cfg.get("n_odd", 2), D, ODD_IN))
        din("odd_w_out", (cfg.get("n_odd", 2), D, D))
        dout("out_c", (4, 128, 8 * 129))
        dout("out_s", (4, 128, 1024))
        dout("out_m", (8, 4))
        A["oga_in"] = nc.dram_tensor("oga_in", [128, 1032], F32, kind="Internal").ap()
        A["oga_out"] = nc.dram_tensor("oga_out", [512, 1032], F32, kind="Internal").ap()
        A["ogb_in"] = nc.dram_tensor("ogb_in", [128, 1048], F32, kind="Internal").ap()
        A["ogb_out"] = nc.dram_tensor("ogb_out", [512, 1048], F32, kind="Internal").ap()
        A["hm_scr"] = nc.dram_tensor("hm_scr", [2, 8], F32, kind="Internal").ap()
        A["dec_scr"] = nc.dram_tensor("dec_scr", [8, 9], F32, kind="Internal").ap()
        A["y_scr"] = nc.dram_tensor("y_scr", [128, 16, NT // 2], F32, kind="Internal").ap()
    LD = cfg.get("depth", DEPTH) if cfg.get("only") is None else 1
    if only_ is None or only_[1] in ("ffn1", "ffn2"):
        din("ffn_w_gate", (LD, 2, D, DFF))
        din("ffn_w_up", (LD, 2, D, DFF))
        din("ffn_w_down", (LD, 2, DFF, D))
    dout("y", (NT, D))

    import contextlib
    with contextlib.ExitStack() as es:
        def sb(name, shape, dt):
            return es.enter_context(nc.sbuf_tensor(name, list(shape), dt))
        T = {}
        T["x"] = sb("x", (128, NCH, NT), F32)
        T["xn"] = sb("xn", (128, NCH, NT), BF16)
        T["ring"] = sb("ring", (128, RING_UNITS, 4096), BF16)
        T["cst"] = sb("cstt", (128, 320), F32)
        T["ng"] = sb("ng", (128, DEPTH * 3, NCH), F32)
        T["nf"] = sb("nf", (128, NCH), F32)
        T["ones_bf"] = sb("ones_bf", (128, 128), BF16)
        T["rstd"] = sb("rstd", (128, NT), F32)
        T["sq"] = [sb("sq%d" % i, (128, NT), BF16) for i in range(2)]
        T["arena"] = sb("arena", (128, ARENA_F32), F32)
        T["pe_"] = sb("pet", (128, 2, 8, 40), F32)
        T["coef"] = sb("coeft", (128, 64), F32)
        T["po"] = sb("pot", (128, 2, 32), F32)
        T["pg"] = sb("pgt", (8, 2, 4), F32)
        T["ps"] = [es.enter_context(nc.psum_tensor("ps%d" % i, [128, 512], F32)) for i in range(8)]
        sems = [es.enter_context(nc.semaphore("s%d" % i)) for i in range(40)]
        block = es.enter_context(nc.Block())
        S = Sched(nc, sems)
        B = Builder(nc, S, T, cfg)
        B.load_consts(A)
        if cfg.get("touch"):
            junk = sb("junk", (1, 64), F32)
            for nm in ("even_w_in", "even_w_out", "st_conv", "ffn_w_gate", "ffn_w_up", "ffn_w_down"):
                if nm in A:
                    a = A[nm]
                    while len(a.shape) > 1:
                        a = a[0]
                    S.dma("sp", junk[0:1, 0:8], a[0:8].unsqueeze(0), writes=[("junk",)])
        B.load_x(A)
        stages = cfg.get("stages", None)
        only = cfg.get("only", None)
        for l in range(cfg.get("depth", DEPTH)):
            if only is None or only == (l, "ffn1"):
                B.norm_to_xn(l * 3 + 0)
                B.ffn(A, l if only is None else 0, 0)
            if stages == (l, "ffn1"):
                break
            if only is None or only == (l, "mix"):
                if l % 2 == 0:
                    B.even_mixer(A, (l // 2) if only is None else 0, l)
                else:
                    B.odd_mixer(A, (l // 2) if only is None else 0, l)
            if stages == (l, "mix"):
                break
            if only is None or only == (l, "ffn2"):
                B.norm_to_xn(l * 3 + 2)
                B.ffn(A, l if only is None else 0, 1)
        B.store_x(A["y"], final_norm=cfg.get("final_norm", True))
        S.finish()
        S.emit(block)
    return nc


def make_consts():
    c = np.zeros((128, 320), np.float32)
    c[:, 0:128] = np.eye(128, dtype=np.float32)
    c[:, 128:256] = np.triu(np.ones((128, 128), np.float32))
    c[:, 256] = EPS
    c[:, 257] = 1.0
    return c


_NC_CACHE = {}


def kernel(**inputs):
    inp = {k: np.asarray(v) for k, v in inputs.items()}
    cfg = {}
    if "full" not in _NC_CACHE:
        _NC_CACHE["full"] = build_nc(cfg)
    nc = _NC_CACHE["full"]
    maps = make_in_maps(inp, cfg)
    res = run_bass_kernel_spmd(nc, maps, core_ids=list(range(8)))
    R = res.results
    f32 = np.float32
    y_prompt = np.zeros((2, 4096, D), f32)
    y_sample = np.zeros((8, 32, D), f32)
    p_conv_a = np.zeros((2, 2, 2, 1024), f32)
    p_conv_b = np.zeros((2, 2, 30, 1024), f32)
    s_conv_a = np.zeros((2, 8, 2, 1024), f32)
    s_conv_b = np.zeros((2, 8, 30, 1024), f32)
    p_c = np.zeros((2, 2, 8, 128, 128), f32)
    p_n = np.zeros((2, 2, 8, 128), f32)
    p_m = np.zeros((2, 2, 8), f32)
    p_s = np.zeros((2, 2, 8, 128, 128), f32)
    s_c = np.zeros((2, 8, 8, 128, 128), f32)
    s_n = np.zeros((2, 8, 8, 128), f32)
    s_m = np.zeros((2, 8, 8), f32)
    s_s = np.zeros((2, 8, 8, 128, 128), f32)

    def conv_split(a):
        a = a.reshape(128, 8, 32)
        return (a[:, :, 0:30].transpose(2, 1, 0).reshape(30, 1024), a[:, :, 30:32].transpose(2, 1, 0).reshape(2, 1024))

    for c in range(8):
        b, r = c // 4, c % 4
        y = R[c]["y"]
        y_prompt[b, r * 1024:(r + 1) * 1024] = y[:1024]
        y_sample[c] = y[1024:]
        for jj in range(2):
            oc = R[c]["out_conv"]
            cb, ca = conv_split(oc[2 * jj + 1])
            s_conv_b[jj, c], s_conv_a[jj, c] = cb, ca
            cc = R[c]["out_c"][2 * jj + 1].reshape(128, 8, 129)
            s_c[jj, c] = cc[:, :, 0:128].transpose(1, 0, 2)
            s_n[jj, c] = cc[:, :, 128].T
            s_m[jj, c] = R[c]["out_m"][:, 2 * jj + 1]
            s_s[jj, c] = R[c]["out_s"][2 * jj + 1].reshape(128, 8, 128).transpose(1, 0, 2)
            if r == 3:
                cb, ca = conv_split(oc[2 * jj + 0])
                p_conv_b[jj, b], p_conv_a[jj, b] = cb, ca
                cc = R[c]["out_c"][2 * jj + 0].reshape(128, 8, 129)
                p_c[jj, b] = cc[:, :, 0:128].transpose(1, 0, 2)
                p_n[jj, b] = cc[:, :, 128].T
                p_m[jj, b] = R[c]["out_m"][:, 2 * jj + 0]
                p_s[jj, b] = R[c]["out_s"][2 * jj + 0].reshape(128, 8, 128).transpose(1, 0, 2)
    return (y_prompt, y_sample, p_conv_a, p_conv_b, p_c, p_n, p_m, p_s,
            s_conv_a, s_conv_b, s_c, s_n, s_m, s_s)


def make_in_maps(inp, cfg):
    LD = cfg.get("depth", DEPTH)
    only = cfg.get("only", None)
    cst = make_consts()
    maps = []
    xp = inp["x_prompt"]
    xs = inp["x_sample"]
    f32 = np.float32
    if only is not None:
        l0 = only[0]
        lsl = slice(l0, l0 + 1)
        jsl = slice(l0 // 2, l0 // 2 + 1)
    else:
        lsl = slice(0, LD)
        jsl = slice(0, 2)
    need_ffn = only is None or only[1] in ("ffn1", "ffn2")
    wg = np.ascontiguousarray(inp["ffn_w_gate"][lsl]) if need_ffn else np.zeros((1, 2, D, DFF), f32)
    wu = np.ascontiguousarray(inp["ffn_w_up"][lsl]) if need_ffn else np.zeros((1, 2, D, DFF), f32)
    wd = np.ascontiguousarray(inp["ffn_w_down"][lsl]) if need_ffn else np.zeros((1, 2, DFF, D), f32)
    ng = np.ascontiguousarray(inp["norm_g"].reshape(DEPTH, 3, NCH, 128).transpose(3, 0, 1, 2).reshape(128, DEPTH * 3, NCH))
    nf = np.ascontiguousarray(inp["norm_f"].reshape(NCH, 128).T)
    pe = np.zeros((128, 2, 8, 40), f32)
    for j in range(2):
        pe[:, j, :, 0:31] = inp["even_conv_b"][j].reshape(31, 8, 128).transpose(2, 1, 0)
        pe[:, j, :, 31:34] = inp["even_conv_a"][j].reshape(3, 8, 128).transpose(2, 1, 0)
        pe[:, j, :, 34] = inp["even_conv_b_bias"][j].reshape(8, 128).T
        pe[:, j, :, 35] = inp["even_ln_g"][j].reshape(8, 128).T
        pe[:, j, :, 36] = inp["even_ln_b"][j].reshape(8, 128).T
    pe = np.ascontiguousarray(pe[:, jsl]) if only is not None else pe
    if pe.shape[1] == 1:
        pe = np.ascontiguousarray(np.concatenate([pe, pe], axis=1))
    ewi = np.ascontiguousarray(inp["even_w_in"][jsl])
    ewo = np.ascontiguousarray(inp["even_w_out"][jsl])
    po = np.zeros((128, 2, 32), f32)
    pg = np.zeros((8, 2, 4), f32)
    for j in range(2):
        po[:, j, 0:8] = inp["odd_norm_c"][j].reshape(8, 128).T
        po[:, j, 8:16] = inp["odd_norm_d"][j].reshape(8, 128).T
        po[:, j, 16:24] = inp["odd_lb_logits"][0].reshape(8, 128).T
        po[:, j, 24:32] = inp["odd_lb_logits"][1].reshape(8, 128).T
        pg[:, j, 0] = inp["odd_bias_i"][j]
        pg[:, j, 1] = inp["odd_bias_f"][j]
    if only is not None:
        po = np.ascontiguousarray(np.concatenate([po[:, jsl], po[:, jsl]], 1))
        pg = np.ascontiguousarray(np.concatenate([pg[:, jsl], pg[:, jsl]], 1))
    owi = np.ascontiguousarray(inp["odd_w_in"][jsl])
    owo = np.ascontiguousarray(inp["odd_w_out"][jsl])
    xov = cfg.get("xin_override", None)
    for c in range(8):
        b, r = c // 4, c % 4
        xin = np.concatenate([xp[b, r * 1024:(r + 1) * 1024], xs[c]], axis=0)
        if xov is not None:
            xin = xov(c, xin)
        coef = np.zeros((128, 64), f32)
        if r > 0:
            coef[:, r - 1] = 1.0
        stc = np.zeros((2, 128, 8, 32), f32)
        for j in range(2):
            stc[j, :, :, 0:30] = inp["state_conv_b"][j, c].reshape(30, 8, 128).transpose(2, 1, 0)
            stc[j, :, :, 30:32] = inp["state_conv_a"][j, c].reshape(2, 8, 128).transpose(2, 1, 0)
        if only is not None:
            stc = np.ascontiguousarray(np.concatenate([stc[jsl], stc[jsl]], axis=0))
        for q in range(4):
            coef[:, 4 + q] = 0.0 if q < r else NEG
            for q2 in range(4):
                coef[:, 8 + 4 * q + q2] = 1.0 if (q < q2 < r) else 0.0
            coef[:, 24 + q] = 1.0 if q < r else 0.0
        st_c = np.zeros((2, 128, 8, 129), f32)
        st_c[:, :, :, 0:128] = inp["state_mlstm_c"][:, c].transpose(0, 2, 1, 3)
        st_c[:, :, :, 128] = inp["state_mlstm_n"][:, c].transpose(0, 2, 1)
        st_s = np.ascontiguousarray(inp["state_hgrn_s"][:, c].transpose(0, 2, 1, 3))
        st_m = np.ascontiguousarray(inp["state_mlstm_m"][:, c].T)
        st_mb = np.ascontiguousarray(np.broadcast_to(inp["state_mlstm_m"][:, c][None], (128, 2, 8)))
        if only is not None:
            st_c = np.ascontiguousarray(np.concatenate([st_c[jsl], st_c[jsl]], 0))
            st_s = np.ascontiguousarray(np.concatenate([st_s[jsl], st_s[jsl]], 0))
            st_m = np.ascontiguousarray(np.concatenate([st_m[:, jsl], st_m[:, jsl]], 1))
            st_mb = np.ascontiguousarray(np.concatenate([st_mb[:, jsl], st_mb[:, jsl]], 1))
        maps.append(dict(xin=np.ascontiguousarray(xin.astype(f32)), cst=cst, norm_g=ng, norm_f=nf,
                         ffn_w_gate=wg, ffn_w_up=wu, ffn_w_down=wd, pe=pe, coef=coef, st_conv=stc,
                         even_w_in=ewi, even_w_out=ewo, po=po, pg=pg, st_c=st_c, st_s=st_s, st_m=st_m, st_mb=st_mb,
                         odd_w_in=owi, odd_w_out=owo))
    return maps
```

```python
import numpy as np
import concourse.bass as bass
import concourse.mybir as mybir
from concourse.bass_utils import run_bass_kernel_spmd

F32 = mybir.dt.float32
BF16 = mybir.dt.bfloat16
AF = mybir.ActivationFunctionType
ALU = mybir.AluOpType
AX = mybir.AxisListType

D = 2048
NCH = 16
DFF = 5632
NT = 1056
NPR = 1024
NSM = 32
TILES = [(0, 512), (512, 512), (1024, 32)]
FFN_TILES = [(0, 352), (352, 352), (704, 352)]
DEPTH = 4
EPS = 1e-6
EVEN_IN = 5120
ODD_IN = 8208
NEG = -1.0e30
RING_UNITS = 5
ARENA_F32 = 13100


def xt(c):
    return [("x", c, ti) for ti in range(3)]


class Sched:
    ENG = ["pe", "act", "dve", "pool", "sp"]

    def __init__(self, nc, sems):
        self.nc = nc
        self.sems = sems
        self.free = list(range(len(sems)))
        self.e = {}
        for n in self.ENG:
            self.e[n] = dict(sem=self.free.pop(0), cnt=0, waited={}, ops=[])
        self.lastw = {}
        self.readers = {}
        self.dq = {}
        for q in ("sp", "pool", "act"):
            self.dq[q] = dict(pool=[self.free.pop(0) for _ in range(8)], pos=0)
        self.dcnt = {}
        self.cc_sems = []

    def _deps(self, eng, reads, writes):
        hard, soft = [], []
        for t in reads:
            w = self.lastw.get(t)
            if w is not None:
                hard.extend(w)
        for t in writes:
            w = self.lastw.get(t)
            if w is not None:
                hard.extend(w)
            r = self.readers.get(t)
            if r:
                soft.extend(r.items())
        return hard, soft

    def _commit(self, tok, reads, writes):
        toks = tok if isinstance(tok, list) else [tok]
        for t in reads:
            r = self.readers.setdefault(t, {})
            for tk in toks:
                if r.get(tk[0], 0) < tk[1]:
                    r[tk[0]] = tk[1]
        for t in writes:
            self.lastw[t] = list(toks)
            self.readers[t] = {}

    def _waits(self, eng, hard, soft):
        E = self.e[eng]
        out = []
        for kind, lst in (("h", hard), ("s", soft)):
            for (si, val) in lst:
                if si == E["sem"]:
                    if eng == "pe" or kind == "s":
                        continue
                if E["waited"].get(si, 0) >= val:
                    continue
                E["waited"][si] = val
                out.append((si, val))
        return out

    def op(self, eng, fn, reads=(), writes=()):
        E = self.e[eng]
        hard, soft = self._deps(eng, reads, writes)
        waits = self._waits(eng, hard, soft)
        E["cnt"] += 1
        tok = (E["sem"], E["cnt"])
        E["ops"].append((waits, fn, ("inc", E["sem"], 1)))
        self._commit(tok, reads, writes)
        return tok

    def dma(self, q, out, in_, reads=(), writes=(), **kw):
        E = self.e[q]
        Q = self.dq[q]
        si = Q["pool"][Q["pos"] % len(Q["pool"])]
        Q["pos"] += 1
        prev = self.dcnt.get(si, 0)
        hard, soft = self._deps(q, reads, writes)
        if prev:
            hard.append((si, prev))
        waits = self._waits(q, hard, soft)
        self.dcnt[si] = prev + 16
        tok = (si, prev + 16)
        E["ops"].append((waits, lambda h: h.dma_start(out=out, in_=in_, **kw), ("inc", si, 16)))
        self._commit(tok, reads, writes)
        return tok

    def dma_multi(self, q, pieces, reads=(), writes=(), **kw):
        E = self.e[q]
        Q = self.dq[q]
        hard, soft = self._deps(q, reads, writes)
        toks = []
        for (out, in_) in pieces:
            si = Q["pool"][Q["pos"] % len(Q["pool"])]
            Q["pos"] += 1
            prev = self.dcnt.get(si, 0)
            h2 = list(hard)
            if prev:
                h2.append((si, prev))
            waits = self._waits(q, h2, soft)
            self.dcnt[si] = prev + 16
            toks.append((si, prev + 16))
            E["ops"].append((waits, lambda h, out=out, in_=in_: h.dma_start(out=out, in_=in_, **kw), ("inc", si, 16)))
        self._commit(toks, reads, writes)
        return toks

    def collective(self, fn, reads=(), writes=()):
        E = self.e["pool"]
        si = self.free.pop(0)
        hard, soft = self._deps("pool", reads, writes)
        waits = self._waits("pool", hard, soft)
        tok = (si, 1)
        E["ops"].append((waits, fn, ("inc", si, None)))
        self._commit(tok, reads, writes)
        return tok

    def barrier(self, engines=("pe", "act", "dve", "sp")):
        toks = []
        for n in engines:
            X = self.e[n]
            if X["cnt"]:
                toks.append((X["sem"], X["cnt"]))
        for q in ("sp", "act"):
            for si in self.dq[q]["pool"]:
                v = self.dcnt.get(si, 0)
                if v:
                    toks.append((si, v))
        for n in engines:
            E = self.e[n]
            waits = []
            for si, v in toks:
                if si == E["sem"]:
                    continue
                if E["waited"].get(si, 0) >= v:
                    continue
                E["waited"][si] = v
                waits.append((si, v))
            if waits:
                E["ops"].append((waits, None, None))

    def finish(self):
        E = self.e["sp"]
        waits = []
        for si, v in self.dcnt.items():
            if E["waited"].get(si, 0) < v:
                waits.append((si, v))
        for n in self.ENG:
            if n == "sp":
                continue
            X = self.e[n]
            if X["cnt"]:
                waits.append((X["sem"], X["cnt"]))
        E["ops"].append((waits, None, None))

    def emit(self, block):
        names = dict(pe="tensor", act="scalar", dve="vector", pool="gpsimd", sp="sync")
        sems = self.sems

        def mk(eng):
            ops = self.e[eng]["ops"]

            def body(h):
                for waits, fn, inc in ops:
                    for si, v in waits:
                        h.wait_ge(sems[si], v)
                    if fn is None:
                        continue
                    ins = fn(h)
                    if inc[2] is None:
                        ins.then_inc(sems[inc[1]])
                    else:
                        ins.then_inc(sems[inc[1]], inc[2])
            return body

        for eng in self.ENG:
            if self.e[eng]["ops"]:
                getattr(block, names[eng])(mk(eng))


class Builder:
    def __init__(self, nc, S, T, cfg):
        self.nc, self.S, self.T, self.cfg = nc, S, T, cfg
        self.ring_pos = 0
        self.RU = T["ring"].shape[1]
        self.ps_rr = 0

    def wload(self, src_ap, nunits, shape):
        if self.ring_pos + nunits > self.RU:
            self.ring_pos = 0
        u0 = self.ring_pos
        self.ring_pos += nunits
        tags = [("ring", u) for u in range(u0, u0 + nunits)]
        flat = self.T["ring"][:, u0:u0 + nunits, :]
        n = 1
        for s in shape:
            n *= s
        assert n <= nunits * 4096, (shape, nunits)
        view = self.T["ring"][:, u0:u0 + nunits, :].rearrange("p u f -> p (u f)")[:, 0:n]
        if len(shape) == 2:
            view = view.rearrange("p (a b) -> p a b", b=shape[1])
        elif len(shape) == 3:
            view = view.rearrange("p (a b c) -> p a b c", b=shape[1], c=shape[2])
        if isinstance(src_ap, list):
            self.S.dma_multi("pool", [(view[:, i], a) for i, a in enumerate(src_ap)], reads=(), writes=tags)
        else:
            self.S.dma("pool", view, src_ap, reads=(), writes=tags)
        return view, tags

    def psum(self):
        i = self.ps_rr % 8
        self.ps_rr += 1
        return self.T["ps"][i], ("ps", i)

    def mm(self, out, pairs, reads, writes):
        def fn(h, out=out, pairs=pairs):
            n = len(pairs)
            ins = None
            for i, (l, r) in enumerate(pairs):
                ins = h.matmul(out, lhsT=l, rhs=r, start=(i == 0), stop=(i == n - 1))
            return ins
        return self.S.op("pe", fn, reads=reads, writes=writes)

    def tr(self, out, in_, ident, reads, writes):
        return self.S.op("pe", lambda h: h.transpose(out, in_, ident), reads=reads, writes=writes)

    def phase(self, specs):
        self.S.barrier()
        off = 0
        out = {}
        offs = {}
        for sp in specs:
            name, shape, dt = sp[0], list(sp[1]), sp[2]
            n = 1
            for v in shape:
                n *= v
            words = n if dt == F32 else (n + 1) // 2
            if len(sp) > 3:
                o = offs[sp[3]]
            else:
                o = off
                off += words
            offs[name] = o
            v = self.T["arena"][:, o:o + words]
            if dt == BF16:
                v = v.bitcast(BF16)[:, 0:n]
            if len(shape) == 2:
                v = v.rearrange("p (a b) -> p a b", b=shape[1])
            elif len(shape) == 3:
                v = v.rearrange("p (a b c) -> p a b c", b=shape[1], c=shape[2])
            out[name] = v
        assert off <= ARENA_F32, (off, ARENA_F32)
        self.T.update(out)
        return out

    def load_consts(self, A):
        S, T = self.S, self.T
        S.dma("sp", T["cst"][:], A["cst"], writes=[("cst",)])
        S.dma("sp", T["ng"][:], A["norm_g"], writes=[("ng",)])
        S.dma("sp", T["nf"][:], A["norm_f"], writes=[("nf",)])
        S.dma("sp", T["pe_"][:], A["pe"], writes=[("pe_",)])
        S.dma("sp", T["coef"][:], A["coef"], writes=[("coef",)])
        if "po" in A:
            S.dma("sp", T["po"][:], A["po"], writes=[("po",)])
            S.dma("sp", T["pg"][:], A["pg"], writes=[("pg",)])
        S.op("dve", lambda h: h.memset(T["ones_bf"][:], 1.0), writes=[("ones_bf",)])

    def load_x(self, A):
        S, T = self.S, self.T
        P = self.phase([("stage0", [D], F32), ("stage1", [D], F32)])
        T["stage"] = [P["stage0"], P["stage1"]]
        ident = T["cst"][:, 0:128]
        blocks = [(b * 128, 128) for b in range(8)] + [(1024, 32)]
        for bi, (t0, n) in enumerate(blocks):
            st = T["stage"][bi % 2]
            stag = ("stage", bi % 2)
            S.dma("sp", st[0:n, :], A["xin"][t0:t0 + n, :], writes=[stag])
            for q in range(4):
                ps, ptag = self.psum()
                for cc in range(4):
                    c = q * 4 + cc
                    self.tr(ps[:, cc * 128:cc * 128 + n], st[0:n, c * 128:(c + 1) * 128], ident[0:n, 0:n],
                            reads=[stag, ("cst",)], writes=[ptag])
                src = ps[:, :].rearrange("p (c t) -> p c t", t=128)[:, :, 0:n]
                dst = T["x"][:, q * 4:q * 4 + 4, t0:t0 + n]
                eng = "act" if (q % 2) else "dve"
                if eng == "act":
                    S.op("act", lambda h, d=dst, s=src: h.copy(out=d, in_=s), reads=[ptag],
                         writes=[("x", c, min(bi // 4, 2)) for c in range(q * 4, q * 4 + 4)])
                else:
                    S.op("dve", lambda h, d=dst, s=src: h.tensor_copy(out=d, in_=s), reads=[ptag],
                         writes=[("x", c, min(bi // 4, 2)) for c in range(q * 4, q * 4 + 4)])

    def rstd_from_x(self):
        S, T = self.S, self.T
        pss = [self.psum() for _ in TILES]
        for c in range(NCH):
            sq = T["sq"][c % 2]
            sqt = ("sq", c % 2)
            S.op("act", lambda h, o=sq, i=T["x"][:, c, :]: h.activation(out=o[:], in_=i, func=AF.Square),
                 reads=xt(c), writes=[sqt])
            for (t0, n), (ps, ptag) in zip(TILES, pss):
                def fn(h, ps=ps, sq=sq, t0=t0, n=n, c=c):
                    return h.matmul(ps[:, 0:n], lhsT=T["ones_bf"][:], rhs=sq[:, t0:t0 + n], start=(c == 0),
                                    stop=(c == NCH - 1))
                S.op("pe", fn, reads=[sqt, ("ones_bf",)], writes=[ptag])
        for (t0, n), (ps, ptag) in zip(TILES, pss):
            S.op("act", lambda h, ps=ps, t0=t0, n=n: h.activation(out=T["rstd"][:, t0:t0 + n], in_=ps[:, 0:n],
                                                                  func=AF.Sqrt, scale=1.0 / D,
                                                                  bias=T["cst"][:, 256:257]),
                 reads=[ptag, ("cst",)], writes=[("rstd", t0)])
            S.op("dve", lambda h, t0=t0, n=n: h.reciprocal(out=T["rstd"][:, t0:t0 + n], in_=T["rstd"][:, t0:t0 + n]),
                 reads=[("rstd", t0)], writes=[("rstd", t0)])

    def norm_to_xn(self, gidx):
        S, T = self.S, self.T
        self.rstd_from_x()
        rs = [("rstd", t0) for t0, _ in TILES]
        for c in range(NCH):
            S.op("dve", lambda h, c=c: h.scalar_tensor_tensor(out=T["xn"][:, c, :], in0=T["x"][:, c, :],
                                                               scalar=T["ng"][:, gidx, c:c + 1], in1=T["rstd"][:],
                                                               op0=ALU.mult, op1=ALU.mult),
                 reads=xt(c) + [("ng",)] + rs, writes=[("xn", c)])

    def ffn(self, A, l, i):
        S, T = self.S, self.T
        G = 2
        P = self.phase([("hg0", [G, NT], BF16), ("hg1", [G, NT], BF16), ("silu0", [512], F32), ("silu1", [512], F32)])
        T["hg"] = [P["hg0"], P["hg1"]]
        T["silu"] = [P["silu0"], P["silu1"]]
        xn_tags = [("xn", c) for c in range(NCH)]
        ngroups = self.cfg.get("ffn_groups", DFF // (128 * G))
        for g in range(ngroups):
            c0 = g * 128 * G
            wg, wg_t = self.wload(A["ffn_w_gate"][l, i, :, c0:c0 + 128 * G].rearrange("(k p) c -> p k c", p=128), 1,
                                  [NCH, 128 * G])
            wu, wu_t = self.wload(A["ffn_w_up"][l, i, :, c0:c0 + 128 * G].rearrange("(k p) c -> p k c", p=128), 1,
                                  [NCH, 128 * G])
            wd, wd_t = self.wload(A["ffn_w_down"][l, i, c0:c0 + 128 * G, :].rearrange("(k p) c -> p k c", p=128), 1,
                                  [G, D])
            hg = T["hg"][g % 2]
            for hc in range(G):
                for (t0, n) in FFN_TILES:
                    psg, tg = self.psum()
                    psu, tu = self.psum()
                    self.mm(psg[:, 0:n], [(wg[:, k, hc * 128:(hc + 1) * 128], T["xn"][:, k, t0:t0 + n]) for k in range(NCH)],
                            reads=wg_t + xn_tags, writes=[tg])
                    self.mm(psu[:, 0:n], [(wu[:, k, hc * 128:(hc + 1) * 128], T["xn"][:, k, t0:t0 + n]) for k in range(NCH)],
                            reads=wu_t + xn_tags, writes=[tu])
                    self.silu_rr = getattr(self, "silu_rr", 0) + 1
                    sl = T["silu"][self.silu_rr % 2]
                    slt = ("silu", self.silu_rr % 2)
                    S.op("act", lambda h, sl=sl, psg=psg, n=n: h.activation(out=sl[:, 0:n], in_=psg[:, 0:n], func=AF.Silu),
                         reads=[tg], writes=[slt])
                    S.op("dve", lambda h, sl=sl, psu=psu, n=n, t0=t0, hc=hc, hg=hg: h.tensor_tensor(
                        out=hg[:, hc, t0:t0 + n], in0=sl[:, 0:n], in1=psu[:, 0:n], op=ALU.mult),
                         reads=[slt, tu], writes=[("hg", g % 2, hc, t0)])
            for m in range(NCH):
                for ti, (t0, n) in enumerate(FFN_TILES):
                    psd, td = self.psum()
                    self.mm(psd[:, 0:n], [(wd[:, hc, m * 128:(m + 1) * 128], hg[:, hc, t0:t0 + n]) for hc in range(G)],
                            reads=wd_t + [("hg", g % 2, hc, t0) for hc in range(G)], writes=[td])
                    S.op("dve", lambda h, psd=psd, m=m, t0=t0, n=n: h.scalar_tensor_tensor(
                        out=T["x"][:, m, t0:t0 + n], in0=psd[:, 0:n], scalar=0.5, in1=T["x"][:, m, t0:t0 + n],
                        op0=ALU.mult, op1=ALU.add), reads=[td, ("x", m, ti)], writes=[("x", m, ti)])


    def even_mixer(self, A, j, l):
        S, T = self.S, self.T
        self.norm_to_xn(l * 3 + 1)
        P = self.phase([("cbf", [8, NT], BF16), ("ub", [1116], F32), ("acc", [1088], F32), ("sg0", [512], F32),
                        ("sg1", [512], F32), ("ua", [1060], F32), ("ca", [1060], F32), ("gbs", [NT], F32),
                        ("ya0", [NT], BF16), ("ya1", [NT], BF16), ("hin", [8, 32], F32), ("stin", [8, 32], F32),
                        ("stc", [2, 8, 32], F32), ("hall", [4, 256], F32, "acc"), ("hbuf", [8, 32], F32, "ca")])
        xn_tags = [("xn", c) for c in range(NCH)]
        sg = [P["sg0"], P["sg1"]]
        pe_ = T["pe_"]
        win = A["even_w_in"][j].rearrange("(k p) (s c q) -> p s c k q", p=128, s=5, c=8, q=128)
        if not self.cfg.get("skip_stin"):
            S.dma("sp", P["stin"], A["st_conv"][j], writes=[("stin",)])
        rr = [0]

        def sgbuf():
            rr[0] += 1
            return sg[rr[0] % 2], ("sg", rr[0] % 2)

        def proj(w, si, t0, n, wt):
            ps, pt = self.psum()
            self.mm(ps[:, 0:n], [(w[:, si, k, :], T["xn"][:, k, t0:t0 + n]) for k in range(NCH)],
                    reads=wt + xn_tags, writes=[pt])
            return ps, pt

        es_ = self.cfg.get("even_stop")
        if es_ == 11:
            return
        for c in range(8):
            wB, wBt = self.wload([win[:, 3, c], win[:, 4, c]], 1, [2, NCH, 128])
            wA, wAt = self.wload([win[:, 0, c], win[:, 2, c]], 1, [2, NCH, 128])
            t0, n = 992, 32
            psx, tx = proj(wA, 0, t0, n, wAt)
            psc, tc = proj(wA, 1, t0, n, wAt)
            psv, tv = proj(wB, 0, t0, n, wBt)
            psg, tg = proj(wB, 1, t0, n, wBt)
            if es_ == 12:
                continue
            b0, bt0 = sgbuf()
            S.op("act", lambda h, b0=b0, psg=psg: h.activation(out=b0[:, 0:32], in_=psg[:, 0:32], func=AF.Sigmoid),
                 reads=[tg], writes=[bt0])
            S.op("dve", lambda h, b0=b0, psv=psv, c=c: h.tensor_tensor(out=P["hbuf"][:, c, 0:30], in0=psv[:, 2:32],
                                                                     in1=b0[:, 2:32], op=ALU.mult),
                 reads=[tv, bt0], writes=[("hbuf",)])
            b1, bt1 = sgbuf()
            S.op("act", lambda h, b1=b1, psx=psx: h.copy(out=b1[:, 0:32], in_=psx[:, 0:32]), reads=[tx], writes=[bt1])
            S.op("dve", lambda h, b1=b1, psc=psc, c=c: h.tensor_tensor(out=P["hbuf"][:, c, 30:32], in0=psc[:, 30:32],
                                                                     in1=b1[:, 30:32], op=ALU.mult),
                 reads=[tc, bt1], writes=[("hbuf",)])
        if es_ in (12, 13):
            return
        S.dma("sp", A["hg_in"], P["hbuf"].rearrange("p c t -> p (c t)"), reads=[("hbuf",)], writes=[("hg_in",)])
        if es_ == 14:
            return
        if self.cfg.get("no_cc"):
            S.dma_multi("sp", [(A["hg_out"][r * 128:(r + 1) * 128, :], A["hg_in"]) for r in range(4)],
                        reads=[("hg_in",)], writes=[("hg_out",)])
        else:
            S.collective(lambda h: h.collective_compute("AllGather", ALU.bypass, replica_groups=[[0, 1, 2, 3], [4, 5, 6, 7]],
                                                        ins=[A["hg_in"]], outs=[A["hg_out"]]),
                         reads=[("hg_in",)], writes=[("hg_out",)])
        S.dma("sp", P["hall"], A["hg_out"].rearrange("(r p) f -> p r f", p=128), reads=[("hg_out",)],
              writes=[("hall",)])
        hin2 = P["hin"].rearrange("p c t -> p (c t)")
        S.op("dve", lambda h: h.tensor_scalar(out=hin2, in0=P["hall"][:, 0, :], scalar1=T["coef"][:, 0:1], scalar2=None,
                                              op0=ALU.mult), reads=[("hall",), ("coef",)], writes=[("hin",)])
        for q in range(1, 4):
            S.op("dve", lambda h, q=q: h.scalar_tensor_tensor(out=hin2, in0=P["hall"][:, q, :],
                                                               scalar=T["coef"][:, q:q + 1], in1=hin2,
                                                               op0=ALU.mult, op1=ALU.add),
                 reads=[("hall",), ("coef",), ("hin",)], writes=[("hin",)])

        if self.cfg.get("even_stop") == 1:
            return
        UBO = [30, 542, 1084]
        UAO = [2, 514, 1028]
        for c in range(8):
            wB, wBt = self.wload([win[:, 3, c], win[:, 4, c]], 1, [2, NCH, 128])
            wA, wAt = self.wload([win[:, 0, c], win[:, 1, c], win[:, 2, c]], 2, [3, NCH, 128])
            S.op("act", lambda h, c=c: h.copy(out=P["ub"][:, 0:30], in_=P["hin"][:, c, 0:30]),
                 reads=[("hin",)], writes=[("ub", 0)])
            S.op("act", lambda h, c=c: h.copy(out=P["ub"][:, 1054:1084], in_=P["stin"][:, c, 0:30]),
                 reads=[("stin",)], writes=[("ub", 0)])
            for ti, (t0, n) in enumerate(TILES):
                psv, tv = proj(wB, 0, t0, n, wBt)
                psg, tg = proj(wB, 1, t0, n, wBt)
                b0, bt0 = sgbuf()
                S.op("act", lambda h, b0=b0, psg=psg, n=n: h.activation(out=b0[:, 0:n], in_=psg[:, 0:n], func=AF.Sigmoid),
                     reads=[tg], writes=[bt0])
                S.op("dve", lambda h, b0=b0, psv=psv, n=n, o=UBO[ti]: h.tensor_tensor(
                    out=P["ub"][:, o:o + n], in0=psv[:, 0:n], in1=b0[:, 0:n], op=ALU.mult),
                     reads=[tv, bt0], writes=[("ub", 0)])
            S.op("dve", lambda h, c=c: h.tensor_scalar(out=P["acc"][:, 0:1086], in0=P["ub"][:, 0:1086],
                                                       scalar1=pe_[:, j, c, 0:1], scalar2=pe_[:, j, c, 34:35],
                                                       op0=ALU.mult, op1=ALU.add),
                 reads=[("ub", 0), ("pe_",)], writes=[("acc",)])
            for k in range(1, 31):
                S.op("dve", lambda h, c=c, k=k: h.scalar_tensor_tensor(
                    out=P["acc"][:, 0:1086], in0=P["ub"][:, k:k + 1086], scalar=pe_[:, j, c, k:k + 1],
                    in1=P["acc"][:, 0:1086], op0=ALU.mult, op1=ALU.add),
                     reads=[("ub", 0), ("acc",)], writes=[("acc",)])
            S.op("act", lambda h, c=c: h.copy(out=P["cbf"][:, c, 0:1024], in_=P["acc"][:, 0:1024]),
                 reads=[("acc",), ("ub", 0)], writes=[("cbf", c)])
            S.op("act", lambda h, c=c: h.copy(out=P["cbf"][:, c, 1024:1056], in_=P["acc"][:, 1054:1086]),
                 reads=[("acc",)], writes=[("cbf", c)])
            S.op("act", lambda h, c=c: h.copy(out=P["stc"][:, 0, c, 0:30], in_=P["ub"][:, 1024:1054]),
                 reads=[("ub", 0)], writes=[("stc",)])
            S.op("act", lambda h, c=c: h.copy(out=P["stc"][:, 1, c, 0:30], in_=P["ub"][:, 1086:1116]),
                 reads=[("ub", 0)], writes=[("stc",)])
            S.op("act", lambda h, c=c: h.copy(out=P["ua"][:, 0:2], in_=P["hin"][:, c, 30:32]),
                 reads=[("hin",)], writes=[("ua",)])
            S.op("act", lambda h, c=c: h.copy(out=P["ua"][:, 1026:1028], in_=P["stin"][:, c, 30:32]),
                 reads=[("stin",)], writes=[("ua",)])
            for ti, (t0, n) in enumerate(TILES):
                psx, tx = proj(wA, 0, t0, n, wAt)
                psb, tb = proj(wA, 1, t0, n, wAt)
                psc, tc = proj(wA, 2, t0, n, wAt)
                b1, bt1 = sgbuf()
                S.op("act", lambda h, b1=b1, psx=psx, n=n: h.copy(out=b1[:, 0:n], in_=psx[:, 0:n]), reads=[tx], writes=[bt1])
                S.op("dve", lambda h, b1=b1, psc=psc, n=n, o=UAO[ti]: h.tensor_tensor(
                    out=P["ua"][:, o:o + n], in0=psc[:, 0:n], in1=b1[:, 0:n], op=ALU.mult),
                     reads=[tc, bt1], writes=[("ua",)])
                S.op("act", lambda h, psb=psb, t0=t0, n=n: h.copy(out=P["gbs"][:, t0:t0 + n], in_=psb[:, 0:n]),
                     reads=[tb], writes=[("gbs",)])
            S.op("dve", lambda h, c=c: h.tensor_scalar(out=P["ca"][:, 0:1058], in0=P["ua"][:, 0:1058],
                                                       scalar1=pe_[:, j, c, 31:32], scalar2=None, op0=ALU.mult),
                 reads=[("ua",), ("pe_",)], writes=[("ca",)])
            for k in (1, 2):
                S.op("dve", lambda h, c=c, k=k: h.scalar_tensor_tensor(
                    out=P["ca"][:, 0:1058], in0=P["ua"][:, k:k + 1058], scalar=pe_[:, j, c, 31 + k:32 + k],
                    in1=P["ca"][:, 0:1058], op0=ALU.mult, op1=ALU.add), reads=[("ua",), ("ca",)], writes=[("ca",)])
            ya = P["ya%d" % (c % 2)]
            yat = ("ya", c % 2)
            S.op("dve", lambda h, ya=ya: h.tensor_tensor(out=ya[:, 0:1024], in0=P["gbs"][:, 0:1024], in1=P["ca"][:, 0:1024],
                                                         op=ALU.mult), reads=[("gbs",), ("ca",)], writes=[yat])
            S.op("dve", lambda h, ya=ya: h.tensor_tensor(out=ya[:, 1024:1056], in0=P["gbs"][:, 1024:1056],
                                                         in1=P["ca"][:, 1026:1058], op=ALU.mult),
                 reads=[("gbs",), ("ca",)], writes=[yat])
            S.dma("sp", A["ya_scr"][:, c, :], ya.bitcast(F32), reads=[yat], writes=[("ya_scr",)])
            S.op("act", lambda h, c=c: h.copy(out=P["stc"][:, 0, c, 30:32], in_=P["ua"][:, 1024:1026]),
                 reads=[("ua",)], writes=[("stc",)])
            S.op("act", lambda h, c=c: h.copy(out=P["stc"][:, 1, c, 30:32], in_=P["ua"][:, 1058:1060]),
                 reads=[("ua",)], writes=[("stc",)])
        S.dma("sp", A["out_conv"][2 * j:2 * j + 2].rearrange("g p f -> p g f"),
              P["stc"].rearrange("p g c t -> p g (c t)"), reads=[("stc",)])

        if self.cfg.get("even_stop") == 2:
            return
        meanb, varb = P["acc"], P["ub"]
        for ti, (t0, n) in enumerate(TILES):
            psa, ta = self.psum()
            psq, tq = self.psum()
            for c in range(8):
                sq = T["sq"][c % 2]
                sqt = ("sq", c % 2)
                S.op("act", lambda h, sq=sq, c=c, t0=t0, n=n: h.activation(out=sq[:, 0:n], in_=P["cbf"][:, c, t0:t0 + n],
                                                                           func=AF.Square),
                     reads=[("cbf", c)], writes=[sqt])
                S.op("pe", lambda h, psa=psa, c=c, t0=t0, n=n: h.matmul(psa[:, 0:n], lhsT=T["ones_bf"][:],
                                                                        rhs=P["cbf"][:, c, t0:t0 + n], start=(c == 0),
                                                                        stop=(c == 7)),
                     reads=[("cbf", c), ("ones_bf",)], writes=[ta])
                S.op("pe", lambda h, psq=psq, sq=sq, c=c, n=n: h.matmul(psq[:, 0:n], lhsT=T["ones_bf"][:], rhs=sq[:, 0:n],
                                                                        start=(c == 0), stop=(c == 7)),
                     reads=[sqt, ("ones_bf",)], writes=[tq])
            S.op("act", lambda h, psa=psa, t0=t0, n=n: h.activation(out=meanb[:, t0:t0 + n], in_=psa[:, 0:n],
                                                                    func=AF.Copy, scale=1.0 / 1024),
                 reads=[ta], writes=[("meanb", ti)])
            S.op("dve", lambda h, t0=t0, n=n: h.tensor_tensor(out=P["gbs"][:, t0:t0 + n], in0=meanb[:, t0:t0 + n],
                                                              in1=meanb[:, t0:t0 + n], op=ALU.mult),
                 reads=[("meanb", ti)], writes=[("msq", ti)])
            S.op("dve", lambda h, psq=psq, t0=t0, n=n: h.scalar_tensor_tensor(
                out=varb[:, t0:t0 + n], in0=psq[:, 0:n], scalar=1.0 / 1024, in1=P["gbs"][:, t0:t0 + n],
                op0=ALU.mult, op1=ALU.subtract), reads=[tq, ("msq", ti)], writes=[("varb", ti)])
            S.op("act", lambda h, t0=t0, n=n: h.activation(out=varb[:, t0:t0 + n], in_=varb[:, t0:t0 + n], func=AF.Sqrt,
                                                           bias=T["cst"][:, 256:257]),
                 reads=[("varb", ti), ("cst",)], writes=[("varb", ti)])
            S.op("dve", lambda h, t0=t0, n=n: h.reciprocal(out=varb[:, t0:t0 + n], in_=varb[:, t0:t0 + n]),
                 reads=[("varb", ti)], writes=[("varb", ti)])
        st = [("meanb", ti) for ti in range(3)] + [("varb", ti) for ti in range(3)]
        tmps = [P["ua"], P["ca"]]
        for c in range(8):
            tm = tmps[c % 2]
            tmt = ("lntmp", c % 2)
            S.op("dve", lambda h, tm=tm, c=c: h.tensor_tensor(out=tm[:, 0:NT], in0=P["cbf"][:, c, :], in1=meanb[:, 0:NT],
                                                              op=ALU.subtract), reads=[("cbf", c)] + st, writes=[tmt])
            S.op("dve", lambda h, tm=tm: h.tensor_tensor(out=tm[:, 0:NT], in0=tm[:, 0:NT], in1=varb[:, 0:NT], op=ALU.mult),
                 reads=[tmt] + st, writes=[tmt])
            S.op("act", lambda h, tm=tm, c=c: h.activation(out=T["xn"][:, 8 + c, :], in_=tm[:, 0:NT], func=AF.Silu,
                                                           scale=pe_[:, j, c, 35:36], bias=pe_[:, j, c, 36:37]),
                 reads=[tmt, ("pe_",)], writes=[("xn", 8 + c)])
        if self.cfg.get("even_stop") == 3:
            return
        S.dma("sp", T["xn"][:, 0:8, :].bitcast(F32), A["ya_scr"], reads=[("ya_scr",)],
              writes=[("xn", c) for c in range(8)])
        self.out_proj(A["even_w_out"][j])


    def odd_mixer(self, A, j, l):
        S, T, cfg = self.S, self.T, self.cfg
        self.norm_to_xn(l * 3 + 1)
        CH = [(b * 128, 128) for b in range(8)] + [(1024, 32)]
        P = self.phase([("cs", [NT], F32), ("U", [NT], F32),
                        ("pT", [9, 8], F32), ("pTl", [8, 8], F32), ("flT", [9, 8], F32), ("decB", [8, 12], F32),
                        ("hs", [64], F32), ("bc", [64], F32), ("small", [128], F32),
                        ("w0", [NT], F32), ("w1", [NT], F32), ("w2", [1060], F32), ("w3", [NT], F32), ("w4", [NT], F32),
                        ("b0", [NT], BF16), ("b1", [NT], BF16), ("b2", [9, 128], BF16), ("b3", [9, 130], BF16),
                        ("yb0", [NT], BF16), ("yb1", [NT], BF16),
                        ("st", [132], F32), ("stb", [132], BF16), ("t0", [132], F32), ("t1", [132], F32),
                        ("m0", [128], BF16), ("m1", [128], BF16), ("m2", [128], BF16), ("m3", [128], BF16),
                        ("gl", [4, 80], F32), ("cl", [4, 132], F32), ("cvec", [16], F32), ("contrib", [24], F32), ("lbv", [16], F32)])
        P["hm"] = P["w4"]
        zero_b = T["cst"][:, 259:260]
        S.op("dve", lambda h: h.memset(P["b3"][:, :, :], 0.0), writes=[("b3",)])
        S.op("dve", lambda h: h.memset(P["st"][:, :], 0.0), writes=[("st",)])
        S.op("dve", lambda h: h.memset(P["stb"][:, :], 0.0), writes=[("stb",)])
        xn_tags = [("xn", c) for c in range(NCH)]
        cst, coef, po, pg = T["cst"], T["coef"], T["po"], T["pg"]
        ident = cst[:, 0:128]
        mask = cst[:, 128:256]
        one_c = cst[:, 257:258]
        eps_c = cst[:, 256:257]
        neg_c = cst[:, 258:259]
        win = A["odd_w_in"][j].rearrange("(k p) c -> p k c", p=128)
        SEG = [(0, 1024), (1024, 32)]
        uid = [0]

        PHYS = dict(ktok="b2", pv="b3", fg="w1", glog="w0", qq="w3", sgg="w4", vtok="b2", BB="w2", kgt="b3", qT="b0", kT="b1", sm="small",
                    smd="small", sm2="bc9")

        def tg(name):
            return (PHYS[name],)

        def fproj(w, wt, t0, n, m=128):
            ps, pt = self.psum()
            self.mm(ps[0:m, 0:n], [(w[:, k, :], T["xn"][:, k, t0:t0 + n]) for k in range(NCH)], reads=wt + xn_tags, writes=[pt])
            return ps, pt

        def tproj(w, wt, t0, n, col):
            ps, pt = col[0], col[1]
            self.mm(ps[0:n, col[2]:col[2] + 128], [(T["xn"][:, k, t0:t0 + n], w[:, k, :]) for k in range(NCH)],
                    reads=wt + xn_tags, writes=[pt])

        wgate, wgt = self.wload(win[:, :, 8192:8208], 1, [NCH, 16])
        S.op("act", lambda h: h.mul(out=P["cvec"][0:8, 0:1], in_=pg[0:8, j, 1:2], mul=-1.0), reads=[("pg",)], writes=[("cvec",)])
        gI, cs, U, hm = P["w0"], P["cs"], P["U"], P["hm"]
        for (t0, n) in TILES:
            ps, pt = self.psum()
            self.mm(ps[0:8, 0:n], [(wgate[:, k, 0:8], T["xn"][:, k, t0:t0 + n]) for k in range(NCH)], reads=wgt + xn_tags, writes=[pt])
            S.op("act", lambda h, ps=ps, t0=t0, n=n: h.activation(out=gI[0:8, t0:t0 + n], in_=ps[0:8, 0:n], func=AF.Identity,
                                                                  bias=pg[0:8, j, 0:1]), reads=[pt, ("pg",)], writes=[("w0",)])
            ps2, pt2 = self.psum()
            self.mm(ps2[0:8, 0:n], [(wgate[:, k, 8:16], T["xn"][:, k, t0:t0 + n]) for k in range(NCH)], reads=wgt + xn_tags, writes=[pt2])
            S.op("act", lambda h, ps2=ps2, t0=t0, n=n: h.activation(out=hm[0:8, t0:t0 + n], in_=ps2[0:8, 0:n], func=AF.Exp, scale=-1.0,
                                                                    bias=P["cvec"][0:8, 0:1]), reads=[pt2, ("cvec",)], writes=[("w4",)])
            S.op("act", lambda h, t0=t0, n=n: h.activation(out=hm[0:8, t0:t0 + n], in_=hm[0:8, t0:t0 + n], func=AF.Ln,
                                                           bias=one_c[0:8, :]), reads=[("w4",), ("cst",)], writes=[("w4",)])
        hm_t = [("w4",)]
        gi_t = [("w0",)]
        for (s0, sn) in SEG:
            S.op("dve", lambda h, s0=s0, sn=sn: h.tensor_tensor_scan(out=cs[0:8, s0:s0 + sn], data0=hm[0:8, s0:s0 + sn],
                                                                     data1=zero_b[0:8, :].broadcast_to([8, sn]), initial=0.0, op0=ALU.add, op1=ALU.add),
                 reads=hm_t + [("cst",)], writes=[("cs", s0)])
        cs_t = [("cs", s0) for s0, _ in SEG]
        S.op("dve", lambda h: h.tensor_tensor(out=U[0:8, :], in0=gI[0:8, :], in1=cs[0:8, :], op=ALU.add), reads=gi_t + cs_t, writes=[("U",)])
        for (s0, sn) in SEG:
            S.op("dve", lambda h, s0=s0, sn=sn: h.tensor_tensor_scan(out=hm[0:8, s0:s0 + sn], data0=U[0:8, s0:s0 + sn],
                                                                     data1=U[0:8, s0:s0 + sn], initial=NEG, op0=ALU.max, op1=ALU.max),
                 reads=[("U",)] + hm_t, writes=[("w4",)])
        cm_t = [("w4",)]
        hs = P["hs"]
        S.op("dve", lambda h: h.tensor_copy(out=hs[0:8, 0:8], in_=hm[0:8, 127:1024:128]), reads=cm_t, writes=[("hs", 0)])
        S.op("dve", lambda h: h.tensor_copy(out=hs[0:8, 8:9], in_=hm[0:8, 1055:1056]), reads=cm_t, writes=[("hs", 0)])
        S.op("dve", lambda h: h.tensor_scalar(out=hs[0:8, 16:17], in0=cs[0:8, 1023:1024], scalar1=-1.0, scalar2=None, op0=ALU.mult),
             reads=cs_t, writes=[("hs", 1)])
        S.op("dve", lambda h: h.tensor_tensor(out=hs[0:8, 17:18], in0=hs[0:8, 16:17], in1=hs[0:8, 7:8], op=ALU.add),
             reads=[("hs", 0), ("hs", 1)], writes=[("hs", 1)])
        S.op("dve", lambda h: h.tensor_scalar(out=hs[0:8, 18:19], in0=hs[0:8, 7:8], scalar1=-1.0, scalar2=None, op0=ALU.mult),
             reads=[("hs", 0)], writes=[("hs", 2)])
        S.op("act", lambda h: h.activation(out=P["w2"][0:8, 0:1024], in_=U[0:8, 0:1024], func=AF.Exp, bias=hs[0:8, 18:19]),
             reads=[("U",), ("hs", 2)], writes=[("w2",)])
        for b in range(8):
            ps, pt = self.psum()
            self.tr(ps[0:128, 0:8], P["w2"][0:8, b * 128:(b + 1) * 128], ident[0:8, 0:8], reads=[("w2",), ("cst",)], writes=[pt])
            S.op("act", lambda h, ps=ps, b=b: h.copy(out=P["pTl"][:, b, :], in_=ps[:, 0:8]), reads=[pt], writes=[("pTl",)])
        S.dma_multi("sp", [(A["hm_scr"][a:a + 1, :].rearrange("a h -> h a"), hs[0:8, 16 + a:17 + a]) for a in range(2)],
                    reads=[("hs", 1)], writes=[("hm_scr",)])
        S.dma("sp", P["contrib"][:, 8:24], A["hm_scr"].rearrange("a h -> (a h)").partition_broadcast(128),
              reads=[("hm_scr",)], writes=[("contrib", 1)])

        if cfg.get("odd_stop") == 1:
            return
        if j == 0:
            S.op("dve", lambda h: h.memset(P["lbv"][:, 0:8], 0.0), writes=[("lb",)])
        else:
            S.op("dve", lambda h: h.tensor_tensor(out=P["lbv"][:, 0:8], in0=po[:, j, 24:32], in1=po[:, j, 16:24], op=ALU.subtract),
                 reads=[("po",)], writes=[("lb",)])
            S.op("act", lambda h: h.activation(out=P["lbv"][:, 0:8], in_=P["lbv"][:, 0:8], func=AF.Sigmoid), reads=[("lb",)], writes=[("lb",)])
        S.op("dve", lambda h: h.tensor_scalar(out=P["lbv"][:, 8:16], in0=P["lbv"][:, 0:8], scalar1=-1.0, scalar2=1.0, op0=ALU.mult, op1=ALU.add),
             reads=[("lb",)], writes=[("oml",)])
        lbt, omlt = P["lbv"][:, 0:8], P["lbv"][:, 8:16]

        def prep_c_kv(hd, ptile):
            wk, wkt = self.wload([win[:, :, 1024 + hd * 128:1024 + (hd + 1) * 128], win[:, :, 2048 + hd * 128:2048 + (hd + 1) * 128]], 1, [2, NCH, 128])
            ktag, vtag = tg("ktok"), tg("pv")
            for ci, (t0, n) in enumerate(CH):
                if ptile is P["pTl"] and ci == 8:
                    continue
                ps, pt = self.psum()
                psv_, ptv_ = self.psum()
                tproj(wk[:, 0], wkt, t0, n, (ps, pt, 0))
                tproj(wk[:, 1], wkt, t0, n, (psv_, ptv_, 0))
                if cfg.get("odd_stop") == 15:
                    continue
                S.op("act", lambda h, ps=ps, ci=ci, n=n: h.activation(out=P["b2"][0:n, ci, :], in_=ps[0:n, 0:128], func=AF.Copy, scale=128 ** -0.5),
                     reads=[pt], writes=[ktag])
                S.op("dve", lambda h, ps=psv_, ci=ci, n=n: h.tensor_scalar(out=P["b3"][0:n, ci, 0:128], in0=ps[0:n, 0:128], scalar1=ptile[0:n, ci, hd:hd + 1],
                                                                           scalar2=None, op0=ALU.mult), reads=[ptv_, ("pT",), ("pTl",)], writes=[vtag])
                if cfg.get("odd_stop") != 16:
                    S.op("act", lambda h, ci=ci, n=n: h.copy(out=P["b3"][0:n, ci, 128:129], in_=ptile[0:n, ci, hd:hd + 1]), reads=[("pT",), ("pTl",)], writes=[vtag])
            return ktag, vtag

        def prep_d(hd, need_q):
            pieces = [win[:, :, 5120 + hd * 128:5120 + (hd + 1) * 128], win[:, :, 6144 + hd * 128:6144 + (hd + 1) * 128]]
            if need_q:
                pieces += [win[:, :, 4096 + hd * 128:4096 + (hd + 1) * 128], win[:, :, 7168 + hd * 128:7168 + (hd + 1) * 128]]
            wd_, wdt = self.wload(pieces, 1 if not need_q else 2, [len(pieces), NCH, 128])
            fgt, glt, qt, sgt, vt = tg("fg"), tg("glog"), tg("qq"), tg("sgg"), tg("vtok")
            for (t0, n) in TILES:
                ps, pt = fproj(wd_[:, 0], wdt, t0, n)
                S.op("act", lambda h, ps=ps, t0=t0, n=n: h.activation(out=P["w1"][:, t0:t0 + n], in_=ps[:, 0:n], func=AF.Sigmoid), reads=[pt], writes=[fgt])
                if need_q:
                    ps, pt = fproj(wd_[:, 2], wdt, t0, n)
                    S.op("act", lambda h, ps=ps, t0=t0, n=n: h.activation(out=P["w3"][:, t0:t0 + n], in_=ps[:, 0:n], func=AF.Silu), reads=[pt], writes=[qt])
                    ps, pt = fproj(wd_[:, 3], wdt, t0, n)
                    S.op("act", lambda h, ps=ps, t0=t0, n=n: h.activation(out=P["w4"][:, t0:t0 + n], in_=ps[:, 0:n], func=AF.Sigmoid), reads=[pt], writes=[sgt])
            S.op("dve", lambda h: h.tensor_scalar(out=P["w1"][:, :], in0=P["w1"][:, :], scalar1=omlt[:, hd:hd + 1], scalar2=lbt[:, hd:hd + 1],
                                                  op0=ALU.mult, op1=ALU.add), reads=[fgt, ("lb",), ("oml",)], writes=[fgt])
            S.op("act", lambda h: h.activation(out=P["w0"][:, :], in_=P["w1"][:, :], func=AF.Ln), reads=[fgt], writes=[glt])
            S.op("dve", lambda h: h.tensor_scalar(out=P["w1"][:, :], in0=P["w1"][:, :], scalar1=-1.0, scalar2=1.0, op0=ALU.mult, op1=ALU.add),
                 reads=[fgt, glt], writes=[fgt])
            bbt = tg("BB")
            S.op("dve", lambda h: h.memset(P["w2"][:, 0:1], 0.0), writes=[bbt])
            S.op("dve", lambda h: h.memset(P["w2"][:, 1025:1026], 0.0), reads=[bbt], writes=[bbt])
            S.op("dve", lambda h: h.tensor_tensor_scan(out=P["w2"][:, 1:1025], data0=P["w0"][:, 0:1024], data1=zero_b.broadcast_to([128, 1024]), initial=0.0,
                                                       op0=ALU.add, op1=ALU.add), reads=[glt, bbt, ("cst",)], writes=[bbt])
            S.op("dve", lambda h: h.tensor_tensor_scan(out=P["w2"][:, 1026:1058], data0=P["w0"][:, 1024:1056], data1=zero_b.broadcast_to([128, 32]), initial=0.0,
                                                       op0=ALU.add, op1=ALU.add), reads=[glt, bbt, ("cst",)], writes=[bbt])
            for ci, (t0, n) in enumerate(CH):
                ps, pt = self.psum()
                tproj(wd_[:, 1], wdt, t0, n, (ps, pt, 0))
                S.op("act", lambda h, ps=ps, ci=ci, n=n: h.copy(out=P["b2"][0:n, ci, :], in_=ps[0:n, 0:128]), reads=[pt], writes=[vt])
            return fgt, bbt, qt, sgt, vt

        def bbi(t):
            return (t + 1) if t < 1024 else (t + 2)

        F_C, F_S, F_B = 0, 0, 1024

        if cfg.get("odd_stop") == 13:
            return
        for hd in range(8):
            ktag, vtag = prep_c_kv(hd, P["pTl"])
            if cfg.get("odd_stop") in (14, 15, 16):
                continue
            ps, pt = self.psum()
            self.mm(ps[:, 0:130], [(P["b2"][:, b, :], P["b3"][:, b, 0:130]) for b in range(8)], reads=[ktag, vtag], writes=[pt])
            S.op("act", lambda h, ps=ps: h.copy(out=P["st"][:, 0:129], in_=ps[:, 0:129]), reads=[pt], writes=[("st",)])
            S.dma("sp", A["oga_in"][:, F_C + hd * 129:F_C + (hd + 1) * 129], P["st"][:, 0:129], reads=[("st",)], writes=[("oga_in",)])
        if cfg.get("odd_stop") == 12:
            return
        for hd in range(8):
            fgt, bbt, qt, sgt, vt = prep_d(hd, False)
            S.op("act", lambda h: h.activation(out=P["w3"][:, 0:1024], in_=P["w2"][:, 1:1025], func=AF.Exp, scale=-1.0, bias=P["w2"][:, 1024:1025]),
                 reads=[bbt], writes=[("w3",)])
            S.op("dve", lambda h: h.tensor_tensor(out=P["w3"][:, 0:1024], in0=P["w3"][:, 0:1024], in1=P["w1"][:, 0:1024], op=ALU.mult),
                 reads=[("w3",), fgt], writes=[("w3",)])
            S.op("act", lambda h, hd=hd: h.copy(out=P["contrib"][:, hd:hd + 1], in_=P["w2"][:, 1024:1025]), reads=[bbt], writes=[("contrib", 0)])
            kts = []
            for b in range(8):
                ps, pt = self.psum()
                self.tr(ps[:, 0:128], P["w3"][:, b * 128:(b + 1) * 128], ident, reads=[("w3",), ("cst",)], writes=[pt])
                kt = tg("kgt")
                S.op("act", lambda h, ps=ps, b=b: h.copy(out=P["b3"][:, b, 0:128], in_=ps[:, 0:128]), reads=[pt], writes=[kt])
                kts.append(kt)
            ps, pt = self.psum()
            self.mm(ps[:, 0:128], [(P["b3"][:, b, 0:128], P["b2"][:, b, :]) for b in range(8)], reads=kts + [vt], writes=[pt])
            S.op("act", lambda h, ps=ps: h.copy(out=P["st"][:, 0:128], in_=ps[:, 0:128]), reads=[pt], writes=[("st",)])
            S.dma("sp", A["ogb_in"][:, F_S + hd * 128:F_S + (hd + 1) * 128], P["st"][:, 0:128], reads=[("st",)], writes=[("ogb_in",)])
        S.dma("sp", A["ogb_in"][:, F_B:F_B + 24], P["contrib"][:, 0:24], reads=[("contrib", 0), ("contrib", 1)], writes=[("ogb_in",)])
        for nm in ("oga", "ogb"):
            if cfg.get("no_cc"):
                S.dma_multi("sp", [(A[nm + "_out"][r * 128:(r + 1) * 128, :], A[nm + "_in"]) for r in range(4)], reads=[(nm + "_in",)], writes=[(nm + "_out",)])
            else:
                S.collective(lambda h, nm=nm: h.collective_compute("AllGather", ALU.bypass, replica_groups=[[0, 1, 2, 3], [4, 5, 6, 7]],
                                                                   ins=[A[nm + "_in"]], outs=[A[nm + "_out"]]), reads=[(nm + "_in",)], writes=[(nm + "_out",)])
        ogva = A["oga_out"].rearrange("(r p) f -> p r f", p=128)
        ogvb = A["ogb_out"].rearrange("(r p) f -> p r f", p=128)

        if cfg.get("odd_stop") == 2:
            return
        gl = P["gl"]
        S.dma("sp", gl[:, :, 0:24], ogvb[:, :, F_B:F_B + 24], reads=[("ogb_out",)], writes=[("gl",)])
        for q in range(4):
            S.op("dve", lambda h, q=q: h.tensor_scalar(out=gl[:, q, 32:40], in0=gl[:, q, 16:24], scalar1=coef[:, 4 + q:5 + q], scalar2=None, op0=ALU.add),
                 reads=[("gl",), ("coef",)], writes=[("gE", q)])
            S.op("dve", lambda h, q=q: h.tensor_scalar(out=gl[:, q, 64:72], in0=gl[:, q, 0:8], scalar1=0.0, scalar2=coef[:, 4 + q:5 + q], op0=ALU.mult, op1=ALU.add),
                 reads=[("gl",), ("coef",)], writes=[("gG", q)])
            for q2 in range(4):
                S.op("dve", lambda h, q=q, q2=q2: h.scalar_tensor_tensor(out=gl[:, q, 32:40], in0=gl[:, q2, 8:16], scalar=coef[:, 8 + 4 * q + q2:9 + 4 * q + q2],
                                                                         in1=gl[:, q, 32:40], op0=ALU.mult, op1=ALU.add),
                     reads=[("gl",), ("coef",), ("gE", q)], writes=[("gE", q)])
                S.op("dve", lambda h, q=q, q2=q2: h.scalar_tensor_tensor(out=gl[:, q, 64:72], in0=gl[:, q2, 0:8], scalar=coef[:, 8 + 4 * q + q2:9 + 4 * q + q2],
                                                                         in1=gl[:, q, 64:72], op0=ALU.mult, op1=ALU.add),
                     reads=[("gl",), ("coef",), ("gG", q)], writes=[("gG", q)])
            S.op("act", lambda h, q=q: h.activation(out=gl[:, q, 64:72], in_=gl[:, q, 64:72], func=AF.Exp), reads=[("gG", q)], writes=[("gG", q)])
        bc = P["bc"]
        S.op("dve", lambda h: h.tensor_scalar(out=bc[:, 8:16], in0=gl[:, 0, 8:16], scalar1=coef[:, 24:25], scalar2=None, op0=ALU.mult),
             reads=[("gl",), ("coef",)], writes=[("bc", 1)])
        for q2 in range(1, 4):
            S.op("dve", lambda h, q2=q2: h.scalar_tensor_tensor(out=bc[:, 8:16], in0=gl[:, q2, 8:16], scalar=coef[:, 24 + q2:25 + q2], in1=bc[:, 8:16],
                                                                op0=ALU.mult, op1=ALU.add), reads=[("gl",), ("coef",), ("bc", 1)], writes=[("bc", 1)])
        S.op("dve", lambda h: h.tensor_tensor(out=bc[:, 0:8], in0=bc[:, 8:16], in1=gl[:, 0, 32:40], op=ALU.max), reads=[("bc", 1), ("gE", 0)], writes=[("bc", 0)])
        for q in range(1, 4):
            S.op("dve", lambda h, q=q: h.tensor_tensor(out=bc[:, 0:8], in0=bc[:, 0:8], in1=gl[:, q, 32:40], op=ALU.max), reads=[("bc", 0), ("gE", q)], writes=[("bc", 0)])
        for q in range(4):
            S.op("dve", lambda h, q=q: h.tensor_tensor(out=gl[:, q, 32:40], in0=gl[:, q, 32:40], in1=bc[:, 0:8], op=ALU.subtract), reads=[("bc", 0), ("gE", q)], writes=[("gE", q)])
            S.op("act", lambda h, q=q: h.activation(out=gl[:, q, 32:40], in_=gl[:, q, 32:40], func=AF.Exp), reads=[("gE", q)], writes=[("gE", q)])
        S.dma("sp", bc[:, 16:24], A["st_mb"][:, j, :], writes=[("bc", 2)])
        S.op("dve", lambda h: h.tensor_tensor(out=P["small"][0:8, 0:8], in0=bc[0:8, 0:8], in1=ident[0:8, 0:8], op=ALU.mult), reads=[("bc", 0), ("cst",)], writes=[("small",)])
        S.op("dve", lambda h: h.tensor_reduce(out=hs[0:8, 40:41], in_=P["small"][0:8, 0:8], axis=AX.X, op=ALU.add), reads=[("small",)], writes=[("hs", 3)])
        S.dma("sp", hs[0:8, 41:42], A["st_m"][:, j:j + 1], writes=[("hs", 4)], allow_slow_non_contiguous=True)
        S.op("dve", lambda h: h.tensor_scalar(out=hs[0:8, 50:58], in0=hs[0:8, 0:8], scalar1=hs[0:8, 40:41], scalar2=None, op0=ALU.max), reads=[("hs", 0), ("hs", 3)], writes=[("hs", 5)])
        S.op("dve", lambda h: h.tensor_scalar(out=hs[0:8, 58:59], in0=hs[0:8, 8:9], scalar1=hs[0:8, 41:42], scalar2=None, op0=ALU.max), reads=[("hs", 0), ("hs", 4), ("hs", 5)], writes=[("hs", 5)])
        S.op("dve", lambda h: h.tensor_scalar(out=hs[0:8, 20:29], in0=hs[0:8, 50:59], scalar1=-1.0, scalar2=None, op0=ALU.mult), reads=[("hs", 5)], writes=[("hs", 6)])
        S.op("dve", lambda h: h.tensor_copy(out=hs[0:8, 31:38], in_=hs[0:8, 50:57]), reads=[("hs", 5)], writes=[("hs", 7)])
        S.op("dve", lambda h: h.tensor_copy(out=hs[0:8, 30:31], in_=hs[0:8, 40:41]), reads=[("hs", 3), ("hs", 7)], writes=[("hs", 7)])
        S.op("dve", lambda h: h.tensor_copy(out=hs[0:8, 38:39], in_=hs[0:8, 41:42]), reads=[("hs", 4), ("hs", 7)], writes=[("hs", 7)])
        S.op("dve", lambda h: h.tensor_tensor(out=hs[0:8, 30:39], in0=hs[0:8, 30:39], in1=hs[0:8, 50:59], op=ALU.subtract), reads=[("hs", 7), ("hs", 5)], writes=[("hs", 7)])
        S.op("act", lambda h: h.activation(out=hs[0:8, 30:39], in_=hs[0:8, 30:39], func=AF.Exp), reads=[("hs", 7)], writes=[("hs", 7)])
        S.dma("sp", A["dec_scr"], hs[0:8, 30:39], reads=[("hs", 7)], writes=[("dec_scr",)])
        for hh in range(8):
            S.dma("sp", P["decB"][:, hh, 0:9], A["dec_scr"][hh, :].partition_broadcast(128), reads=[("dec_scr",)], writes=[("decB",)])
        S.op("dve", lambda h: h.tensor_tensor(out=hs[0:8, 44:45], in0=hs[0:8, 57:58], in1=cs[0:8, 1023:1024], op=ALU.subtract), reads=[("hs", 5)] + cs_t, writes=[("hs", 8)])
        S.op("dve", lambda h: h.tensor_tensor(out=hs[0:8, 45:46], in0=hs[0:8, 58:59], in1=cs[0:8, 1055:1056], op=ALU.subtract), reads=[("hs", 5), ("hs", 8)] + cs_t, writes=[("hs", 8)])
        S.dma("sp", A["out_m"][:, 2 * j:2 * j + 2], hs[0:8, 44:46], reads=[("hs", 8)], allow_slow_non_contiguous=True)
        for ci, (t0, n) in enumerate(CH):
            S.op("act", lambda h, ci=ci, t0=t0, n=n: h.activation(out=P["w2"][0:8, t0:t0 + n], in_=U[0:8, t0:t0 + n], func=AF.Exp, bias=hs[0:8, 20 + ci:21 + ci]),
                 reads=[("U",), ("hs", 6), ("w3",)], writes=[("w2",)])
            S.op("act", lambda h, ci=ci, t0=t0, n=n: h.activation(out=P["w3"][0:8, t0:t0 + n], in_=cs[0:8, t0:t0 + n], func=AF.Exp, bias=hs[0:8, 20 + ci:21 + ci]),
                 reads=cs_t + [("hs", 6), ("w3",)], writes=[("w3",)])
            ps, pt = self.psum()
            self.tr(ps[0:n, 0:8], P["w2"][0:8, t0:t0 + n], ident[0:8, 0:8], reads=[("w2",), ("cst",)], writes=[pt])
            S.op("act", lambda h, ps=ps, ci=ci, n=n: h.copy(out=P["pT"][0:n, ci, :], in_=ps[0:n, 0:8]), reads=[pt], writes=[("pT",)])
            ps, pt = self.psum()
            self.tr(ps[0:n, 0:8], P["w3"][0:8, t0:t0 + n], ident[0:8, 0:8], reads=[("w3",), ("cst",)], writes=[pt])
            S.op("act", lambda h, ps=ps, ci=ci, n=n: h.copy(out=P["flT"][0:n, ci, :], in_=ps[0:n, 0:8]), reads=[pt], writes=[("flT",)])


        if cfg.get("odd_stop") == 3:
            return
        ybs = [P["yb0"], P["yb1"]]
        mts = [P["m0"], P["m1"], P["m2"], P["m3"]]
        stt = ("st",)

        def load_state_c(hd, g):
            if g == 0:
                S.dma("sp", P["cl"][:, :, 0:129], ogva[:, :, F_C + hd * 129:F_C + (hd + 1) * 129], reads=[("oga_out",)], writes=[("cl",)])
                S.op("dve", lambda h: h.tensor_scalar(out=P["st"][:, 0:129], in0=P["cl"][:, 0, 0:129], scalar1=gl[:, 0, 32 + hd:33 + hd], scalar2=None, op0=ALU.mult),
                     reads=[("cl",), ("gE", 0)], writes=[stt])
                for q in range(1, 4):
                    S.op("dve", lambda h, q=q: h.scalar_tensor_tensor(out=P["st"][:, 0:129], in0=P["cl"][:, q, 0:129], scalar=gl[:, q, 32 + hd:33 + hd], in1=P["st"][:, 0:129],
                                                                     op0=ALU.mult, op1=ALU.add), reads=[("cl",), ("gE", q), stt], writes=[stt])
            else:
                S.dma("sp", P["st"][:, 0:129], A["st_c"][j, :, hd, :], writes=[stt])

        def load_state_s(hd, g):
            if g == 0:
                S.dma("sp", P["cl"][:, :, 0:128], ogvb[:, :, F_S + hd * 128:F_S + (hd + 1) * 128], reads=[("ogb_out",)], writes=[("cl",)])
                S.op("dve", lambda h: h.tensor_scalar(out=P["st"][:, 0:128], in0=P["cl"][:, 0, 0:128], scalar1=gl[:, 0, 64 + hd:65 + hd], scalar2=None, op0=ALU.mult),
                     reads=[("cl",), ("gG", 0)], writes=[stt])
                for q in range(1, 4):
                    S.op("dve", lambda h, q=q: h.scalar_tensor_tensor(out=P["st"][:, 0:128], in0=P["cl"][:, q, 0:128], scalar=gl[:, q, 64 + hd:65 + hd], in1=P["st"][:, 0:128],
                                                                     op0=ALU.mult, op1=ALU.add), reads=[("cl",), ("gG", q), stt], writes=[stt])
            else:
                S.dma("sp", P["st"][:, 0:128], A["st_s"][j, :, hd, :], writes=[stt])

        def emit_y(yb, ybt, trs, gate, ncol, hd, gtag=None):
            for (ps, pt, t0, n) in trs:
                S.op("dve", lambda h, ps=ps, t0=t0, n=n: h.scalar_tensor_tensor(out=yb[:, t0:t0 + n], in0=ps[:, 0:n], scalar=po[:, j, ncol + hd:ncol + hd + 1],
                                                                               in1=gate[:, t0:t0 + n], op0=ALU.mult, op1=ALU.mult),
                     reads=[pt, ("po",), gtag], writes=[ybt])

        rr = [0]

        def c_head(hd):
            ktag, vtag = prep_c_kv(hd, P["pT"])
            wq, wqt = self.wload([win[:, :, hd * 128:(hd + 1) * 128], win[:, :, 1024 + hd * 128:1024 + (hd + 1) * 128],
                                  win[:, :, 3072 + hd * 128:3072 + (hd + 1) * 128]], 2, [3, NCH, 128])
            qtg, ktg = tg("qT"), tg("kT")
            for (t0, n) in TILES:
                ps, pt = fproj(wq[:, 0], wqt, t0, n)
                S.op("act", lambda h, ps=ps, t0=t0, n=n: h.copy(out=P["b0"][:, t0:t0 + n], in_=ps[:, 0:n]), reads=[pt], writes=[qtg])
                ps, pt = fproj(wq[:, 1], wqt, t0, n)
                S.op("act", lambda h, ps=ps, t0=t0, n=n: h.activation(out=P["b1"][:, t0:t0 + n], in_=ps[:, 0:n], func=AF.Copy, scale=128 ** -0.5), reads=[pt], writes=[ktg])
                ps, pt = fproj(wq[:, 2], wqt, t0, n)
                S.op("act", lambda h, ps=ps, t0=t0, n=n: h.activation(out=P["w4"][:, t0:t0 + n], in_=ps[:, 0:n], func=AF.Sigmoid), reads=[pt], writes=[("w4",)])
            yb, ybt = ybs[hd % 2], ("yb%d" % (hd % 2),)
            for g, (s0, sn) in enumerate(SEG):
                load_state_c(hd, g)
                for ci, (t0, n) in enumerate(CH):
                    if (t0 >= 1024) != (g == 1):
                        continue
                    S.op("dve", lambda h, ci=ci: h.tensor_scalar(out=P["st"][:, 0:129], in0=P["st"][:, 0:129], scalar1=P["decB"][:, hd, ci:ci + 1], scalar2=None, op0=ALU.mult),
                         reads=[stt, ("decB",)], writes=[stt])
                    S.op("act", lambda h: h.copy(out=P["stb"][:, 0:130], in_=P["st"][:, 0:130]), reads=[stt], writes=[("stb",)])
                    ps, pt = self.psum()
                    self.mm(ps[0:n, 0:n], [(P["b1"][:, t0:t0 + n], P["b0"][:, t0:t0 + n])], reads=[qtg, ktg], writes=[pt])
                    rr[0] += 1
                    mtile, mtag = mts[rr[0] % 4], ("m%d" % (rr[0] % 4),)
                    S.op("dve", lambda h, ps=ps, n=n, mtile=mtile: h.tensor_tensor(out=mtile[0:n, 0:n], in0=ps[0:n, 0:n], in1=mask[0:n, 0:n], op=ALU.mult),
                         reads=[pt, ("cst",)], writes=[mtag])
                    po_, pot = self.psum()
                    self.mm(po_[0:n, 0:130], [(mtile[0:n, 0:n], P["b3"][0:n, ci, 0:130]), (P["b0"][:, t0:t0 + n], P["stb"][:, 0:130])],
                            reads=[mtag, vtag, qtg, ("stb",)], writes=[pot])
                    pc, pct = self.psum()
                    self.mm(pc[:, 0:130], [(P["b2"][0:n, ci, :], P["b3"][0:n, ci, 0:130])], reads=[ktag, vtag], writes=[pct])
                    S.op("dve", lambda h, pc=pc: h.tensor_tensor(out=P["st"][:, 0:129], in0=P["st"][:, 0:129], in1=pc[:, 0:129], op=ALU.add),
                         reads=[stt, pct, ("stb",)], writes=[stt])
                    sm, smt = P["small"], tg("sm")
                    S.op("act", lambda h, po_=po_, n=n: h.activation(out=sm[0:n, 0:1], in_=po_[0:n, 128:129], func=AF.Abs), reads=[pot], writes=[smt])
                    S.op("dve", lambda h, n=n, ci=ci: h.tensor_tensor(out=sm[0:n, 0:1], in0=sm[0:n, 0:1], in1=P["flT"][0:n, ci, hd:hd + 1], op=ALU.max),
                         reads=[smt, ("flT",)], writes=[smt])
                    S.op("dve", lambda h, n=n: h.reciprocal(out=sm[0:n, 1:2], in_=sm[0:n, 0:1]), reads=[smt], writes=[smt])
                    tt, ttt = (P["t0"], ("t0",)) if rr[0] % 2 else (P["t1"], ("t1",))
                    S.op("act", lambda h, po_=po_, n=n, tt=tt: h.activation(out=tt[0:n, 0:128], in_=po_[0:n, 0:128], func=AF.Square, scale=sm[0:n, 1:2], accum_out=sm[0:n, 2:3]),
                         reads=[pot, smt], writes=[ttt, smt])
                    S.op("act", lambda h, n=n: h.activation(out=sm[0:n, 3:4], in_=sm[0:n, 2:3], func=AF.Sqrt, scale=1.0 / 128, bias=eps_c[0:n, :]), reads=[smt, ("cst",)], writes=[smt])
                    S.op("dve", lambda h, n=n: h.reciprocal(out=sm[0:n, 3:4], in_=sm[0:n, 3:4]), reads=[smt], writes=[smt])
                    S.op("dve", lambda h, n=n: h.tensor_tensor(out=sm[0:n, 4:5], in0=sm[0:n, 3:4], in1=sm[0:n, 1:2], op=ALU.mult), reads=[smt], writes=[smt])
                    S.op("act", lambda h, po_=po_, n=n, tt=tt: h.activation(out=tt[0:n, 0:128], in_=po_[0:n, 0:128], func=AF.Copy, scale=sm[0:n, 4:5]),
                         reads=[pot, smt, ttt], writes=[ttt])
                    ptr, ptt = self.psum()
                    self.tr(ptr[:, 0:n], tt[0:n, 0:128], ident[0:n, 0:n], reads=[ttt, ("cst",)], writes=[ptt])
                    emit_y(yb, ybt, [(ptr, ptt, t0, n)], P["w4"], 0, hd, ("w4",))
                S.dma("sp", A["out_c"][2 * j + g, :, hd * 129:(hd + 1) * 129], P["st"][:, 0:129], reads=[stt])
            S.dma("sp", A["y_scr"][:, hd, :], yb.bitcast(F32), reads=[ybt], writes=[("y_scr",)])

        for hd_ in range(8):
            c_head(hd_)

        if cfg.get("odd_stop") == 4:
            return
        BB = P["w2"]

        def d_head(hd):
            fgt, bbt, qt, sgt, vt = prep_d(hd, True)
            S.op("act", lambda h: h.copy(out=P["w0"][:, :], in_=P["w4"][:, :]), reads=[sgt], writes=[("w0",)])
            yb, ybt = ybs[hd % 2], ("yb%d" % (hd % 2),)
            for g, (s0, sn) in enumerate(SEG):
                load_state_s(hd, g)
                for ci, (t0, n) in enumerate(CH):
                    if (t0 >= 1024) != (g == 1):
                        continue
                    half = n // 2
                    i0, im, il = bbi(t0 - 1), bbi(t0 + half - 1), bbi(t0 + n - 1)
                    if t0 == 1024:
                        i0 = 1025
                    sm, smt = P["small"], tg("smd")
                    S.op("dve", lambda h, i0=i0, im=im: h.tensor_scalar(out=sm[:, 8:9], in0=BB[:, im:im + 1], scalar1=-1.0, scalar2=None, op0=ALU.mult), reads=[bbt], writes=[smt])
                    S.op("dve", lambda h, i0=i0: h.tensor_scalar(out=sm[:, 9:10], in0=BB[:, i0:i0 + 1], scalar1=-1.0, scalar2=None, op0=ALU.mult), reads=[bbt, smt], writes=[smt])
                    bsl = BB[:, bbi(t0):bbi(t0) + n]
                    e0, e0t = (P["t0"], ("t0",))
                    e1, e1t = (P["t1"], ("t1",))
                    S.op("act", lambda h, n=n, bsl=bsl: h.activation(out=e0[:, 0:n], in_=bsl, func=AF.Exp, bias=sm[:, 8:9]), reads=[bbt, smt], writes=[e0t])
                    S.op("dve", lambda h, n=n, t0=t0: h.tensor_tensor(out=P["m0"][:, 0:n], in0=e0[:, 0:n], in1=P["w3"][:, t0:t0 + n], op=ALU.mult), reads=[e0t, qt], writes=[("m0",)])
                    S.op("act", lambda h, n=n, bsl=bsl, im=im: h.activation(out=e1[:, 0:n], in_=bsl, func=AF.Exp, scale=-1.0, bias=BB[:, im:im + 1]), reads=[bbt], writes=[e1t])
                    S.op("dve", lambda h, n=n, t0=t0: h.tensor_tensor(out=P["m1"][:, 0:n], in0=e1[:, 0:n], in1=P["w1"][:, t0:t0 + n], op=ALU.mult), reads=[e1t, fgt], writes=[("m1",)])
                    S.op("act", lambda h, n=n, bsl=bsl: h.activation(out=e0[:, 0:n], in_=bsl, func=AF.Exp, bias=sm[:, 9:10]), reads=[bbt, smt, ("m0",)], writes=[e0t])
                    S.op("dve", lambda h, n=n, t0=t0: h.tensor_tensor(out=P["m2"][:, 0:n], in0=e0[:, 0:n], in1=P["w3"][:, t0:t0 + n], op=ALU.mult), reads=[e0t, qt], writes=[("m2",)])
                    S.op("act", lambda h, n=n, bsl=bsl, il=il: h.activation(out=e1[:, 0:n], in_=bsl, func=AF.Exp, scale=-1.0, bias=BB[:, il:il + 1]), reads=[bbt, ("m1",)], writes=[e1t])
                    S.op("dve", lambda h, n=n, t0=t0: h.tensor_tensor(out=e1[:, 0:n], in0=e1[:, 0:n], in1=P["w1"][:, t0:t0 + n], op=ALU.mult), reads=[e1t, fgt], writes=[e1t])
                    S.op("act", lambda h, il=il: h.activation(out=sm[:, 10:11], in_=BB[:, il:il + 1], func=AF.Exp, bias=sm[:, 9:10]), reads=[bbt, smt], writes=[smt])
                    pa, pat = self.psum()
                    self.mm(pa[0:n, half:n], [(P["m1"][:, 0:n], P["m0"][:, half:n])], reads=[("m0",), ("m1",)], writes=[pat])
                    pb, pbt = self.psum()
                    self.mm(pb[0:half, 0:half], [(P["m1"][:, 0:half], P["m0"][:, 0:half])], reads=[("m0",), ("m1",)], writes=[pbt])
                    S.op("dve", lambda h, n=n: h.memset(P["m3"][0:n, 0:n], 0.0), reads=[("m3",)], writes=[("m3",)])
                    S.op("dve", lambda h, pa=pa, n=n, half=half: h.tensor_tensor(out=P["m3"][0:n, half:n], in0=pa[0:n, half:n], in1=mask[0:n, half:n], op=ALU.mult),
                         reads=[pat, ("cst",), ("m3",)], writes=[("m3",)])
                    S.op("dve", lambda h, pb=pb, half=half: h.tensor_tensor(out=P["m3"][0:half, 0:half], in0=pb[0:half, 0:half], in1=mask[0:half, 0:half], op=ALU.mult),
                         reads=[pbt, ("cst",), ("m3",)], writes=[("m3",)])
                    S.op("act", lambda h: h.copy(out=P["stb"][:, 0:128], in_=P["st"][:, 0:128]), reads=[stt], writes=[("stb",)])
                    po_, pot = self.psum()
                    self.mm(po_[0:n, 0:128], [(P["m3"][0:n, 0:n], P["b2"][0:n, ci, :]), (P["m2"][:, 0:n], P["stb"][:, 0:128])],
                            reads=[("m3",), vt, ("m2",), ("stb",)], writes=[pot])
                    pk, pkt = self.psum()
                    self.tr(pk[0:n, 0:128], e1[:, 0:n], ident, reads=[e1t, ("cst",)], writes=[pkt])
                    S.op("act", lambda h, pk=pk, n=n, ci=ci: h.copy(out=P["b3"][0:n, ci, 0:128], in_=pk[0:n, 0:128]), reads=[pkt], writes=[("b3",)])
                    pn, pnt = self.psum()
                    self.mm(pn[:, 0:128], [(P["b3"][0:n, ci, 0:128], P["b2"][0:n, ci, :])], reads=[("b3",), vt], writes=[pnt])
                    S.op("dve", lambda h, pn=pn: h.scalar_tensor_tensor(out=P["st"][:, 0:128], in0=P["st"][:, 0:128], scalar=sm[:, 10:11], in1=pn[:, 0:128], op0=ALU.mult, op1=ALU.add),
                         reads=[stt, pnt, smt, ("stb",)], writes=[stt])
                    sm2, sm2t = P["bc"], tg("sm2")
                    S.op("act", lambda h, po_=po_, n=n: h.activation(out=e0[0:n, 0:128], in_=po_[0:n, 0:128], func=AF.Square, accum_out=sm2[0:n, 32:33]),
                         reads=[pot, ("m2",), e0t], writes=[e0t, sm2t])
                    S.op("act", lambda h, n=n: h.activation(out=sm2[0:n, 33:34], in_=sm2[0:n, 32:33], func=AF.Sqrt, scale=1.0 / 128, bias=eps_c[0:n, :]), reads=[sm2t, ("cst",)], writes=[sm2t])
                    S.op("dve", lambda h, n=n: h.reciprocal(out=sm2[0:n, 33:34], in_=sm2[0:n, 33:34]), reads=[sm2t], writes=[sm2t])
                    S.op("act", lambda h, po_=po_, n=n: h.activation(out=e0[0:n, 0:128], in_=po_[0:n, 0:128], func=AF.Copy, scale=sm2[0:n, 33:34]), reads=[pot, sm2t, e0t], writes=[e0t])
                    ptr, ptt = self.psum()
                    self.tr(ptr[:, 0:n], e0[0:n, 0:128], ident[0:n, 0:n], reads=[e0t, ("cst",)], writes=[ptt])
                    emit_y(yb, ybt, [(ptr, ptt, t0, n)], P["w0"], 8, hd, ("w0",))
                S.dma("sp", A["out_s"][2 * j + g, :, hd * 128:(hd + 1) * 128], P["st"][:, 0:128], reads=[stt])
            S.dma("sp", A["y_scr"][:, 8 + hd, :], yb.bitcast(F32), reads=[ybt], writes=[("y_scr",)])
        for hd_ in range(8):
            d_head(hd_)
        S.dma("sp", T["xn"][:, :, :].bitcast(F32), A["y_scr"], reads=[("y_scr",)], writes=[("xn", c) for c in range(NCH)])
        self.out_proj(A["odd_w_out"][j])

    def out_proj(self, wout):
        S, T = self.S, self.T
        xn_tags = [("xn", c) for c in range(NCH)]
        wv = wout.rearrange("(k p) c -> p k c", p=128)
        for mb in range(8):
            w, wt = self.wload(wv[:, :, mb * 256:(mb + 1) * 256], 1, [NCH, 256])
            for mm_ in range(2):
                m = mb * 2 + mm_
                for ti, (t0, n) in enumerate(TILES):
                    ps, pt = self.psum()
                    self.mm(ps[:, 0:n], [(w[:, k, mm_ * 128:(mm_ + 1) * 128], T["xn"][:, k, t0:t0 + n]) for k in range(NCH)],
                            reads=wt + xn_tags, writes=[pt])
                    S.op("dve", lambda h, ps=ps, m=m, t0=t0, n=n: h.tensor_tensor(
                        out=T["x"][:, m, t0:t0 + n], in0=ps[:, 0:n], in1=T["x"][:, m, t0:t0 + n], op=ALU.add),
                         reads=[pt, ("x", m, ti)], writes=[("x", m, ti)])

    def store_x(self, dst, final_norm):
        S, T = self.S, self.T
        P = self.phase([("stage0", [D], F32), ("stage1", [D], F32)])
        T["stage"] = [P["stage0"], P["stage1"]]
        ident = T["cst"][:, 0:128]
        src_name = "x"
        if final_norm:
            self.rstd_from_x()
            rs = [("rstd", t0) for t0, _ in TILES]
            for c in range(NCH):
                S.op("dve", lambda h, c=c: h.scalar_tensor_tensor(out=T["x"][:, c, :], in0=T["x"][:, c, :],
                                                                   scalar=T["nf"][:, c:c + 1], in1=T["rstd"][:],
                                                                   op0=ALU.mult, op1=ALU.mult),
                     reads=xt(c) + [("nf",)] + rs, writes=xt(c))
        blocks = [(b * 128, 128) for b in range(8)] + [(1024, 32)]
        for bi, (t0, n) in enumerate(blocks):
            st = T["stage"][bi % 2]
            stag = ("stage", bi % 2)
            for q in range(4):
                ps, ptag = self.psum()
                for cc in range(4):
                    c = q * 4 + cc
                    self.tr(ps[0:n, cc * 128:(cc + 1) * 128], T["x"][:, c, t0:t0 + n], ident,
                            reads=xt(c) + [("cst",)], writes=[ptag])
                if q % 2:
                    S.op("act", lambda h, st=st, ps=ps, q=q, n=n: h.copy(out=st[0:n, q * 512:(q + 1) * 512], in_=ps[0:n, :]),
                         reads=[ptag], writes=[stag])
                else:
                    S.op("dve", lambda h, st=st, ps=ps, q=q, n=n: h.tensor_copy(out=st[0:n, q * 512:(q + 1) * 512], in_=ps[0:n, :]),
                         reads=[ptag], writes=[stag])
            S.dma("sp", dst[t0:t0 + n, :], st[0:n, :], reads=[stag])


def build_nc(cfg):
    nc = bass.Bass("TRN2", target_bir_lowering=False)
    A = {}

    def din(name, shape):
        A[name] = nc.dram_tensor(name, list(shape), F32, kind="ExternalInput").ap()

    def dout(name, shape):
        A[name] = nc.dram_tensor(name, list(shape), F32, kind="ExternalOutput").ap()

    only_ = cfg.get("only")
    if only_ is not None:
        cfg = dict(cfg, n_even=1, n_odd=1)
    din("xin", (NT, D))
    din("cst", (128, 320))
    din("norm_g", (128, DEPTH * 3, NCH))
    din("norm_f", (128, NCH))
    din("pe", (128, 2, 8, 40))
    din("coef", (128, 64))
    use_even = (only_ is None and cfg.get("stages") != (0, "ffn1")) or (only_ is not None and only_[1] == "mix" and only_[0] % 2 == 0)
    if use_even:
        din("st_conv", (2, 128, 8, 32))
        din("even_w_in", (cfg.get("n_even", 2), D, EVEN_IN))
        din("even_w_out", (cfg.get("n_even", 2), D, D))
        dout("out_conv", (4, 128, 256))
    A["hg_in"] = nc.dram_tensor("hg_in", [128, 256], F32, kind="Internal").ap()
    A["hg_out"] = nc.dram_tensor("hg_out", [512, 256], F32, kind="Internal").ap()
    A["ya_scr"] = nc.dram_tensor("ya_scr", [128, 8, NT // 2], F32, kind="Internal").ap()
    use_odd = (only_ is None and cfg.get("depth", DEPTH) > 1) or (only_ is not None and only_[1] == "mix" and only_[0] % 2 == 1)
    if use_odd:
        din("po", (128, 2, 32))
        din("pg", (8, 2, 4))
        din("st_c", (2, 128, 8, 129))
        din("st_s", (2, 128, 8, 128))
        din("st_m", (8, 2))
        din("st_mb", (128, 2, 8))
        din("odd_w_in", (cfg.get("n_odd", 2), D, ODD_IN))
        din("odd_w_out", (cfg.get("n_odd", 2), D, D))
        dout("out_c", (4, 128, 8 * 129))
        dout("out_s", (4, 128, 1024))
        dout("out_m", (8, 4))
        A["oga_in"] = nc.dram_tensor("oga_in", [128, 1032], F32, kind="Internal").ap()
        A["oga_out"] = nc.dram_tensor("oga_out", [512, 1032], F32, kind="Internal").ap()
        A["ogb_in"] = nc.dram_tensor("ogb_in", [128, 1048], F32, kind="Internal").ap()
        A["ogb_out"] = nc.dram_tensor("ogb_out", [512, 1048], F32, kind="Internal").ap()
        A["hm_scr"] = nc.dram_tensor("hm_scr", [2, 8], F32, kind="Internal").ap()
        A["dec_scr"] = nc.dram_tensor("dec_scr", [8, 9], F32, kind="Internal").ap()
        A["y_scr"] = nc.dram_tensor("y_scr", [128, 16, NT // 2], F32, kind="Internal").ap()
    LD = cfg.get("depth", DEPTH) if cfg.get("only") is None else 1
    if only_ is None or only_[1] in ("ffn1", "ffn2"):
        din("ffn_w_gate", (LD, 2, D, DFF))
        din("ffn_w_up", (LD, 2, D, DFF))
        din("ffn_w_down", (LD, 2, DFF, D))
    dout("y", (NT, D))

    import contextlib
    with contextlib.ExitStack() as es:
        def sb(name, shape, dt):
            return es.enter_context(nc.sbuf_tensor(name, list(shape), dt))
        T = {}
        T["x"] = sb("x", (128, NCH, NT), F32)
        T["xn"] = sb("xn", (128, NCH, NT), BF16)
        T["ring"] = sb("ring", (128, RING_UNITS, 4096), BF16)
        T["cst"] = sb("cstt", (128, 320), F32)
        T["ng"] = sb("ng", (128, DEPTH * 3, NCH), F32)
        T["nf"] = sb("nf", (128, NCH), F32)
        T["ones_bf"] = sb("ones_bf", (128, 128), BF16)
        T["rstd"] = sb("rstd", (128, NT), F32)
        T["sq"] = [sb("sq%d" % i, (128, NT), BF16) for i in range(2)]
        T["arena"] = sb("arena", (128, ARENA_F32), F32)
        T["pe_"] = sb("pet", (128, 2, 8, 40), F32)
        T["coef"] = sb("coeft", (128, 64), F32)
        T["po"] = sb("pot", (128, 2, 32), F32)
        T["pg"] = sb("pgt", (8, 2, 4), F32)
        T["ps"] = [es.enter_context(nc.psum_tensor("ps%d" % i, [128, 512], F32)) for i in range(8)]
        sems = [es.enter_context(nc.semaphore("s%d" % i)) for i in range(40)]
        block = es.enter_context(nc.Block())
        S = Sched(nc, sems)
        B = Builder(nc, S, T, cfg)
        B.load_consts(A)
        if cfg.get("touch"):
            junk = sb("junk", (1, 64), F32)
            for nm in ("even_w_in", "even_w_out", "st_conv", "ffn_w_gate", "ffn_w_up", "ffn_w_down"):
                if nm in A:
                    a = A[nm]
                    while len(a.shape) > 1:
                        a = a[0]
                    S.dma("sp", junk[0:1, 0:8], a[0:8].unsqueeze(0), writes=[("junk",)])
        B.load_x(A)
        stages = cfg.get("stages", None)
        only = cfg.get("only", None)
        for l in range(cfg.get("depth", DEPTH)):
            if only is None or only == (l, "ffn1"):
                B.norm_to_xn(l * 3 + 0)
                B.ffn(A, l if only is None else 0, 0)
            if stages == (l, "ffn1"):
                break
            if only is None or only == (l, "mix"):
                if l % 2 == 0:
                    B.even_mixer(A, (l // 2) if only is None else 0, l)
                else:
                    B.odd_mixer(A, (l // 2) if only is None else 0, l)
            if stages == (l, "mix"):
                break
            if only is None or only == (l, "ffn2"):
                B.norm_to_xn(l * 3 + 2)
                B.ffn(A, l if only is None else 0, 1)
        B.store_x(A["y"], final_norm=cfg.get("final_norm", True))
        S.finish()
        S.emit(block)
    return nc


def make_consts():
    c = np.zeros((128, 320), np.float32)
    c[:, 0:128] = np.eye(128, dtype=np.float32)
    c[:, 128:256] = np.triu(np.ones((128, 128), np.float32))
    c[:, 256] = EPS
    c[:, 257] = 1.0
    return c


_NC_CACHE = {}


def kernel(**inputs):
    inp = {k: np.asarray(v) for k, v in inputs.items()}
    cfg = {}
    if "full" not in _NC_CACHE:
        _NC_CACHE["full"] = build_nc(cfg)
    nc = _NC_CACHE["full"]
    maps = make_in_maps(inp, cfg)
    res = run_bass_kernel_spmd(nc, maps, core_ids=list(range(8)))
    R = res.results
    f32 = np.float32
    y_prompt = np.zeros((2, 4096, D), f32)
    y_sample = np.zeros((8, 32, D), f32)
    p_conv_a = np.zeros((2, 2, 2, 1024), f32)
    p_conv_b = np.zeros((2, 2, 30, 1024), f32)
    s_conv_a = np.zeros((2, 8, 2, 1024), f32)
    s_conv_b = np.zeros((2, 8, 30, 1024), f32)
    p_c = np.zeros((2, 2, 8, 128, 128), f32)
    p_n = np.zeros((2, 2, 8, 128), f32)
    p_m = np.zeros((2, 2, 8), f32)
    p_s = np.zeros((2, 2, 8, 128, 128), f32)
    s_c = np.zeros((2, 8, 8, 128, 128), f32)
    s_n = np.zeros((2, 8, 8, 128), f32)
    s_m = np.zeros((2, 8, 8), f32)
    s_s = np.zeros((2, 8, 8, 128, 128), f32)

    def conv_split(a):
        a = a.reshape(128, 8, 32)
        return (a[:, :, 0:30].transpose(2, 1, 0).reshape(30, 1024), a[:, :, 30:32].transpose(2, 1, 0).reshape(2, 1024))

    for c in range(8):
        b, r = c // 4, c % 4
        y = R[c]["y"]
        y_prompt[b, r * 1024:(r + 1) * 1024] = y[:1024]
        y_sample[c] = y[1024:]
        for jj in range(2):
            oc = R[c]["out_conv"]
            cb, ca = conv_split(oc[2 * jj + 1])
            s_conv_b[jj, c], s_conv_a[jj, c] = cb, ca
            cc = R[c]["out_c"][2 * jj + 1].reshape(128, 8, 129)
            s_c[jj, c] = cc[:, :, 0:128].transpose(1, 0, 2)
            s_n[jj, c] = cc[:, :, 128].T
            s_m[jj, c] = R[c]["out_m"][:, 2 * jj + 1]
            s_s[jj, c] = R[c]["out_s"][2 * jj + 1].reshape(128, 8, 128).transpose(1, 0, 2)
            if r == 3:
                cb, ca = conv_split(oc[2 * jj + 0])
                p_conv_b[jj, b], p_conv_a[jj, b] = cb, ca
                cc = R[c]["out_c"][2 * jj + 0].reshape(128, 8, 129)
                p_c[jj, b] = cc[:, :, 0:128].transpose(1, 0, 2)
                p_n[jj, b] = cc[:, :, 128].T
                p_m[jj, b] = R[c]["out_m"][:, 2 * jj + 0]
                p_s[jj, b] = R[c]["out_s"][2 * jj + 0].reshape(128, 8, 128).transpose(1, 0, 2)
    return (y_prompt, y_sample, p_conv_a, p_conv_b, p_c, p_n, p_m, p_s,
            s_conv_a, s_conv_b, s_c, s_n, s_m, s_s)


def make_in_maps(inp, cfg):
    LD = cfg.get("depth", DEPTH)
    only = cfg.get("only", None)
    cst = make_consts()
    maps = []
    xp = inp["x_prompt"]
    xs = inp["x_sample"]
    f32 = np.float32
    if only is not None:
        l0 = only[0]
        lsl = slice(l0, l0 + 1)
        jsl = slice(l0 // 2, l0 // 2 + 1)
    else:
        lsl = slice(0, LD)
        jsl = slice(0, 2)
    need_ffn = only is None or only[1] in ("ffn1", "ffn2")
    wg = np.ascontiguousarray(inp["ffn_w_gate"][lsl]) if need_ffn else np.zeros((1, 2, D, DFF), f32)
    wu = np.ascontiguousarray(inp["ffn_w_up"][lsl]) if need_ffn else np.zeros((1, 2, D, DFF), f32)
    wd = np.ascontiguousarray(inp["ffn_w_down"][lsl]) if need_ffn else np.zeros((1, 2, DFF, D), f32)
    ng = np.ascontiguousarray(inp["norm_g"].reshape(DEPTH, 3, NCH, 128).transpose(3, 0, 1, 2).reshape(128, DEPTH * 3, NCH))
    nf = np.ascontiguousarray(inp["norm_f"].reshape(NCH, 128).T)
    pe = np.zeros((128, 2, 8, 40), f32)
    for j in range(2):
        pe[:, j, :, 0:31] = inp["even_conv_b"][j].reshape(31, 8, 128).transpose(2, 1, 0)
        pe[:, j, :, 31:34] = inp["even_conv_a"][j].reshape(3, 8, 128).transpose(2, 1, 0)
        pe[:, j, :, 34] = inp["even_conv_b_bias"][j].reshape(8, 128).T
        pe[:, j, :, 35] = inp["even_ln_g"][j].reshape(8, 128).T
        pe[:, j, :, 36] = inp["even_ln_b"][j].reshape(8, 128).T
    pe = np.ascontiguousarray(pe[:, jsl]) if only is not None else pe
    if pe.shape[1] == 1:
        pe = np.ascontiguousarray(np.concatenate([pe, pe], axis=1))
    ewi = np.ascontiguousarray(inp["even_w_in"][jsl])
    ewo = np.ascontiguousarray(inp["even_w_out"][jsl])
    po = np.zeros((128, 2, 32), f32)
    pg = np.zeros((8, 2, 4), f32)
    for j in range(2):
        po[:, j, 0:8] = inp["odd_norm_c"][j].reshape(8, 128).T
        po[:, j, 8:16] = inp["odd_norm_d"][j].reshape(8, 128).T
        po[:, j, 16:24] = inp["odd_lb_logits"][0].reshape(8, 128).T
        po[:, j, 24:32] = inp["odd_lb_logits"][1].reshape(8, 128).T
        pg[:, j, 0] = inp["odd_bias_i"][j]
        pg[:, j, 1] = inp["odd_bias_f"][j]
    if only is not None:
        po = np.ascontiguousarray(np.concatenate([po[:, jsl], po[:, jsl]], 1))
        pg = np.ascontiguousarray(np.concatenate([pg[:, jsl], pg[:, jsl]], 1))
    owi = np.ascontiguousarray(inp["odd_w_in"][jsl])
    owo = np.ascontiguousarray(inp["odd_w_out"][jsl])
    xov = cfg.get("xin_override", None)
    for c in range(8):
        b, r = c // 4, c % 4
        xin = np.concatenate([xp[b, r * 1024:(r + 1) * 1024], xs[c]], axis=0)
        if xov is not None:
            xin = xov(c, xin)
        coef = np.zeros((128, 64), f32)
        if r > 0:
            coef[:, r - 1] = 1.0
        stc = np.zeros((2, 128, 8, 32), f32)
        for j in range(2):
            stc[j, :, :, 0:30] = inp["state_conv_b"][j, c].reshape(30, 8, 128).transpose(2, 1, 0)
            stc[j, :, :, 30:32] = inp["state_conv_a"][j, c].reshape(2, 8, 128).transpose(2, 1, 0)
        if only is not None:
            stc = np.ascontiguousarray(np.concatenate([stc[jsl], stc[jsl]], axis=0))
        for q in range(4):
            coef[:, 4 + q] = 0.0 if q < r else NEG
            for q2 in range(4):
                coef[:, 8 + 4 * q + q2] = 1.0 if (q < q2 < r) else 0.0
            coef[:, 24 + q] = 1.0 if q < r else 0.0
        st_c = np.zeros((2, 128, 8, 129), f32)
        st_c[:, :, :, 0:128] = inp["state_mlstm_c"][:, c].transpose(0, 2, 1, 3)
        st_c[:, :, :, 128] = inp["state_mlstm_n"][:, c].transpose(0, 2, 1)
        st_s = np.ascontiguousarray(inp["state_hgrn_s"][:, c].transpose(0, 2, 1, 3))
        st_m = np.ascontiguousarray(inp["state_mlstm_m"][:, c].T)
        st_mb = np.ascontiguousarray(np.broadcast_to(inp["state_mlstm_m"][:, c][None], (128, 2, 8)))
        if only is not None:
            st_c = np.ascontiguousarray(np.concatenate([st_c[jsl], st_c[jsl]], 0))
            st_s = np.ascontiguousarray(np.concatenate([st_s[jsl], st_s[jsl]], 0))
            st_m = np.ascontiguousarray(np.concatenate([st_m[:, jsl], st_m[:, jsl]], 1))
            st_mb = np.ascontiguousarray(np.concatenate([st_mb[:, jsl], st_mb[:, jsl]], 1))
        maps.append(dict(xin=np.ascontiguousarray(xin.astype(f32)), cst=cst, norm_g=ng, norm_f=nf,
                         ffn_w_gate=wg, ffn_w_up=wu, ffn_w_down=wd, pe=pe, coef=coef, st_conv=stc,
                         even_w_in=ewi, even_w_out=ewo, po=po, pg=pg, st_c=st_c, st_s=st_s, st_m=st_m, st_mb=st_mb,
                         odd_w_in=owi, odd_w_out=owo))
    return maps
```
